# Optimizing a Trainium2 kernel written in Bass

```python
import math
import jax
import jax.numpy as jnp
from jax import lax
import numpy as np

D_MODEL = 2048
BATCH = 8
SEQ = 2048
DEPTH = 2

RET_HEADS = 4
RET_QK_DIM = 256
RET_V_DIM = D_MODEL // RET_HEADS
RET_QK_WIDTH = RET_HEADS * RET_QK_DIM
CHUNK = 128
ROPE_BASE = 10000.0
SSM_GROUP = 16
SSM_GROUPS = D_MODEL // SSM_GROUP
SSM_STATE = 64
DT_MIN = 0.001
DT_MAX = 0.1
D_FF = ((8 * D_MODEL // 3 + 255) // 256) * 256
IN_WIDTH = 2 * RET_QK_WIDTH + 5 * D_MODEL
EPS = 1e-6

kernel_name = "hybrid_retention_s5_gated_encoder"


def rms_norm(x, g):
    xf = x.astype(jnp.float32)
    y = xf * lax.rsqrt(jnp.mean(xf * xf, axis=-1, keepdims=True) + EPS)
    return (y * g.astype(jnp.float32)).astype(x.dtype)


def rotary(x):
    L = x.shape[1]
    half = x.shape[-1] // 2
    inv = 1.0 / (ROPE_BASE ** (jnp.arange(half, dtype=jnp.float32) / half))
    ang = jnp.arange(L, dtype=jnp.float32)[:, None] * inv[None, :]
    cos = jnp.cos(ang)[None, :, None, :]
    sin = jnp.sin(ang)[None, :, None, :]
    xf = x.astype(jnp.float32)
    x1, x2 = xf[..., :half], xf[..., half:]
    return jnp.concatenate([x1 * cos - x2 * sin, x1 * sin + x2 * cos], axis=-1)


def retention(q, k, v, log_gamma):
    f32 = jnp.float32
    b, l, h, dk = q.shape
    dv = v.shape[-1]
    nc = l // CHUNK
    q = q.reshape(b, nc, CHUNK, h, dk)
    k = (k * dk ** -0.5).reshape(b, nc, CHUNK, h, dk)
    v = v.astype(f32).reshape(b, nc, CHUNK, h, dv)
    lg = log_gamma.astype(f32)
    lg_f, lg_b = lg[0], lg[1]
    t = jnp.arange(CHUNK, dtype=f32)
    diff = t[:, None] - t[None, :]
    dmat = jnp.exp(jnp.where(diff >= 0, lg_f[:, None, None] * diff, -lg_b[:, None, None] * diff))
    scores = jnp.einsum('bnthd,bnshd->bnhts', q, k) * dmat
    y = jnp.einsum('bnhts,bnshe->bnthe', scores, v)
    kf = k * jnp.exp(lg_f[None, :] * (CHUNK - 1.0 - t)[:, None])[:, :, None]
    kb = k * jnp.exp(lg_b[None, :] * t[:, None])[:, :, None]
    kv_f = jnp.einsum('bnshd,bnshe->nbhde', kf, v)
    kv_b = jnp.einsum('bnshd,bnshe->nbhde', kb, v)
    decay_f = jnp.exp(lg_f * CHUNK)[None, :, None, None]
    decay_b = jnp.exp(lg_b * CHUNK)[None, :, None, None]

    def step_f(s, kv):
        return decay_f * s + kv, s

    def step_b(s, kv):
        return decay_b * s + kv, s

    zero = jnp.zeros((b, h, dk, dv), f32)
    _, s_f = lax.scan(step_f, zero, kv_f)
    _, s_b = lax.scan(step_b, zero, kv_b, reverse=True)
    qf = q * jnp.exp(lg_f[None, :] * (t[:, None] + 1.0))[:, :, None]
    qb = q * jnp.exp(lg_b[None, :] * (CHUNK - t)[:, None])[:, :, None]
    y = y + jnp.einsum('bnthd,nbhde->bnthe', qf, s_f) + jnp.einsum('bnthd,nbhde->bnthe', qb, s_b)
    return y.reshape(b, l, h, dv)


def _linear_recurrence(e1, e2):
    a1, b1 = e1
    a2, b2 = e2
    return a1 * a2, a2 * b1 + b2


def s5_direction(u, a_re, a_im, log_dt, b_re, b_im, c_re, c_im, reverse):
    f32 = jnp.float32
    lam = lax.complex(a_re.astype(f32), a_im.astype(f32))
    dt = jnp.exp(log_dt.astype(f32))[:, None]
    lam_bar = jnp.exp(lam * dt)
    b_c = lax.complex(b_re.astype(f32), b_im.astype(f32))
    b_bar = ((lam_bar - 1.0) / lam)[:, :, None] * b_c
    bu = lax.complex(jnp.einsum('blgh,gph->blgp', u, jnp.real(b_bar)),
                     jnp.einsum('blgh,gph->blgp', u, jnp.imag(b_bar)))
    a = jnp.broadcast_to(lam_bar, bu.shape)
    _, xs = lax.associative_scan(_linear_recurrence, (a, bu), axis=1, reverse=reverse)
    return (jnp.einsum('blgp,ghp->blgh', jnp.real(xs), c_re.astype(f32))
            - jnp.einsum('blgp,ghp->blgh', jnp.imag(xs), c_im.astype(f32)))


def hybrid_mixer(h, w_in, log_gamma, a_re, a_im, log_dt, b_re, b_im, c_re, c_im,
                 d_skip, w_glu, b_glu, w_out):
    bsz, l, _ = h.shape
    dt_in = h.dtype
    proj = h @ w_in
    cuts = [RET_QK_WIDTH, 2 * RET_QK_WIDTH, 2 * RET_QK_WIDTH + D_MODEL,
            2 * RET_QK_WIDTH + 2 * D_MODEL, 2 * RET_QK_WIDTH + 3 * D_MODEL,
            2 * RET_QK_WIDTH + 4 * D_MODEL]
    q, k, v, g, u, gate_r, gate_s = jnp.split(proj, cuts, axis=-1)

    q = rotary(q.reshape(bsz, l, RET_HEADS, RET_QK_DIM))
    k = rotary(k.reshape(bsz, l, RET_HEADS, RET_QK_DIM))
    v = v.reshape(bsz, l, RET_HEADS, RET_V_DIM)
    y = retention(q, k, v, log_gamma)
    y = y * lax.rsqrt(jnp.mean(y * y, axis=-1, keepdims=True) + EPS)
    ret_out = jax.nn.silu(g.astype(jnp.float32)) * y.reshape(bsz, l, D_MODEL)

    uf = u.astype(jnp.float32)
    ug = uf.reshape(bsz, l, SSM_GROUPS, SSM_GROUP)
    ys = (s5_direction(ug, a_re[0], a_im[0], log_dt[0], b_re[0], b_im[0], c_re[0], c_im[0], False)
          + s5_direction(ug, a_re[1], a_im[1], log_dt[1], b_re[1], b_im[1], c_re[1], c_im[1], True))
    ys = ys.reshape(bsz, l, D_MODEL) + d_skip.astype(jnp.float32) * uf
    ys = jax.nn.gelu(ys).astype(dt_in)
    ssm_out = ys * jax.nn.sigmoid(ys @ w_glu + b_glu)

    merged = (jax.nn.sigmoid(gate_r) * ret_out.astype(dt_in)
              + jax.nn.sigmoid(gate_s) * ssm_out)
    return merged @ w_out


def swiglu(h, w_gate, w_up, w_down):
    return (jax.nn.silu(h @ w_gate) * (h @ w_up)) @ w_down


def setup_inputs(seed: int = 0) -> dict:
    key = jax.random.key(seed)
    ks = jax.random.split(key, 24)
    f32 = jnp.float32
    G, P, Hg = SSM_GROUPS, SSM_STATE, SSM_GROUP
    nrm = lambda k, shape, scale: jax.random.normal(k, shape, f32) * scale
    x = nrm(ks[0], (BATCH, SEQ, D_MODEL), 1.0)
    ln_mix_g = 1.0 + nrm(ks[1], (DEPTH, D_MODEL), 0.02)
    w_in = nrm(ks[2], (DEPTH, D_MODEL, IN_WIDTH), D_MODEL ** -0.5)
    base_lg = jnp.log(1.0 - 2.0 ** (-5.0 - jnp.arange(RET_HEADS, dtype=f32)))
    ret_log_gamma = base_lg[None, None, :] * (1.0 + nrm(ks[3], (DEPTH, 2, RET_HEADS), 0.05))
    n = jnp.arange(P, dtype=f32)
    ssm_a_re = -0.5 + nrm(ks[4], (DEPTH, 2, G, P), 0.01)
    ssm_a_im = math.pi * n[None, None, None, :] + nrm(ks[5], (DEPTH, 2, G, P), 0.01)
    ssm_log_dt = jax.random.uniform(ks[6], (DEPTH, 2, G), f32,
                                    math.log(DT_MIN), math.log(DT_MAX))
    ssm_b_re = nrm(ks[7], (DEPTH, 2, G, P, Hg), (2.0 * Hg) ** -0.5)
    ssm_b_im = nrm(ks[8], (DEPTH, 2, G, P, Hg), (2.0 * Hg) ** -0.5)
    ssm_c_re = nrm(ks[9], (DEPTH, 2, G, Hg, P), (2.0 * P) ** -0.5)
    ssm_c_im = nrm(ks[10], (DEPTH, 2, G, Hg, P), (2.0 * P) ** -0.5)
    ssm_d = nrm(ks[11], (DEPTH, D_MODEL), 1.0)
    w_glu = nrm(ks[12], (DEPTH, D_MODEL, D_MODEL), D_MODEL ** -0.5)
    b_glu = nrm(ks[13], (DEPTH, D_MODEL), 0.01)
    w_out = nrm(ks[14], (DEPTH, D_MODEL, D_MODEL), D_MODEL ** -0.5)
    ln_ffn_g = 1.0 + nrm(ks[15], (DEPTH, D_MODEL), 0.02)
    w_ffn_gate = nrm(ks[16], (DEPTH, D_MODEL, D_FF), D_MODEL ** -0.5)
    w_ffn_up = nrm(ks[17], (DEPTH, D_MODEL, D_FF), D_MODEL ** -0.5)
    w_ffn_down = nrm(ks[18], (DEPTH, D_FF, D_MODEL), D_FF ** -0.5)
    ln_final_g = 1.0 + nrm(ks[19], (D_MODEL,), 0.02)
    return {"x": x, "ln_mix_g": ln_mix_g, "w_in": w_in, "ret_log_gamma": ret_log_gamma,
            "ssm_a_re": ssm_a_re, "ssm_a_im": ssm_a_im, "ssm_log_dt": ssm_log_dt,
            "ssm_b_re": ssm_b_re, "ssm_b_im": ssm_b_im, "ssm_c_re": ssm_c_re,
            "ssm_c_im": ssm_c_im, "ssm_d": ssm_d, "w_glu": w_glu, "b_glu": b_glu,
            "w_out": w_out, "ln_ffn_g": ln_ffn_g, "w_ffn_gate": w_ffn_gate,
            "w_ffn_up": w_ffn_up, "w_ffn_down": w_ffn_down, "ln_final_g": ln_final_g}


def reference(x, ln_mix_g, w_in, ret_log_gamma, ssm_a_re, ssm_a_im, ssm_log_dt,
              ssm_b_re, ssm_b_im, ssm_c_re, ssm_c_im, ssm_d, w_glu, b_glu, w_out,
              ln_ffn_g, w_ffn_gate, w_ffn_up, w_ffn_down, ln_final_g):
    for i in range(DEPTH):
        h = rms_norm(x, ln_mix_g[i])
        x = x + hybrid_mixer(h, w_in[i], ret_log_gamma[i], ssm_a_re[i], ssm_a_im[i],
                             ssm_log_dt[i], ssm_b_re[i], ssm_b_im[i], ssm_c_re[i],
                             ssm_c_im[i], ssm_d[i], w_glu[i], b_glu[i], w_out[i])
        h = rms_norm(x, ln_ffn_g[i])
        x = x + swiglu(h, w_ffn_gate[i], w_ffn_up[i], w_ffn_down[i])
    return rms_norm(x, ln_final_g)
```

```python
import math, os
from contextlib import ExitStack
import numpy as np
import concourse.bass as bass
import concourse.mybir as mybir
from concourse.bass_utils import run_bass_kernel_spmd

F32 = mybir.dt.float32
BF16 = mybir.dt.bfloat16
I32 = mybir.dt.int32
AF = mybir.ActivationFunctionType
ALU = mybir.AluOpType

L = 2048
D = 2048
KC = 16
DFF = 5632
FC = 44
INW = 12288
EPS = 1e-6
NLAYER = 2
TWO_PI = 2.0 * math.pi
C1 = 6.28125
C2 = TWO_PI - C1
MOFF = 1920
MW = 3968


class Buf:
    def __init__(self, name, excl=False):
        self.name = name
        self.excl = excl
        self.w = {}
        self.r = {}
        self.dsem = None
        self.dkind = None


class Sched:
    def __init__(self, nc, stack):
        self.nc = nc
        self.stack = stack
        self.eng = {'pe': nc.tensor, 'act': nc.scalar, 'dve': nc.vector, 'pool': nc.gpsimd, 'sp': nc.sync}
        self.sems = {}
        self.cnt = {}
        self.isdma = {}
        self.seen = {e: {} for e in self.eng}
        self.self_sync = {'pe': False, 'act': True, 'dve': True, 'pool': True}
        self.nsem = 0
        self.dpools = {'hw': [], 'sw': []}
        self.dnexts = {'hw': 0, 'sw': 0}
        for e in ('pe', 'act', 'dve', 'pool'):
            self._newsem('E_' + e, False)

    def _newsem(self, key, isdma):
        s = self.stack.enter_context(self.nc.semaphore('s_' + key))
        self.sems[key] = s
        self.cnt[key] = 0
        self.isdma[key] = isdma
        self.nsem += 1
        return key

    def _deps(self, reads, writes):
        deps = {}
        for b in reads:
            for k, v in b.w.items():
                if deps.get(k, 0) < v:
                    deps[k] = v
        for b in writes:
            for dd in (b.w, b.r):
                for k, v in dd.items():
                    if deps.get(k, 0) < v:
                        deps[k] = v
        return deps

    def _wait(self, e, deps):
        for k, v in deps.items():
            if self.isdma[k]:
                v = self.cnt[k]
            if self.seen[e].get(k, 0) < v:
                self.eng[e].wait_ge(self.sems[k], v)
                self.seen[e][k] = v

    def op(self, e, fn, reads=(), writes=()):
        ex = [b for b in reads if b.excl]
        if ex:
            reads = [b for b in reads if not b.excl]
            writes = list(writes) + ex
        self._wait(e, self._deps(reads, writes))
        ins = fn(self.eng[e])
        k = 'E_' + e
        self.cnt[k] += 1
        ins.then_inc(self.sems[k], 1)
        v = self.cnt[k]
        if not self.self_sync[e]:
            self.seen[e][k] = v
        for b in reads:
            b.r[k] = v
        for b in writes:
            b.w[k] = v
        return ins

    def dma(self, q, out, in_, sb, reads=(), writes=(), **kw):
        self._wait(q, self._deps(reads, writes))
        kind = 'sw' if q == 'pool' else 'hw'
        if sb.dsem is None:
            pool_, cap = self.dpools[kind], (8 if kind == 'sw' else 36)
            if len(pool_) < cap:
                pool_.append(self._newsem('D%s%d' % (kind, len(pool_)), True))
                sb.dsem = pool_[-1]
            else:
                sb.dsem = pool_[self.dnexts[kind] % cap]
            self.dnexts[kind] += 1
            sb.dkind = kind
        assert sb.dkind == kind, (sb.name, sb.dkind, kind)
        k = sb.dsem
        ins = self.eng[q].dma_start(out=out, in_=in_, **kw)
        ins.then_inc(self.sems[k], 16)
        self.cnt[k] += 16
        v = self.cnt[k]
        for b in reads:
            b.r[k] = v
        for b in writes:
            b.w[k] = v
        return ins

    def barrier(self):
        deps = {k: self.cnt[k] for k in self.cnt if self.cnt[k] > 0}
        for e in self.eng:
            self._wait(e, dict(deps))

    def finish(self):
        self.barrier()
        done = self._newsem('DONE', False)
        for e in self.eng:
            self.eng[e].sem_inc(self.sems[done], 1)
        self.eng['sp'].wait_ge(self.sems[done], len(self.eng))
        for k, sm in self.sems.items():
            if k != done:
                self.eng['sp'].sem_clear(sm)
        self.eng['sp'].sem_clear(self.sems[done])


def _consts():
    ident = np.eye(128, dtype=np.float32)
    half = 128
    inv = (1.0 / (10000.0 ** (np.arange(half, dtype=np.float32) / np.float32(half)))).astype(np.float32)
    ang = (np.arange(L, dtype=np.float32)[None, :] * inv[:, None]).astype(np.float32)
    cos = np.cos(ang.astype(np.float64)).astype(np.float32)
    sin = np.sin(ang.astype(np.float64)).astype(np.float32)
    j = np.arange(MW, dtype=np.float32)[None, :]
    p = np.arange(128, dtype=np.float32)[:, None]
    dtab = (j - MOFF - p).astype(np.float32)
    g = np.arange(128)
    par = np.stack([(g % 2 == 0), (g % 2 == 1)], axis=1).astype(np.float32)
    sel = (g[:, None] // 2 == np.arange(64)[None, :]).astype(np.float32)
    bm2 = (g[:, None] // 32 == g[None, :] // 32).astype(np.float32)
    return {"c_ident": ident, "c_cos": cos, "c_sin": sin, "c_dtab": dtab, "c_par": par, "c_sel": sel, "c_bm2": bm2}


def build_nc(dbg=None):
    nc = bass.Bass("TRN2", target_bir_lowering=False)
    dbg = dbg or {}
    stop_after = dbg.get("stop_after")
    nlayer = dbg.get("nlayer", NLAYER)

    def din(name, shape, dt=F32):
        return nc.dram_tensor(name, list(shape), dt, kind="ExternalInput").ap()

    x_in = din("x", [L, D])
    ln_mix_g = din("ln_mix_g", [2, D])
    w_in = din("w_in", [2, D, INW])
    lg_in = din("ret_log_gamma", [2, 8])
    a_re = din("ssm_a_re", [2, 2, 128, 64])
    a_im = din("ssm_a_im", [2, 2, 128, 64])
    log_dt = din("ssm_log_dt", [2, 2, 128])
    b_re = din("ssm_b_re", [2, 2, 128, 64, 16])
    b_im = din("ssm_b_im", [2, 2, 128, 64, 16])
    c_re = din("ssm_c_re", [2, 2, 128, 16, 64])
    c_im = din("ssm_c_im", [2, 2, 128, 16, 64])
    ssm_d = din("ssm_d", [2, D])
    w_glu = din("w_glu", [2, D, D])
    b_glu = din("b_glu", [2, D])
    w_out = din("w_out", [2, D, D])
    ln_ffn_g = din("ln_ffn_g", [2, D])
    w_fg = din("w_ffn_gate", [2, D, DFF])
    w_fu = din("w_ffn_up", [2, D, DFF])
    w_fd = din("w_ffn_down", [2, DFF, D])
    ln_final_g = din("ln_final_g", [1, D])
    c_ident = din("c_ident", [128, 128])
    c_cos = din("c_cos", [128, L])
    c_sin = din("c_sin", [128, L])
    c_dtab = din("c_dtab", [128, MW])
    c_par = din("c_par", [128, 2])
    c_sel = din("c_sel", [128, 64])
    c_bm2 = din("c_bm2", [128, 128])
    out_d = nc.dram_tensor("out", [L, D], F32, kind="ExternalOutput").ap()

    skind = "ExternalOutput" if dbg.get("dump") else "Internal"

    def dscr(name, shape, dt):
        k = "ExternalInput" if name in dbg.get("preload", ()) else skind
        return nc.dram_tensor(name, list(shape), dt, kind=k).ap()

    qT_d = dscr("s_qT", [1024, L], BF16)
    kT_d = dscr("s_kT", [1024, L], BF16)
    v_d = dscr("s_v", [L, D], BF16)
    sgT_d = dscr("s_sgT", [D, L], BF16)
    uT_d = dscr("s_uT", [D, L], BF16)
    srT_d = dscr("s_srT", [D, L], BF16)
    ssT_d = dscr("s_ssT", [D, L], BF16)
    mrT_d = dscr("s_mrT", [D, L], BF16)
    ysT_d = dscr("s_ysT", [D, L], BF16)
    aT_d = dscr("s_aT", [DFF, L], BF16)
    xa_d = dscr("s_xa", [L, D], F32)
    xb_d = dscr("s_xb", [L, D], F32)
    B = {n: Buf(n) for n in ["x", "qT", "kT", "v", "sgT", "uT", "srT", "ssT", "mrT", "ysT", "aT", "xa", "xb", "out", "params"]}

    _uid = [0]

    def SBT(name, shape, dt):
        _uid[0] += 1
        return nc.sbuf_tensor("%s_%d" % (name, _uid[0]), shape, dt)

    with ExitStack() as st:
        S = Sched(nc, st)

        def sb(name, shape, dt):
            t = st.enter_context(SBT(name, list(shape), dt))
            return t, Buf(name)

        ps = []
        for i in range(8):
            t = st.enter_context(nc.psum_tensor("ps%d" % i, [128, 512], F32))
            ps.append((t, Buf("ps%d" % i, excl=True)))
        ident, identB = sb("ident", [128, 128], F32)
        identb, identbB = sb("identb", [128, 128], BF16)
        onesb, onesbB = sb("onesb", [128, 128], BF16)
        S.dma('sp', ident[:], c_ident[:, :], identB, writes=[identB])
        S.op('dve', lambda e: e.tensor_copy(identb[:], ident[:]), [identB], [identbB])
        S.op('dve', lambda e: e.memset(onesb[:], 1.0), [], [onesbB])
        epsg, epsgB = sb("epsg", [128, 1], F32)
        S.op('dve', lambda e: e.memset(epsg[:], EPS), [], [epsgB])

        V = lambda fn, r, w: S.op('dve', fn, r, w)
        A = lambda fn, r, w: S.op('act', fn, r, w)
        P = lambda fn, r, w: S.op('pe', fn, r, w)
        G = lambda fn, r, w: S.op('pool', fn, r, w)

        def wload(dst, dstB, wsrc, c0, ncol, k0=0, nk=KC, dk0=0):
            src = wsrc[k0 * 128:(k0 + nk) * 128, c0:c0 + ncol].rearrange("(kc p) n -> p kc n", p=128)
            S.dma('pool', dst[:, dk0:dk0 + nk, 0:ncol], src, dstB, reads=[B["params"]], writes=[dstB])

        def stage_norm(xsrc, xB, gvec, hT, hTB, stg):
            with ExitStack() as s2:
                def sb2(name, shape, dt):
                    t = s2.enter_context(SBT(stg + name, list(shape), dt))
                    return t, Buf(stg + name)
                gbc, gbcB = sb2("gbc", [128, D], F32)
                S.dma('sp', gbc[:], gvec.partition_broadcast(128), gbcB, reads=[B["params"]], writes=[gbcB])
                xt = [sb2("xt%d" % i, [128, D], F32) for i in range(3)]
                junk, junkB = sb2("junk", [128, D], BF16)
                hb = [sb2("hb%d" % i, [128, D], BF16) for i in range(2)]
                st_ = [sb2("st%d" % i, [128, 4], F32) for i in range(3)]

                def load(tt):
                    x_t, x_B = xt[tt % 3]
                    S.dma('sp', x_t[:], xsrc[tt * 128:(tt + 1) * 128, :], x_B, reads=[xB], writes=[x_B])

                def stats(tt):
                    x_t, x_B = xt[tt % 3]
                    s_t, s_B = st_[tt % 3]
                    A(lambda e: e.activation(out=junk[:], in_=x_t[:], func=AF.Square, accum_out=s_t[:, 0:1]), [x_B], [junkB, s_B])
                    A(lambda e: e.copy(out=s_t[:, 1:2], in_=s_t[:, 0:1]), [s_B], [s_B])
                    A(lambda e: e.activation(out=s_t[:, 2:3], in_=s_t[:, 1:2], func=AF.Ln, bias=epsg[:, 0:1], scale=1.0 / D), [s_B, epsgB], [s_B])
                    A(lambda e: e.activation(out=s_t[:, 0:1], in_=s_t[:, 2:3], func=AF.Exp, scale=-0.5), [s_B], [s_B])

                load(0)
                load(1)
                stats(0)
                for tt in range(16):
                    x_t, x_B = xt[tt % 3]
                    h_t, h_B = hb[tt % 2]
                    s_t, s_B = st_[tt % 3]
                    if tt + 2 < 16:
                        load(tt + 2)
                    if tt + 1 < 16:
                        stats(tt + 1)
                    V(lambda e: e.scalar_tensor_tensor(out=h_t[:], in0=x_t[:], scalar=s_t[:, 0:1], in1=gbc[:], op0=ALU.mult, op1=ALU.mult), [x_B, s_B, gbcB], [h_B])
                    for q4 in range(4):
                        pt, pB = ps[(tt * 4 + q4) % 8]
                        ptb = pt[:].bitcast(BF16)
                        def tr(e, q4=q4, ptb=ptb):
                            ins = None
                            for i in range(4):
                                kc = q4 * 4 + i
                                ins = e.transpose(ptb[:, i * 128:(i + 1) * 128], h_t[:, kc * 128:(kc + 1) * 128], identb[:])
                            return ins
                        P(tr, [h_B, identbB], [pB])
                        dst = hT[:, q4 * 4:(q4 + 1) * 4, tt * 128:(tt + 1) * 128]
                        srcv = ptb[:, 0:512].rearrange("p (a b) -> p a b", a=4)
                        if q4 == 0:
                            V(lambda e, dst=dst, srcv=srcv: e.tensor_copy(dst, srcv), [pB], [hTB])
                        else:
                            G2 = A
                            G2(lambda e, dst=dst, srcv=srcv: e.copy(out=dst, in_=srcv), [pB], [hTB])

        def stage_inproj(l, hT, hTB):
            with ExitStack() as s2:
                def sb2(name, shape, dt):
                    t = s2.enter_context(SBT("B" + name, list(shape), dt))
                    return t, Buf("B" + name)
                wb = [sb2("w%d" % i, [128, KC, 512], BF16) for i in range(2)]
                stg = [sb2("stg%d" % i, [128, 4, L], BF16) for i in range(2)]
                cosT, cosB = sb2("cos", [128, L], F32)
                sinT, sinB = sb2("sin", [128, L], F32)
                tmp = [sb2("tmp%d" % i, [128, 512], F32) for i in range(4)]
                S.dma('sp', cosT[:], c_cos[:, :], cosB, writes=[cosB])
                S.dma('sp', sinT[:], c_sin[:, :], sinB, writes=[sinB])
                psi = [0]

                def nextps():
                    r = ps[psi[0] % 6]
                    psi[0] += 1
                    return r
                for cg in range(24):
                    w_t, w_B = wb[cg % 2]
                    g_t, g_B = stg[cg % 2]
                    wload(w_t, w_B, w_in[l], cg * 512, 512)

                    def mm_feat(pt, nc_, tg, w_t=w_t):
                        def f(e):
                            ins = None
                            for kc in range(KC):
                                ins = e.matmul(pt[:], w_t[:, kc, nc_ * 128:(nc_ + 1) * 128], hT[:, kc, tg * 512:(tg + 1) * 512], start=(kc == 0), stop=(kc == KC - 1))
                            return ins
                        return f
                    if cg < 4:
                        isk = cg >= 2
                        sc = (1.0 / 16.0) if isk else 1.0
                        for tg in range(4):
                            tsl = slice(tg * 512, (tg + 1) * 512)
                            for hh in range(2):
                                p1, p1B = nextps()
                                p2, p2B = nextps()
                                P(mm_feat(p1, 2 * hh, tg), [w_B, hTB], [p1B])
                                P(mm_feat(p2, 2 * hh + 1, tg), [w_B, hTB], [p2B])
                                (t1, t1B), (t2, t2B), (t3, t3B), (t4, t4B) = tmp
                                V(lambda e: e.scalar_tensor_tensor(out=t1[:], in0=p1[:], scalar=sc, in1=cosT[:, tsl], op0=ALU.mult, op1=ALU.mult), [p1B, cosB], [t1B])
                                V(lambda e: e.scalar_tensor_tensor(out=t2[:], in0=p2[:], scalar=sc, in1=sinT[:, tsl], op0=ALU.mult, op1=ALU.mult), [p2B, sinB], [t2B])
                                V(lambda e: e.scalar_tensor_tensor(out=t3[:], in0=p1[:], scalar=sc, in1=sinT[:, tsl], op0=ALU.mult, op1=ALU.mult), [p1B, sinB], [t3B])
                                V(lambda e: e.scalar_tensor_tensor(out=t4[:], in0=p2[:], scalar=sc, in1=cosT[:, tsl], op0=ALU.mult, op1=ALU.mult), [p2B, cosB], [t4B])
                                G(lambda e: e.tensor_tensor(out=g_t[:, 2 * hh, tsl], in0=t1[:], in1=t2[:], op=ALU.subtract), [t1B, t2B], [g_B])
                                G(lambda e: e.tensor_tensor(out=g_t[:, 2 * hh + 1, tsl], in0=t3[:], in1=t4[:], op=ALU.add), [t3B, t4B], [g_B])
                        dd, dB = (kT_d, B["kT"]) if isk else (qT_d, B["qT"])
                        r0 = (cg % 2) * 512
                        S.dma('sp', dd[r0:r0 + 512, :].rearrange("(a p) t -> p a t", p=128), g_t[:], g_B, reads=[g_B], writes=[dB])
                    elif cg < 8:
                        for tt in range(16):
                            pt, pB = nextps()
                            def f(e, pt=pt, tt=tt, w_t=w_t):
                                ins = None
                                for kc in range(KC):
                                    ins = e.matmul(pt[:], hT[:, kc, tt * 128:(tt + 1) * 128], w_t[:, kc, :], start=(kc == 0), stop=(kc == KC - 1))
                                return ins
                            P(f, [w_B, hTB], [pB])
                            dst = g_t[:, tt // 4, (tt % 4) * 512:(tt % 4 + 1) * 512]
                            if tt % 2 == 0:
                                V(lambda e, dst=dst, pt=pt: e.tensor_copy(dst, pt[:]), [pB], [g_B])
                            else:
                                A(lambda e, dst=dst, pt=pt: e.copy(out=dst, in_=pt[:]), [pB], [g_B])
                        c0 = (cg - 4) * 512
                        S.dma('sp', v_d[:, c0:c0 + 512].rearrange("(a b p) n -> p a b n", p=128, b=4),
                              g_t[:].rearrange("p a (b n) -> p a b n", b=4), g_B, reads=[g_B], writes=[B["v"]])
                    else:
                        grp = (cg - 8) // 4
                        func = [AF.Silu, AF.Copy, AF.Sigmoid, AF.Sigmoid][grp]
                        dd, dB = [(sgT_d, B["sgT"]), (uT_d, B["uT"]), (srT_d, B["srT"]), (ssT_d, B["ssT"])][grp]
                        for nc_ in range(4):
                            for tg in range(4):
                                pt, pB = nextps()
                                P(mm_feat(pt, nc_, tg), [w_B, hTB], [pB])
                                dst = g_t[:, nc_, tg * 512:(tg + 1) * 512]
                                if func == AF.Copy and (tg % 2 == 0):
                                    V(lambda e, dst=dst, pt=pt: e.tensor_copy(dst, pt[:]), [pB], [g_B])
                                else:
                                    A(lambda e, dst=dst, pt=pt, func=func: e.activation(out=dst, in_=pt[:], func=func), [pB], [g_B])
                        r0 = ((cg - 8) % 4) * 512
                        S.dma('sp', dd[r0:r0 + 512, :].rearrange("(a p) t -> p a t", p=128), g_t[:], g_B, reads=[g_B], writes=[dB])

        def stage_ret(l):
            with ExitStack() as s2:
                def sb2(name, shape, dt):
                    t = s2.enter_context(SBT("C" + name, list(shape), dt))
                    return t, Buf("C" + name)
                dtab, dtabB = sb2("dtab", [128, MW], F32)
                S.dma('sp', dtab[:], c_dtab[:, :], dtabB, writes=[dtabB])
                lgt, lgB = sb2("lg", [128, 16], F32)
                S.dma('sp', lgt[:, 0:8], lg_in[l:l + 1, :].partition_broadcast(128), lgB, reads=[B["params"]], writes=[lgB])
                V(lambda e: e.tensor_scalar(out=lgt[:, 8:16], in0=lgt[:, 0:8], scalar1=-1.0, scalar2=None, op0=ALU.mult), [lgB], [lgB])
                m1, m1B = sb2("m1", [128, MW], F32)
                m2, m2B = sb2("m2", [128, MW], F32)
                tm, tmB = sb2("tm", [128, MW], F32)
                qh2 = [sb2("qh%d" % i, [128, 2, L], BF16) for i in range(2)]
                kh2 = [sb2("kh%d" % i, [128, 2, L], BF16) for i in range(2)]
                vh2 = [sb2("vh%d" % i, [128, 16, 512], BF16) for i in range(2)]
                sg, sgB = sb2("sg", [128, 4, L], BF16)
                sr, srB = sb2("sr", [128, 4, L], BF16)
                mr, mrB = sb2("mr", [128, 4, L], BF16)
                pT = [sb2("pT%d" % i, [128, 512], BF16) for i in range(3)]
                sq, sqB = sb2("sq", [128, 4, 512], BF16)
                yc, ycB = sb2("yc", [128, 4, 512], F32)
                rs, rsB = sb2("rs", [128, 512], F32)
                rs2, rs2B = sb2("rs2", [128, 512], F32)
                epsc, epscB = sb2("epsc", [128, 1], F32)
                V(lambda e: e.memset(epsc[:], EPS), [], [epscB])
                def load_qkv(hd):
                    qh, qhB = qh2[hd % 2]
                    kh, khB = kh2[hd % 2]
                    vh, vhB = vh2[hd % 2]
                    S.dma('sp', qh[:], qT_d[hd * 256:(hd + 1) * 256, :].rearrange("(a p) t -> p a t", p=128), qhB, reads=[B["qT"]], writes=[qhB])
                    S.dma('sp', kh[:], kT_d[hd * 256:(hd + 1) * 256, :].rearrange("(a p) t -> p a t", p=128), khB, reads=[B["kT"]], writes=[khB])
                    S.dma('sp', vh[:], v_d[:, hd * 512:(hd + 1) * 512].rearrange("(a p) e -> p a e", p=128), vhB, reads=[B["v"]], writes=[vhB])

                for hd in range(4):
                    qh, qhB = qh2[hd % 2]
                    kh, khB = kh2[hd % 2]
                    vh, vhB = vh2[hd % 2]
                    V(lambda e: e.tensor_scalar(out=m1[:], in0=dtab[:], scalar1=lgt[:, hd:hd + 1], scalar2=None, op0=ALU.mult), [dtabB, lgB], [m1B])
                    V(lambda e: e.tensor_scalar(out=m2[:], in0=dtab[:], scalar1=lgt[:, 12 + hd:13 + hd], scalar2=None, op0=ALU.mult), [dtabB, lgB], [m2B])
                    V(lambda e: e.tensor_tensor(out=m1[:], in0=m1[:], in1=m2[:], op=ALU.min), [m1B, m2B], [m1B])
                    A(lambda e: e.activation(out=tm[:], in_=m1[:], func=AF.Exp), [m1B], [tmB])
                    if hd == 0:
                        load_qkv(0)
                    if hd + 1 < 4:
                        load_qkv(hd + 1)
                    S.dma('sp', sg[:], sgT_d[hd * 512:(hd + 1) * 512, :].rearrange("(a p) t -> p a t", p=128), sgB, reads=[B["sgT"]], writes=[sgB])
                    S.dma('sp', sr[:], srT_d[hd * 512:(hd + 1) * 512, :].rearrange("(a p) t -> p a t", p=128), srB, reads=[B["srT"]], writes=[srB])
                    V(lambda e: e.tensor_tensor(out=sg[:], in0=sg[:], in1=sr[:], op=ALU.mult), [sgB, srB], [sgB])
                    SBANK = [4, 5, 7]

                    def issue_S(it):
                        tg_, sc_ = it // 16, it % 16
                        pS, pSB = ps[SBANK[it % 3]]
                        tsl_ = slice(tg_ * 512, (tg_ + 1) * 512)
                        def fs(e):
                            ins = None
                            for dc in range(2):
                                ins = e.matmul(pS[:], kh[:, dc, sc_ * 128:(sc_ + 1) * 128], qh[:, dc, tsl_], start=(dc == 0), stop=(dc == 1))
                            return ins
                        P(fs, [khB, qhB], [pSB])

                    issue_S(0)
                    issue_S(1)
                    for tg in range(4):
                        tsl = slice(tg * 512, (tg + 1) * 512)
                        for sc in range(16):
                            it = tg * 16 + sc
                            pS, pSB = ps[SBANK[it % 3]]
                            p_t, p_B = pT[it % 3]
                            off = 512 * tg - 128 * sc + MOFF
                            V(lambda e, p_t=p_t, pS=pS, off=off: e.tensor_tensor(out=p_t[:], in0=pS[:], in1=tm[:, off:off + 512], op=ALU.mult), [pSB, tmB], [p_B])
                            def fy(e, p_t=p_t, sc=sc):
                                ins = None
                                for ec in range(4):
                                    ins = e.matmul(ps[ec][0][:], vh[:, sc, ec * 128:(ec + 1) * 128], p_t[:], start=(sc == 0), stop=(sc == 15))
                                return ins
                            P(fy, [vhB, p_B], [ps[0][1], ps[1][1], ps[2][1], ps[3][1]])
                            if it + 2 < 64:
                                issue_S(it + 2)
                        for ec in range(4):
                            A(lambda e, ec=ec: e.activation(out=sq[:, ec, :], in_=ps[ec][0][:], func=AF.Square), [ps[ec][1]], [sqB])
                            A(lambda e, ec=ec: e.copy(out=yc[:, ec, :], in_=ps[ec][0][:]), [ps[ec][1]], [ycB])
                        pq, pqB = ps[6]
                        def fq(e):
                            ins = None
                            for ec in range(4):
                                ins = e.matmul(pq[:], onesb[:], sq[:, ec, :], start=(ec == 0), stop=(ec == 3))
                            return ins
                        P(fq, [onesbB, sqB], [pqB])
                        A(lambda e: e.activation(out=rs2[:], in_=pq[:], func=AF.Ln, bias=epsc[:, 0:1], scale=1.0 / 512.0), [pqB, epscB], [rs2B])
                        A(lambda e: e.activation(out=rs[:], in_=rs2[:], func=AF.Exp, scale=-0.5), [rs2B], [rsB])
                        for ec in range(4):
                            G(lambda e, ec=ec: e.tensor_tensor(out=yc[:, ec, :], in0=yc[:, ec, :], in1=rs[:], op=ALU.mult), [ycB, rsB], [ycB])
                            G(lambda e, ec=ec: e.tensor_tensor(out=mr[:, ec, tsl], in0=yc[:, ec, :], in1=sg[:, ec, tsl], op=ALU.mult), [ycB, sgB], [mrB])
                    S.dma('sp', mrT_d[hd * 512:(hd + 1) * 512, :].rearrange("(a p) t -> p a t", p=128), mr[:], mrB, reads=[mrB], writes=[B["mrT"]])

        def stage_s5(l):
            with ExitStack() as s2:
                def sb2(name, shape, dt):
                    t = s2.enter_context(SBT("D" + name, list(shape), dt))
                    return t, Buf("D" + name)
                NPW = 16
                pw_idx = {0: 0, 1: 1, 2: 2, 3: 3, 4: 4, 5: 5, 6: 6, 7: 7, 8: 8, 16: 9, 32: 10, 64: 11, 128: 12, 256: 13, 512: 14, 1024: 15}
                PWr = [sb2("pwr%d" % d, [128, NPW, 64], F32) for d in range(2)]
                PWi = [sb2("pwi%d" % d, [128, NPW, 64], F32) for d in range(2)]
                PWn = [sb2("pwn%d" % d, [128, NPW, 64], F32) for d in range(2)]
                BDn = [[sb2("BDn%d%d" % (d, ri), [128, 64, 32], F32) for ri in range(2)] for d in range(2)]
                CTn = [[sb2("CTn%d%d" % (d, ri), [128, 64, 32], F32) for ri in range(2)] for d in range(2)]
                dsk, dskB = sb2("dsk", [128, 16], F32)
                bm2, bm2B = sb2("bm2", [128, 128], F32)
                S.dma('sp', bm2[:], c_bm2[:, :], bm2B, writes=[bm2B])
                vec16, vec16B = sb2("vec16", [16, 128], F32)
                S.dma('sp', vec16[:], ssm_d[l].rearrange("(c p) -> c p", p=128), vec16B, reads=[B["params"]], writes=[vec16B])
                P(lambda e: e.transpose(ps[7][0][:, 0:16], vec16[:], ident[0:16, 0:16]), [vec16B, identB], [ps[7][1]])
                V(lambda e: e.tensor_copy(dsk[:], ps[7][0][:, 0:16]), [ps[7][1]], [dskB])
                with ExitStack() as s3:
                    def sb3(name, shape, dt):
                        t = s3.enter_context(SBT("Dp" + name, list(shape), dt))
                        return t, Buf("Dp" + name)
                    T = {}
                    for nm in ["are", "aim", "ldt", "dtv", "xr", "th", "mag", "n", "r", "msk", "sn", "cs", "th2", "lre", "lim", "lm1", "nr", "ni", "den", "cr", "ci", "t0", "t1"]:
                        T[nm] = sb3(nm, [128, 64], F32)
                    ni32, ni32B = sb3("ni32", [128, 64], I32)
                    An, AnB = sb3("An", [128, 130], F32)
                    Am, AmB = sb3("Am", [128, 128], F32)
                    par, parB = sb3("par", [128, 2], F32)
                    sel, selB = sb3("sel", [128, 64], F32)
                    onesf, onesfB = sb3("onesf", [128, 64], F32)
                    S.dma('sp', par[:], c_par[:, :], parB, writes=[parB])
                    S.dma('sp', sel[:], c_sel[:, :], selB, writes=[selB])
                    V(lambda e: e.memset(onesf[:], 1.0), [], [onesfB])
                    bre, breB = sb3("bre", [128, 64, 16], F32)
                    bim, bimB = sb3("bim", [128, 64, 16], F32)
                    t16a, t16aB = sb3("t16a", [128, 64, 16], F32)
                    t16b, t16bB = sb3("t16b", [128, 64, 16], F32)
                    CBr, CBrB = sb3("CBr", [32, 64, 128], F32)
                    CBi, CBiB = sb3("CBi", [32, 64, 128], F32)

                    def tt_(o, a, b, op):
                        V(lambda e: e.tensor_tensor(out=T[o][0][:], in0=T[a][0][:], in1=T[b][0][:], op=op), [T[a][1], T[b][1]], [T[o][1]])

                    def ts_(o, a, s1, s2v, op0, op1=None):
                        if op1 is None:
                            V(lambda e: e.tensor_scalar(out=T[o][0][:], in0=T[a][0][:], scalar1=s1, scalar2=None, op0=op0), [T[a][1]], [T[o][1]])
                        else:
                            V(lambda e: e.tensor_scalar(out=T[o][0][:], in0=T[a][0][:], scalar1=s1, scalar2=s2v, op0=op0, op1=op1), [T[a][1]], [T[o][1]])

                    def sine(o, a):
                        ts_("t0", a, 1.0 / TWO_PI, 0.5, ALU.mult, ALU.add)
                        V(lambda e: e.tensor_copy(ni32[:], T["t0"][0][:]), [T["t0"][1]], [ni32B])
                        V(lambda e: e.tensor_copy(T["n"][0][:], ni32[:]), [ni32B], [T["n"][1]])
                        V(lambda e: e.scalar_tensor_tensor(out=T["r"][0][:], in0=T["n"][0][:], scalar=-C1, in1=T[a][0][:], op0=ALU.mult, op1=ALU.add), [T["n"][1], T[a][1]], [T["r"][1]])
                        V(lambda e: e.scalar_tensor_tensor(out=T["r"][0][:], in0=T["n"][0][:], scalar=-C2, in1=T["r"][0][:], op0=ALU.mult, op1=ALU.add), [T["n"][1], T["r"][1]], [T["r"][1]])
                        ts_("msk", "r", math.pi, -TWO_PI, ALU.is_gt, ALU.mult)
                        tt_("r", "r", "msk", ALU.add)
                        ts_("msk", "r", -math.pi, TWO_PI, ALU.is_lt, ALU.mult)
                        tt_("r", "r", "msk", ALU.add)
                        ts_("r", "r", math.pi, -math.pi, ALU.min, ALU.max)
                        A(lambda e: e.activation(out=T[o][0][:], in_=T["r"][0][:], func=AF.Sin), [T["r"][1]], [T[o][1]])

                    for d in range(2):
                        for g2 in range(2):
                            S.dma('sp', bre[g2 * 64:(g2 + 1) * 64], b_re[l, d].rearrange("(r g) p h -> g p r h", g=2)[g2], breB, reads=[B["params"]], writes=[breB])
                            S.dma('sp', bim[g2 * 64:(g2 + 1) * 64], b_im[l, d].rearrange("(r g) p h -> g p r h", g=2)[g2], bimB, reads=[B["params"]], writes=[bimB])
                        S.dma('sp', An[:, 0:64], a_re[l, d], AnB, reads=[B["params"]], writes=[AnB])
                        S.dma('sp', An[:, 64:128], a_im[l, d], AnB, reads=[B["params"]], writes=[AnB])
                        S.dma('sp', An[:, 128:129], log_dt[l, d].rearrange("(g o) -> g o", o=1), AnB, reads=[B["params"]], writes=[AnB])
                        for si, nm in enumerate(["are", "aim", "ldt"]):
                            for g2 in range(2):
                                if si < 2:
                                    V(lambda e, g2=g2, si=si: e.tensor_scalar(out=Am[:, g2 * 64:(g2 + 1) * 64], in0=An[:, si * 64:(si + 1) * 64], scalar1=par[:, g2:g2 + 1], scalar2=None, op0=ALU.mult), [AnB, parB], [AmB])
                                else:
                                    V(lambda e, g2=g2: e.tensor_scalar(out=Am[:, g2 * 64:(g2 + 1) * 64], in0=onesf[:], scalar1=An[:, 128:129], scalar2=par[:, g2:g2 + 1], op0=ALU.mult, op1=ALU.mult), [AnB, parB, onesfB], [AmB])
                            P(lambda e: e.matmul(ps[6][0][:, 0:64], Am[:], sel[:], start=True, stop=True), [AmB, selB], [ps[6][1]])
                            V(lambda e, nm=nm: e.tensor_copy(T[nm][0][:], ps[6][0][:, 0:64]), [ps[6][1]], [T[nm][1]])
                        A(lambda e: e.activation(out=T["dtv"][0][:], in_=T["ldt"][0][:], func=AF.Exp), [T["ldt"][1]], [T["dtv"][1]])
                        tt_("xr", "are", "dtv", ALU.mult)
                        tt_("th", "aim", "dtv", ALU.mult)
                        A(lambda e: e.activation(out=T["mag"][0][:], in_=T["xr"][0][:], func=AF.Exp), [T["xr"][1]], [T["mag"][1]])
                        sine("sn", "th")
                        ts_("th2", "th", math.pi / 2.0, None, ALU.add)
                        sine("cs", "th2")
                        tt_("lre", "mag", "cs", ALU.mult)
                        tt_("lim", "mag", "sn", ALU.mult)
                        ts_("lm1", "lre", -1.0, None, ALU.add)
                        tt_("t0", "lm1", "are", ALU.mult)
                        tt_("t1", "lim", "aim", ALU.mult)
                        tt_("nr", "t0", "t1", ALU.add)
                        tt_("t0", "lim", "are", ALU.mult)
                        tt_("t1", "lm1", "aim", ALU.mult)
                        tt_("ni", "t0", "t1", ALU.subtract)
                        tt_("t0", "are", "are", ALU.mult)
                        tt_("t1", "aim", "aim", ALU.mult)
                        tt_("den", "t0", "t1", ALU.add)
                        V(lambda e: e.reciprocal(out=T["t0"][0][:], in_=T["den"][0][:]), [T["den"][1]], [T["t0"][1]])
                        tt_("cr", "nr", "t0", ALU.mult)
                        tt_("ci", "ni", "t0", ALU.mult)
                        crb = T["cr"][0][:].unsqueeze(2).to_broadcast([128, 64, 16])
                        cib = T["ci"][0][:].unsqueeze(2).to_broadcast([128, 64, 16])
                        (BDr, BDrB), (BDi, BDiB) = BDn[d]
                        V(lambda e: e.memset(BDr[:], 0.0), [], [BDrB])
                        V(lambda e: e.memset(BDi[:], 0.0), [], [BDiB])
                        V(lambda e: e.tensor_tensor(out=t16a[:], in0=bre[:], in1=crb, op=ALU.mult), [breB, T["cr"][1]], [t16aB])
                        V(lambda e: e.tensor_tensor(out=t16b[:], in0=bim[:], in1=cib, op=ALU.mult), [bimB, T["ci"][1]], [t16bB])
                        for g2 in range(2):
                            V(lambda e, g2=g2: e.tensor_tensor(out=BDr[g2 * 64:(g2 + 1) * 64, :, g2 * 16:(g2 + 1) * 16], in0=t16a[g2 * 64:(g2 + 1) * 64], in1=t16b[g2 * 64:(g2 + 1) * 64], op=ALU.subtract), [t16aB, t16bB], [BDrB])
                        V(lambda e: e.tensor_tensor(out=t16a[:], in0=bim[:], in1=crb, op=ALU.mult), [bimB, T["cr"][1]], [t16aB])
                        V(lambda e: e.tensor_tensor(out=t16b[:], in0=bre[:], in1=cib, op=ALU.mult), [breB, T["ci"][1]], [t16bB])
                        for g2 in range(2):
                            V(lambda e, g2=g2: e.tensor_tensor(out=BDi[g2 * 64:(g2 + 1) * 64, :, g2 * 16:(g2 + 1) * 16], in0=t16a[g2 * 64:(g2 + 1) * 64], in1=t16b[g2 * 64:(g2 + 1) * 64], op=ALU.add), [t16aB, t16bB], [BDiB])
                        V(lambda e: e.memset(CBr[:], 0.0), [], [CBrB])
                        V(lambda e: e.memset(CBi[:], 0.0), [], [CBiB])
                        for g2 in range(2):
                            S.dma('sp', CBr[g2 * 16:(g2 + 1) * 16, :, g2 * 64:(g2 + 1) * 64], c_re[l, d].rearrange("(r g) h p -> g h r p", g=2)[g2], CBrB, reads=[B["params"]], writes=[CBrB])
                            S.dma('sp', CBi[g2 * 16:(g2 + 1) * 16, :, g2 * 64:(g2 + 1) * 64], c_im[l, d].rearrange("(r g) h p -> g h r p", g=2)[g2], CBiB, reads=[B["params"]], writes=[CBiB])
                        for ri, (cb, cbB) in enumerate([(CBr, CBrB), (CBi, CBiB)]):
                            ctn, ctnB = CTn[d][ri]
                            for q in range(4):
                                pt, pB = ps[4 + q % 2]
                                def ftr(e, pt=pt, cb=cb, q=q):
                                    ins = None
                                    for i in range(16):
                                        ins = e.transpose(pt[:, i * 32:(i + 1) * 32], cb[:, q * 16 + i, :], ident[0:32, 0:32])
                                    return ins
                                P(ftr, [cbB, identB], [pB])
                                V(lambda e, pt=pt, q=q, ctn=ctn: e.tensor_copy(ctn[:, q * 16:(q + 1) * 16, :], pt[:].rearrange("p (a b) -> p a b", b=32)), [pB], [ctnB])
                        pr_, prB = PWr[d]
                        pi_, piB = PWi[d]
                        pn_, pnB = PWn[d]
                        V(lambda e: e.memset(pr_[:, 0, :], 1.0), [], [prB])
                        V(lambda e: e.memset(pi_[:, 0, :], 0.0), [], [piB])
                        V(lambda e: e.tensor_copy(pr_[:, 1, :], T["lre"][0][:]), [T["lre"][1]], [prB])
                        V(lambda e: e.tensor_copy(pi_[:, 1, :], T["lim"][0][:]), [T["lim"][1]], [piB])

                        def cmul(o, a, b):
                            t0 = T["t0"][0]
                            t1 = T["t1"][0]
                            V(lambda e: e.tensor_tensor(out=t0[:], in0=pr_[:, a, :], in1=pr_[:, b, :], op=ALU.mult), [prB], [T["t0"][1]])
                            V(lambda e: e.tensor_tensor(out=t1[:], in0=pi_[:, a, :], in1=pi_[:, b, :], op=ALU.mult), [piB], [T["t1"][1]])
                            V(lambda e: e.tensor_tensor(out=pr_[:, o, :], in0=t0[:], in1=t1[:], op=ALU.subtract), [T["t0"][1], T["t1"][1]], [prB])
                            V(lambda e: e.tensor_tensor(out=t0[:], in0=pr_[:, a, :], in1=pi_[:, b, :], op=ALU.mult), [prB, piB], [T["t0"][1]])
                            V(lambda e: e.tensor_tensor(out=t1[:], in0=pi_[:, a, :], in1=pr_[:, b, :], op=ALU.mult), [prB, piB], [T["t1"][1]])
                            V(lambda e: e.tensor_tensor(out=pi_[:, o, :], in0=t0[:], in1=t1[:], op=ALU.add), [T["t0"][1], T["t1"][1]], [piB])
                        for k in range(2, 9):
                            cmul(k, k - 1, 1)
                        for k in range(9, 16):
                            cmul(k, k - 1, k - 1)
                        V(lambda e: e.tensor_scalar(out=pn_[:], in0=pi_[:], scalar1=-1.0, scalar2=None, op0=ALU.mult), [piB], [pnB])
                S.barrier()
                BDa = [sb2("BDa%d" % ri, [128, 2, 8, 128], BF16) for ri in range(2)]
                w1, w1B = sb2("w1", [128, 9, 128], F32)
                w2, w2B = sb2("w2", [128, 9, 128], F32)
                BTc = sb2("BTc", [128, 32, 128], BF16)
                CTc = [sb2("CTc%d" % i, [128, 2, 2, 9, 128], BF16) for i in range(2)]
                KTc = sb2("KTc", [128, 16, 128], BF16)
                uTs = [sb2("uT%d" % i, [128, L], BF16) for i in range(2)]
                PADW = 128
                VS = [[[sb2("VS%d%d%d" % (q, d, i), [128, 2, PADW + 256], F32) for i in range(2)] for d in range(2)] for q in range(2)]
                for q in range(2):
                    for d in range(2):
                        for i in range(2):
                            G(lambda e, q=q, d=d, i=i: e.memset(VS[q][d][i][0][:], 0.0), [], [VS[q][d][i][1]])
                XBt = [[sb2("XB%d%d" % (d, i), [128, 2, 256], BF16) for i in range(4)] for d in range(2)]
                ysf, ysfB = sb2("ysf", [128, L], F32)
                g1, g1B = sb2("g1", [128, 1024], F32)
                yst, ystB = sb2("yst", [128, L], BF16)
                identbf = identb
                GC = 2.0 * math.sqrt(2.0 / math.pi)
                Yb = [ps[i][1] for i in range(4)]
                bt_t, bt_B = BTc
                kt_t, kt_B = KTc

                def Yap(t):
                    return ps[t // 2][0][:, (t % 2) * 256:(t % 2 + 1) * 256]

                def prepA(ck):
                    pr0 = ck * 4
                    ct_t, ct_B = CTc[ck % 2]
                    for d in range(2):
                        pwB = [PWr[d][1], PWi[d][1], PWn[d][1]]
                        for isC in (False, True):
                            nk = 9 if isC else 8
                            src = CTn[d] if isC else BDn[d]
                            Prb = PWr[d][0][:, 0:nk, pr0:pr0 + 4].unsqueeze(3).to_broadcast([128, nk, 4, 32])
                            Pib = PWi[d][0][:, 0:nk, pr0:pr0 + 4].unsqueeze(3).to_broadcast([128, nk, 4, 32])
                            Pnb = PWn[d][0][:, 0:nk, pr0:pr0 + 4].unsqueeze(3).to_broadcast([128, nk, 4, 32])
                            sre = src[0][0][:, pr0:pr0 + 4, :].unsqueeze(1).to_broadcast([128, nk, 4, 32])
                            sim = src[1][0][:, pr0:pr0 + 4, :].unsqueeze(1).to_broadcast([128, nk, 4, 32])
                            sB = [src[0][1], src[1][1]]
                            a1 = w1[:, 0:nk, :].rearrange("p k (a b) -> p k a b", b=32)
                            a2 = w2[:, 0:nk, :].rearrange("p k (a b) -> p k a b", b=32)
                            if isC:
                                ore = ct_t[:, d, 0, :, :].rearrange("p k (a b) -> p k a b", b=32)
                                oim = ct_t[:, d, 1, :, :].rearrange("p k (a b) -> p k a b", b=32)
                                oB = [ct_B, ct_B]
                            else:
                                ore = BDa[0][0][:, d, :, :].rearrange("p k (a b) -> p k a b", b=32)
                                oim = BDa[1][0][:, d, :, :].rearrange("p k (a b) -> p k a b", b=32)
                                oB = [BDa[0][1], BDa[1][1]]
                            G(lambda e: e.tensor_tensor(out=a1, in0=sre, in1=Prb, op=ALU.mult), sB + pwB, [w1B])
                            G(lambda e: e.tensor_tensor(out=a2, in0=sim, in1=Pib, op=ALU.mult), sB + pwB, [w2B])
                            G(lambda e: e.tensor_tensor(out=ore, in0=a1, in1=a2, op=ALU.subtract), [w1B, w2B], [oB[0]])
                            if not isC:
                                G(lambda e: e.tensor_tensor(out=a1, in0=sre, in1=Pib, op=ALU.mult), sB + pwB, [w1B])
                                G(lambda e: e.tensor_tensor(out=a2, in0=sim, in1=Prb, op=ALU.mult), sB + pwB, [w2B])
                                G(lambda e: e.tensor_tensor(out=oim, in0=a1, in1=a2, op=ALU.add), [w1B, w2B], [oB[1]])
                            else:
                                G(lambda e: e.tensor_tensor(out=a1, in0=sre, in1=Pnb, op=ALU.mult), sB + pwB, [w1B])
                                G(lambda e: e.tensor_tensor(out=a2, in0=sim, in1=Prb, op=ALU.mult), sB + pwB, [w2B])
                                G(lambda e: e.tensor_tensor(out=oim, in0=a1, in1=a2, op=ALU.subtract), [w1B, w2B], [oB[1]])

                itp = [0]

                def prepB(ck):
                    ct_t, ct_B = CTc[ck % 2]
                    for d in range(2):
                        for ri in range(2):
                            pt, pB = ps[6 + itp[0] % 2]
                            itp[0] += 1
                            ptb = pt[:].bitcast(BF16)
                            def ftr(e, ptb=ptb, d=d, ri=ri):
                                ins = None
                                for k in range(8):
                                    ins = e.transpose(ptb[:, k * 128:(k + 1) * 128], BDa[ri][0][:, d, k, :], identbf[:])
                                return ins
                            P(ftr, [BDa[ri][1], identbB], [pB])
                            i0_ = (d * 2 + ri) * 8
                            A(lambda e, ptb=ptb, i0_=i0_: e.copy(out=bt_t[:, i0_:i0_ + 8, :], in_=ptb.rearrange("p (a b) -> p a b", b=128)), [pB], [bt_B])
                def prepB_KT(ck):
                    ct_t, ct_B = CTc[ck % 2]
                    for d in range(2):
                        for t4 in range(2):
                            pt, pB = ps[6 + itp[0] % 2]
                            itp[0] += 1
                            def fk(e, pt=pt, d=d, t4=t4):
                                ins = None
                                for i in range(4):
                                    tau = t4 * 4 + i
                                    e.matmul(pt[:, i * 128:(i + 1) * 128], BDa[0][0][:, d, 0, :], ct_t[:, d, 0, tau, :], start=True, stop=False)
                                    ins = e.matmul(pt[:, i * 128:(i + 1) * 128], BDa[1][0][:, d, 0, :], ct_t[:, d, 1, tau, :], start=False, stop=True)
                                return ins
                            P(fk, [BDa[0][1], BDa[1][1], ct_B], [pB])
                            V(lambda e, pt=pt, d=d, t4=t4: e.tensor_tensor(out=kt_t[:, d * 8 + t4 * 4:d * 8 + t4 * 4 + 4, :], in0=pt[:].rearrange("p (a b) -> p a b", b=128), in1=bm2[:].unsqueeze(1).to_broadcast([128, 4, 128]), op=ALU.mult), [pB, bm2B], [kt_B])

                prepA(0)
                for ck in range(16):
                    pr0 = ck * 4
                    u_t, u_B = uTs[ck % 2]
                    ct_t, ct_B = CTc[ck % 2]
                    S.dma('sp', u_t[:], uT_d[ck * 128:(ck + 1) * 128, :], u_B, reads=[B["uT"]], writes=[u_B])
                    u3 = u_t[:].rearrange("p (c j) -> p c j", j=8)
                    prepB(ck)

                    def intra():
                      for t in range(8):
                        def fi(e, t=t):
                            ins = None
                            first = (t % 2 == 0)
                            for j in range(0, t + 1):
                                ins = e.matmul(Yap(t), kt_t[:, 0 * 8 + (t - j), :], u3[:, :, j], start=first, stop=True, skip_group_check=True)
                                first = False
                            for j in range(t, 8):
                                ins = e.matmul(Yap(t), kt_t[:, 1 * 8 + (j - t), :], u3[:, :, j], start=False, stop=True, skip_group_check=True)
                            return ins
                        P(fi, [kt_B, u_B], [Yb[t // 2]])

                    def SI(pr4):
                        rs_ = slice(pr4 * 32, (pr4 + 1) * 32)
                        for d in range(2):
                            vp, vpB = ps[4 + d]
                            def fv(e, vp=vp, d=d):
                                ins = None
                                for ri in range(2):
                                    for j in range(8):
                                        k = (7 - j) if d == 0 else j
                                        ins = e.matmul(vp[:, ri * 256:(ri + 1) * 256], bt_t[rs_, (d * 2 + ri) * 8 + k, :], u3[rs_, :, j], start=(j == 0), stop=(j == 7), tile_position=(pr4 * 32, 0))
                                return ins
                            P(fv, [bt_B, u_B], [vpB])
                            v0, v0B = VS[pr4 % 2][d][0]
                            off = PADW if d == 0 else 0
                            A(lambda e, vp=vp, v0=v0, off=off: e.copy(out=v0[:, :, off:off + 256], in_=vp[:].rearrange("p (a b) -> p a b", b=256)), [vpB], [v0B])

                    def HS_SO(pr4):
                        pr = pr0 + pr4
                        rs_ = slice(pr4 * 32, (pr4 + 1) * 32)
                        cur = {0: 0, 1: 0}
                        dd = 1
                        while dd < 256:
                            k = pw_idx[8 * dd]
                            ctx = []
                            for d in range(2):
                                s_t, s_B = VS[pr4 % 2][d][cur[d]]
                                t_t, t_B = VS[pr4 % 2][d][1 - cur[d]]
                                off = PADW if d == 0 else 0
                                so = off - dd if d == 0 else off + dd
                                ar = PWr[d][0][:, k, pr:pr + 1]
                                ai = PWi[d][0][:, k, pr:pr + 1]
                                an = PWn[d][0][:, k, pr:pr + 1]
                                pwB = [PWr[d][1], PWi[d][1], PWn[d][1]]
                                ctx.append((s_t, s_B, t_t, t_B, off, so, ar, ai, an, pwB))
                                cur[d] = 1 - cur[d]
                            for (s_t, s_B, t_t, t_B, off, so, ar, ai, an, pwB) in ctx:
                                V(lambda e: e.scalar_tensor_tensor(out=t_t[:, :, off:off + 256], in0=s_t[:, :, so:so + 256], scalar=ar, in1=s_t[:, :, off:off + 256], op0=ALU.mult, op1=ALU.add), [s_B] + pwB, [t_B])
                            for (s_t, s_B, t_t, t_B, off, so, ar, ai, an, pwB) in ctx:
                                V(lambda e: e.scalar_tensor_tensor(out=t_t[:, 0, off:off + 256], in0=s_t[:, 1, so:so + 256], scalar=an, in1=t_t[:, 0, off:off + 256], op0=ALU.mult, op1=ALU.add), [s_B, t_B] + pwB, [t_B])
                            for (s_t, s_B, t_t, t_B, off, so, ar, ai, an, pwB) in ctx:
                                V(lambda e: e.scalar_tensor_tensor(out=t_t[:, 1, off:off + 256], in0=s_t[:, 0, so:so + 256], scalar=ai, in1=t_t[:, 1, off:off + 256], op0=ALU.mult, op1=ALU.add), [s_B, t_B] + pwB, [t_B])
                            dd *= 2
                        for d in range(2):
                            f_t, f_B = VS[pr4 % 2][d][cur[d]]
                            off = PADW if d == 0 else 0
                            xb_t, xb_B = XBt[d][pr4]
                            A(lambda e, f_t=f_t, off=off, xb_t=xb_t: e.copy(out=xb_t[:], in_=f_t[:, :, off:off + 256]), [f_B], [xb_B])

                    def SO(prs):
                        def fo(e):
                            ins = None
                            for d in range(2):
                                for t in range(8):
                                    ex = (t + 1) if d == 0 else (8 - t)
                                    for ri in range(2):
                                        for pr4 in prs:
                                            rs_ = slice(pr4 * 32, (pr4 + 1) * 32)
                                            xb_t = XBt[d][pr4][0]
                                            if d == 0:
                                                o_ = ps[t // 2][0][rs_, (t % 2) * 256 + 1:(t % 2) * 256 + 256]
                                                r_ = xb_t[:, ri, 0:255]
                                            else:
                                                o_ = ps[t // 2][0][rs_, (t % 2) * 256:(t % 2) * 256 + 255]
                                                r_ = xb_t[:, ri, 1:256]
                                            ins = e.matmul(o_, ct_t[:, d, ri, ex, rs_], r_, start=False, stop=True, skip_group_check=True, tile_position=(0, pr4 * 32))
                            return ins
                        P(fo, [ct_B] + [XBt[d][p_][1] for d in range(2) for p_ in prs], Yb)

                    SI(0)
                    SI(1)
                    prepB_KT(ck)
                    if ck + 1 < 16:
                        prepA(ck + 1)
                    intra()
                    HS_SO(0)
                    SI(2)
                    HS_SO(1)
                    SO([0, 1])
                    SI(3)
                    HS_SO(2)
                    HS_SO(3)
                    SO([2, 3])
                    ysf3 = ysf[:].rearrange("p (c j) -> p c j", j=8)
                    for t in range(8):
                        V(lambda e, t=t: e.scalar_tensor_tensor(out=ysf3[:, :, t], in0=u3[:, :, t], scalar=dsk[:, ck:ck + 1], in1=Yap(t), op0=ALU.mult, op1=ALU.add), [u_B, dskB, Yb[t // 2]], [ysfB])
                    for hf in range(2):
                        hs = slice(hf * 1024, (hf + 1) * 1024)
                        A(lambda e: e.activation(out=g1[:], in_=ysf[:, hs], func=AF.Square), [ysfB], [g1B])
                        G(lambda e: e.tensor_scalar(out=g1[:], in0=g1[:], scalar1=0.044715, scalar2=1.0, op0=ALU.mult, op1=ALU.add), [g1B], [g1B])
                        G(lambda e: e.tensor_tensor(out=g1[:], in0=g1[:], in1=ysf[:, hs], op=ALU.mult), [g1B, ysfB], [g1B])
                        A(lambda e: e.activation(out=g1[:], in_=g1[:], func=AF.Sigmoid, scale=GC), [g1B], [g1B])
                        G(lambda e: e.tensor_tensor(out=yst[:, hs], in0=g1[:], in1=ysf[:, hs], op=ALU.mult), [g1B, ysfB], [ystB])
                    S.dma('sp', ysT_d[ck * 128:(ck + 1) * 128, :], yst[:], ystB, reads=[ystB], writes=[B["ysT"]])

        def stage_merge_out(l, xsrc, xsB, xdst, xdB):
            with ExitStack() as s2:
                def sb2(name, shape, dt):
                    t = s2.enter_context(SBT("E" + name, list(shape), dt))
                    return t, Buf("E" + name)
                ysT, ysTB = sb2("ysT", [128, KC, L], BF16)
                mg, mgB = sb2("mg", [128, KC, L], BF16)
                wb = [sb2("w%d" % i, [128, KC, 512], BF16) for i in range(2)]
                mrt = [sb2("mr%d" % i, [128, L], BF16) for i in range(2)]
                sst = [sb2("ss%d" % i, [128, L], BF16) for i in range(2)]
                sgl = [sb2("sgl%d" % i, [128, 512], F32) for i in range(2)]
                bg, bgB = sb2("bg", [128, 16], F32)
                xr_ = [sb2("xr%d" % i, [128, 512], F32) for i in range(2)]
                xo_ = [sb2("xo%d" % i, [128, 512], F32) for i in range(2)]
                vec16, vec16B = sb2("vec16", [16, 128], F32)
                S.dma('sp', vec16[:], b_glu[l].rearrange("(c p) -> c p", p=128), vec16B, reads=[B["params"]], writes=[vec16B])
                P(lambda e: e.transpose(ps[7][0][:, 0:16], vec16[:], ident[0:16, 0:16]), [vec16B, identB], [ps[7][1]])
                V(lambda e: e.tensor_copy(bg[:], ps[7][0][:, 0:16]), [ps[7][1]], [bgB])
                for kc in range(KC):
                    S.dma('sp', ysT[:, kc, :], ysT_d[kc * 128:(kc + 1) * 128, :], ysTB, reads=[B["ysT"]], writes=[ysTB])
                it = 0
                for cg in range(4):
                    w_t, w_B = wb[cg % 2]
                    wload(w_t, w_B, w_glu[l], cg * 512, 512)
                    for nc_ in range(4):
                        ch = cg * 4 + nc_
                        m_t, m_B = mrt[ch % 2]
                        s_t, s_B = sst[ch % 2]
                        S.dma('sp', m_t[:], mrT_d[ch * 128:(ch + 1) * 128, :], m_B, reads=[B["mrT"]], writes=[m_B])
                        S.dma('sp', s_t[:], ssT_d[ch * 128:(ch + 1) * 128, :], s_B, reads=[B["ssT"]], writes=[s_B])
                        for tg in range(4):
                            tsl = slice(tg * 512, (tg + 1) * 512)
                            pt, pB = ps[it % 4]
                            g_t, g_B = sgl[it % 2]
                            it += 1
                            def f(e, pt=pt, nc_=nc_, tsl=tsl, w_t=w_t):
                                ins = None
                                for kc in range(KC):
                                    ins = e.matmul(pt[:], w_t[:, kc, nc_ * 128:(nc_ + 1) * 128], ysT[:, kc, tsl], start=(kc == 0), stop=(kc == KC - 1))
                                return ins
                            P(f, [w_B, ysTB], [pB])
                            A(lambda e, pt=pt, g_t=g_t, ch=ch: e.activation(out=g_t[:], in_=pt[:], func=AF.Sigmoid, bias=bg[:, ch:ch + 1]), [pB, bgB], [g_B])
                            V(lambda e, g_t=g_t, ch=ch, tsl=tsl: e.tensor_tensor(out=g_t[:], in0=g_t[:], in1=ysT[:, ch, tsl], op=ALU.mult), [g_B, ysTB], [g_B])
                            V(lambda e, g_t=g_t, s_t=s_t, tsl=tsl: e.tensor_tensor(out=g_t[:], in0=g_t[:], in1=s_t[:, tsl], op=ALU.mult), [g_B, s_B], [g_B])
                            V(lambda e, g_t=g_t, m_t=m_t, ch=ch, tsl=tsl: e.tensor_tensor(out=mg[:, ch, tsl], in0=g_t[:], in1=m_t[:, tsl], op=ALU.add), [g_B, m_B], [mgB])
                it = 0
                for ng in range(4):
                    w_t, w_B = wb[ng % 2]
                    wload(w_t, w_B, w_out[l], ng * 512, 512)
                    for tt in range(16):
                        pt, pB = ps[4 + it % 4]
                        xr_t, xr_B = xr_[it % 2]
                        xo_t, xo_B = xo_[it % 2]
                        it += 1
                        def f(e, pt=pt, tt=tt, w_t=w_t):
                            ins = None
                            for kc in range(KC):
                                ins = e.matmul(pt[:], mg[:, kc, tt * 128:(tt + 1) * 128], w_t[:, kc, :], start=(kc == 0), stop=(kc == KC - 1))
                            return ins
                        P(f, [w_B, mgB], [pB])
                        S.dma('sp', xr_t[:], xsrc[tt * 128:(tt + 1) * 128, ng * 512:(ng + 1) * 512], xr_B, reads=[xsB], writes=[xr_B])
                        V(lambda e, pt=pt, xr_t=xr_t, xo_t=xo_t: e.tensor_tensor(out=xo_t[:], in0=pt[:], in1=xr_t[:], op=ALU.add), [pB, xr_B], [xo_B])
                        S.dma('sp', xdst[tt * 128:(tt + 1) * 128, ng * 512:(ng + 1) * 512], xo_t[:], xo_B, reads=[xo_B], writes=[xdB])

        def stage_ffn_up(l, hT, hTB):
            with ExitStack() as s2:
                def sb2(name, shape, dt):
                    t = s2.enter_context(SBT("H" + name, list(shape), dt))
                    return t, Buf("H" + name)
                wg = [sb2("wg%d" % i, [128, KC, 512], BF16) for i in range(2)]
                wu = [sb2("wu%d" % i, [128, KC, 512], BF16) for i in range(2)]
                stg = [sb2("stg%d" % i, [128, 4, L], BF16) for i in range(2)]
                sl = [sb2("sl%d" % i, [128, 512], F32) for i in range(2)]
                it = 0
                for fg in range(11):
                    g_t, g_B = wg[fg % 2]
                    u_t, u_B = wu[fg % 2]
                    s_t, s_B = stg[fg % 2]
                    wload(g_t, g_B, w_fg[l], fg * 512, 512)
                    wload(u_t, u_B, w_fu[l], fg * 512, 512)
                    for fc in range(4):
                        for tg in range(4):
                            tsl = slice(tg * 512, (tg + 1) * 512)
                            pg, pgB = ps[(it % 4) * 2]
                            pu, puB = ps[(it % 4) * 2 + 1]
                            l_t, l_B = sl[it % 2]
                            it += 1
                            def f(w_t, pt, fc=fc, tsl=tsl):
                                def ff(e):
                                    ins = None
                                    for kc in range(KC):
                                        ins = e.matmul(pt[:], w_t[:, kc, fc * 128:(fc + 1) * 128], hT[:, kc, tsl], start=(kc == 0), stop=(kc == KC - 1))
                                    return ins
                                return ff
                            P(f(g_t, pg), [g_B, hTB], [pgB])
                            P(f(u_t, pu), [u_B, hTB], [puB])
                            A(lambda e, pg=pg, l_t=l_t: e.activation(out=l_t[:], in_=pg[:], func=AF.Silu), [pgB], [l_B])
                            V(lambda e, pu=pu, l_t=l_t, s_t=s_t, fc=fc, tsl=tsl: e.tensor_tensor(out=s_t[:, fc, tsl], in0=pu[:], in1=l_t[:], op=ALU.mult), [puB, l_B], [s_B])
                    S.dma('sp', aT_d[fg * 512:(fg + 1) * 512, :].rearrange("(a p) t -> p a t", p=128), s_t[:], s_B, reads=[s_B], writes=[B["aT"]])

        def stage_ffn_down(l, xsrc, xsB, xdst, xdB):
            with ExitStack() as s2:
                def sb2(name, shape, dt):
                    t = s2.enter_context(SBT("I" + name, list(shape), dt))
                    return t, Buf("I" + name)
                Ah, AhB = sb2("A", [128, FC, 1024], BF16)
                wd = [sb2("wd%d" % i, [128, FC, 512], BF16) for i in range(2)]
                xr_ = [sb2("xr%d" % i, [128, 512], F32) for i in range(2)]
                xo_ = [sb2("xo%d" % i, [128, 512], F32) for i in range(2)]
                it = 0
                wi = 0
                for half in range(2):
                    for f4 in range(4):
                        S.dma('sp', Ah[:, f4 * 11:(f4 + 1) * 11, :], aT_d[f4 * 11 * 128:(f4 + 1) * 11 * 128, half * 1024:(half + 1) * 1024].rearrange("(a p) t -> p a t", p=128), AhB, reads=[B["aT"]], writes=[AhB])
                    for ng in range(4):
                        w_t, w_B = wd[wi % 2]
                        wi += 1
                        for f4 in range(4):
                            wload(w_t, w_B, w_fd[l], ng * 512, 512, k0=f4 * 11, nk=11, dk0=f4 * 11)
                        for t8 in range(8):
                            tt = half * 8 + t8
                            pt, pB = ps[it % 4]
                            xr_t, xr_B = xr_[it % 2]
                            xo_t, xo_B = xo_[it % 2]
                            it += 1
                            def f(e, pt=pt, t8=t8, w_t=w_t):
                                ins = None
                                for fc in range(FC):
                                    ins = e.matmul(pt[:], Ah[:, fc, t8 * 128:(t8 + 1) * 128], w_t[:, fc, :], start=(fc == 0), stop=(fc == FC - 1))
                                return ins
                            P(f, [w_B, AhB], [pB])
                            S.dma('sp', xr_t[:], xsrc[tt * 128:(tt + 1) * 128, ng * 512:(ng + 1) * 512], xr_B, reads=[xsB], writes=[xr_B])
                            V(lambda e, pt=pt, xr_t=xr_t, xo_t=xo_t: e.tensor_tensor(out=xo_t[:], in0=pt[:], in1=xr_t[:], op=ALU.add), [pB, xr_B], [xo_B])
                            S.dma('sp', xdst[tt * 128:(tt + 1) * 128, ng * 512:(ng + 1) * 512], xo_t[:], xo_B, reads=[xo_B], writes=[xdB])

        def stage_final(xsrc, xsB):
            with ExitStack() as s2:
                def sb2(name, shape, dt):
                    t = s2.enter_context(SBT("Z" + name, list(shape), dt))
                    return t, Buf("Z" + name)
                gbc, gbcB = sb2("gbc", [128, D], F32)
                S.dma('sp', gbc[:], ln_final_g.partition_broadcast(128), gbcB, reads=[B["params"]], writes=[gbcB])
                xt = [sb2("xt%d" % i, [128, D], F32) for i in range(3)]
                ot = [sb2("ot%d" % i, [128, D], F32) for i in range(2)]
                junk, junkB = sb2("junk", [128, D], BF16)
                st_ = [sb2("st%d" % i, [128, 4], F32) for i in range(3)]

                def load(tt):
                    x_t, x_B = xt[tt % 3]
                    S.dma('sp', x_t[:], xsrc[tt * 128:(tt + 1) * 128, :], x_B, reads=[xsB], writes=[x_B])

                def stats(tt):
                    x_t, x_B = xt[tt % 3]
                    s_t, s_B = st_[tt % 3]
                    A(lambda e: e.activation(out=junk[:], in_=x_t[:], func=AF.Square, accum_out=s_t[:, 0:1]), [x_B], [junkB, s_B])
                    A(lambda e: e.copy(out=s_t[:, 1:2], in_=s_t[:, 0:1]), [s_B], [s_B])
                    A(lambda e: e.activation(out=s_t[:, 2:3], in_=s_t[:, 1:2], func=AF.Ln, bias=epsg[:, 0:1], scale=1.0 / D), [s_B, epsgB], [s_B])
                    A(lambda e: e.activation(out=s_t[:, 0:1], in_=s_t[:, 2:3], func=AF.Exp, scale=-0.5), [s_B], [s_B])

                load(0)
                load(1)
                stats(0)
                for tt in range(16):
                    x_t, x_B = xt[tt % 3]
                    o_t, o_B = ot[tt % 2]
                    s_t, s_B = st_[tt % 3]
                    if tt + 2 < 16:
                        load(tt + 2)
                    if tt + 1 < 16:
                        stats(tt + 1)
                    V(lambda e: e.scalar_tensor_tensor(out=o_t[:], in0=x_t[:], scalar=s_t[:, 0:1], in1=gbc[:], op0=ALU.mult, op1=ALU.mult), [x_B, s_B, gbcB], [o_B])
                    S.dma('sp', out_d[tt * 128:(tt + 1) * 128, :], o_t[:], o_B, reads=[o_B], writes=[B["out"]])

        def run():
            stages = dbg.get("stages")
            on = lambda n: (stages is None) or (n in stages)
            xcur, xcurB = x_in, B["x"]
            for l in range(nlayer):
                if on("norm1") or on("inproj"):
                    with ExitStack() as sh:
                        hT = sh.enter_context(SBT("hT_a%d" % l, [128, KC, L], BF16))
                        hTB = Buf("hT")
                        stage_norm(xcur, xcurB, ln_mix_g[l:l + 1, :], hT, hTB, "A%d" % l)
                        S.barrier()
                        if on("inproj"):
                            stage_inproj(l, hT, hTB)
                            S.barrier()
                if on("ret"):
                    stage_ret(l)
                    S.barrier()
                if on("s5"):
                    stage_s5(l)
                    S.barrier()
                if on("merge"):
                    stage_merge_out(l, xcur, xcurB, xa_d, B["xa"])
                    S.barrier()
                if on("ffnup"):
                    with ExitStack() as sh:
                        hT = sh.enter_context(SBT("hT_b%d" % l, [128, KC, L], BF16))
                        hTB = Buf("hT2")
                        stage_norm(xa_d, B["xa"], ln_ffn_g[l:l + 1, :], hT, hTB, "G%d" % l)
                        S.barrier()
                        stage_ffn_up(l, hT, hTB)
                        S.barrier()
                if on("ffndown"):
                    stage_ffn_down(l, xa_d, B["xa"], xb_d, B["xb"])
                    S.barrier()
                xcur, xcurB = xb_d, B["xb"]
            if on("final"):
                stage_final(xcur, xcurB)
        run()
        S.finish()
    return nc


_NC = None


def _prep(inputs):
    f = lambda a: np.ascontiguousarray(np.asarray(a, dtype=np.float32))
    shared = {k: f(v) for k, v in inputs.items() if k != "x"}
    shared["ret_log_gamma"] = shared["ret_log_gamma"].reshape(2, 8)
    shared["ln_final_g"] = shared["ln_final_g"].reshape(1, D)
    shared.update(_consts())
    x = f(inputs["x"])
    return [dict(shared, x=x[b]) for b in range(8)]


def kernel(**inputs):
    global _NC
    if _NC is None:
        _NC = build_nc()
    in_maps = _prep(inputs)
    res = run_bass_kernel_spmd(_NC, in_maps, core_ids=list(range(8)))
    return np.stack([np.asarray(r["out"], dtype=np.float32) for r in res.results], axis=0)
```

```python
import math, os
from contextlib import ExitStack
import numpy as np
import concourse.bass as bass
import concourse.mybir as mybir
from concourse.bass_utils import run_bass_kernel_spmd

F32 = mybir.dt.float32
BF16 = mybir.dt.bfloat16
I32 = mybir.dt.int32
AF = mybir.ActivationFunctionType
ALU = mybir.AluOpType

L = 2048
D = 2048
KC = 16
DFF = 5632
FC = 44
INW = 12288
EPS = 1e-6
NLAYER = 2
TWO_PI = 2.0 * math.pi
C1 = 6.28125
C2 = TWO_PI - C1
MOFF = 1920
MW = 3968


class Buf:
    def __init__(self, name, excl=False):
        self.name = name
        self.excl = excl
        self.w = {}
        self.r = {}
        self.dsem = None
        self.dkind = None


class Sched:
    def __init__(self, nc, stack):
        self.nc = nc
        self.stack = stack
        self.eng = {'pe': nc.tensor, 'act': nc.scalar, 'dve': nc.vector, 'pool': nc.gpsimd, 'sp': nc.sync}
        self.sems = {}
        self.cnt = {}
        self.isdma = {}
        self.seen = {e: {} for e in self.eng}
        self.self_sync = {'pe': False, 'act': True, 'dve': True, 'pool': True}
        self.nsem = 0
        self.dpools = {'hw': [], 'sw': []}
        self.dnexts = {'hw': 0, 'sw': 0}
        for e in ('pe', 'act', 'dve', 'pool'):
            self._newsem('E_' + e, False)

    def _newsem(self, key, isdma):
        s = self.stack.enter_context(self.nc.semaphore('s_' + key))
        self.sems[key] = s
        self.cnt[key] = 0
        self.isdma[key] = isdma
        self.nsem += 1
        return key

    def _deps(self, reads, writes):
        deps = {}
        for b in reads:
            for k, v in b.w.items():
                if deps.get(k, 0) < v:
                    deps[k] = v
        for b in writes:
            for dd in (b.w, b.r):
                for k, v in dd.items():
                    if deps.get(k, 0) < v:
                        deps[k] = v
        return deps

    def _wait(self, e, deps):
        for k, v in deps.items():
            if self.isdma[k]:
                v = self.cnt[k]
            if self.seen[e].get(k, 0) < v:
                self.eng[e].wait_ge(self.sems[k], v)
                self.seen[e][k] = v

    def op(self, e, fn, reads=(), writes=()):
        ex = [b for b in reads if b.excl]
        if ex:
            reads = [b for b in reads if not b.excl]
            writes = list(writes) + ex
        self._wait(e, self._deps(reads, writes))
        ins = fn(self.eng[e])
        k = 'E_' + e
        self.cnt[k] += 1
        ins.then_inc(self.sems[k], 1)
        v = self.cnt[k]
        if not self.self_sync[e]:
            self.seen[e][k] = v
        for b in reads:
            b.r[k] = v
        for b in writes:
            b.w[k] = v
        return ins

    def dma(self, q, out, in_, sb, reads=(), writes=(), **kw):
        self._wait(q, self._deps(reads, writes))
        kind = 'sw' if q == 'pool' else 'hw'
        if sb.dsem is None:
            pool_, cap = self.dpools[kind], (8 if kind == 'sw' else 36)
            if len(pool_) < cap:
                pool_.append(self._newsem('D%s%d' % (kind, len(pool_)), True))
                sb.dsem = pool_[-1]
            else:
                sb.dsem = pool_[self.dnexts[kind] % cap]
            self.dnexts[kind] += 1
            sb.dkind = kind
        assert sb.dkind == kind, (sb.name, sb.dkind, kind)
        k = sb.dsem
        ins = self.eng[q].dma_start(out=out, in_=in_, **kw)
        ins.then_inc(self.sems[k], 16)
        self.cnt[k] += 16
        v = self.cnt[k]
        for b in reads:
            b.r[k] = v
        for b in writes:
            b.w[k] = v
        return ins

    def barrier(self):
        deps = {k: self.cnt[k] for k in self.cnt if self.cnt[k] > 0}
        for e in self.eng:
            self._wait(e, dict(deps))

    def finish(self):
        self.barrier()
        done = self._newsem('DONE', False)
        for e in self.eng:
            self.eng[e].sem_inc(self.sems[done], 1)
        self.eng['sp'].wait_ge(self.sems[done], len(self.eng))
        for k, sm in self.sems.items():
            if k != done:
                self.eng['sp'].sem_clear(sm)
        self.eng['sp'].sem_clear(self.sems[done])


def _consts():
    ident = np.eye(128, dtype=np.float32)
    half = 128
    inv = (1.0 / (10000.0 ** (np.arange(half, dtype=np.float32) / np.float32(half)))).astype(np.float32)
    ang = (np.arange(L, dtype=np.float32)[None, :] * inv[:, None]).astype(np.float32)
    cos = np.cos(ang.astype(np.float64)).astype(np.float32)
    sin = np.sin(ang.astype(np.float64)).astype(np.float32)
    j = np.arange(MW, dtype=np.float32)[None, :]
    p = np.arange(128, dtype=np.float32)[:, None]
    dtab = (j - MOFF - p).astype(np.float32)
    g = np.arange(128)
    par = np.stack([(g % 2 == 0), (g % 2 == 1)], axis=1).astype(np.float32)
    sel = (g[:, None] // 2 == np.arange(64)[None, :]).astype(np.float32)
    bm2 = (g[:, None] // 32 == g[None, :] // 32).astype(np.float32)
    return {"c_ident": ident, "c_cos": cos, "c_sin": sin, "c_dtab": dtab, "c_par": par, "c_sel": sel, "c_bm2": bm2}


def build_nc(dbg=None):
    nc = bass.Bass("TRN2", target_bir_lowering=False)
    dbg = dbg or {}
    stop_after = dbg.get("stop_after")
    nlayer = dbg.get("nlayer", NLAYER)

    def din(name, shape, dt=F32):
        return nc.dram_tensor(name, list(shape), dt, kind="ExternalInput").ap()

    x_in = din("x", [L, D])
    ln_mix_g = din("ln_mix_g", [2, D])
    w_in = din("w_in", [2, D, INW])
    lg_in = din("ret_log_gamma", [2, 8])
    a_re = din("ssm_a_re", [2, 2, 128, 64])
    a_im = din("ssm_a_im", [2, 2, 128, 64])
    log_dt = din("ssm_log_dt", [2, 2, 128])
    b_re = din("ssm_b_re", [2, 2, 128, 64, 16])
    b_im = din("ssm_b_im", [2, 2, 128, 64, 16])
    c_re = din("ssm_c_re", [2, 2, 128, 16, 64])
    c_im = din("ssm_c_im", [2, 2, 128, 16, 64])
    ssm_d = din("ssm_d", [2, D])
    w_glu = din("w_glu", [2, D, D])
    b_glu = din("b_glu", [2, D])
    w_out = din("w_out", [2, D, D])
    ln_ffn_g = din("ln_ffn_g", [2, D])
    w_fg = din("w_ffn_gate", [2, D, DFF])
    w_fu = din("w_ffn_up", [2, D, DFF])
    w_fd = din("w_ffn_down", [2, DFF, D])
    ln_final_g = din("ln_final_g", [1, D])
    c_ident = din("c_ident", [128, 128])
    c_cos = din("c_cos", [128, L])
    c_sin = din("c_sin", [128, L])
    c_dtab = din("c_dtab", [128, MW])
    c_par = din("c_par", [128, 2])
    c_sel = din("c_sel", [128, 64])
    c_bm2 = din("c_bm2", [128, 128])
    out_d = nc.dram_tensor("out", [L, D], F32, kind="ExternalOutput").ap()

    skind = "ExternalOutput" if dbg.get("dump") else "Internal"

    def dscr(name, shape, dt):
        k = "ExternalInput" if name in dbg.get("preload", ()) else skind
        return nc.dram_tensor(name, list(shape), dt, kind=k).ap()

    qT_d = dscr("s_qT", [1024, L], BF16)
    kT_d = dscr("s_kT", [1024, L], BF16)
    v_d = dscr("s_v", [L, D], BF16)
    sgT_d = dscr("s_sgT", [D, L], BF16)
    uT_d = dscr("s_uT", [D, L], BF16)
    srT_d = dscr("s_srT", [D, L], BF16)
    ssT_d = dscr("s_ssT", [D, L], BF16)
    mrT_d = dscr("s_mrT", [D, L], BF16)
    ysT_d = dscr("s_ysT", [D, L], BF16)
    aT_d = dscr("s_aT", [DFF, L], BF16)
    xa_d = dscr("s_xa", [L, D], F32)
    xb_d = dscr("s_xb", [L, D], F32)
    B = {n: Buf(n) for n in ["x", "qT", "kT", "v", "sgT", "uT", "srT", "ssT", "mrT", "ysT", "aT", "xa", "xb", "out", "params"]}

    _uid = [0]

    def SBT(name, shape, dt):
        _uid[0] += 1
        return nc.sbuf_tensor("%s_%d" % (name, _uid[0]), shape, dt)

    with ExitStack() as st:
        S = Sched(nc, st)

        def sb(name, shape, dt):
            t = st.enter_context(SBT(name, list(shape), dt))
            return t, Buf(name)

        ps = []
        for i in range(8):
            t = st.enter_context(nc.psum_tensor("ps%d" % i, [128, 512], F32))
            ps.append((t, Buf("ps%d" % i, excl=True)))
        ident, identB = sb("ident", [128, 128], F32)
        identb, identbB = sb("identb", [128, 128], BF16)
        onesb, onesbB = sb("onesb", [128, 128], BF16)
        S.dma('sp', ident[:], c_ident[:, :], identB, writes=[identB])
        S.op('dve', lambda e: e.tensor_copy(identb[:], ident[:]), [identB], [identbB])
        S.op('dve', lambda e: e.memset(onesb[:], 1.0), [], [onesbB])

        V = lambda fn, r, w: S.op('dve', fn, r, w)
        A = lambda fn, r, w: S.op('act', fn, r, w)
        P = lambda fn, r, w: S.op('pe', fn, r, w)
        G = lambda fn, r, w: S.op('pool', fn, r, w)

        def wload(dst, dstB, wsrc, c0, ncol, k0=0, nk=KC, dk0=0):
            src = wsrc[k0 * 128:(k0 + nk) * 128, c0:c0 + ncol].rearrange("(kc p) n -> p kc n", p=128)
            S.dma('pool', dst[:, dk0:dk0 + nk, 0:ncol], src, dstB, reads=[B["params"]], writes=[dstB])

        def stage_norm(xsrc, xB, gvec, hT, hTB, stg):
            with ExitStack() as s2:
                def sb2(name, shape, dt):
                    t = s2.enter_context(SBT(stg + name, list(shape), dt))
                    return t, Buf(stg + name)
                gbc, gbcB = sb2("gbc", [128, D], F32)
                S.dma('sp', gbc[:], gvec.partition_broadcast(128), gbcB, reads=[B["params"]], writes=[gbcB])
                xt = [sb2("xt%d" % i, [128, D], F32) for i in range(2)]
                junk, junkB = sb2("junk", [128, D], BF16)
                hb = [sb2("hb%d" % i, [128, D], BF16) for i in range(2)]
                st_ = [sb2("st%d" % i, [128, 4], F32) for i in range(2)]
                for tt in range(16):
                    x_t, x_B = xt[tt % 2]
                    h_t, h_B = hb[tt % 2]
                    s_t, s_B = st_[tt % 2]
                    S.dma('sp', x_t[:], xsrc[tt * 128:(tt + 1) * 128, :], x_B, reads=[xB], writes=[x_B])
                    A(lambda e: e.activation(out=junk[:], in_=x_t[:], func=AF.Square, accum_out=s_t[:, 0:1]), [x_B], [junkB, s_B])
                    A(lambda e: e.copy(out=s_t[:, 1:2], in_=s_t[:, 0:1]), [s_B], [s_B])
                    V(lambda e: e.tensor_scalar(out=s_t[:, 2:3], in0=s_t[:, 1:2], scalar1=1.0 / D, scalar2=EPS, op0=ALU.mult, op1=ALU.add), [s_B], [s_B])
                    A(lambda e: e.activation(out=s_t[:, 3:4], in_=s_t[:, 2:3], func=AF.Sqrt), [s_B], [s_B])
                    V(lambda e: e.reciprocal(out=s_t[:, 0:1], in_=s_t[:, 3:4]), [s_B], [s_B])
                    V(lambda e: e.scalar_tensor_tensor(out=h_t[:], in0=x_t[:], scalar=s_t[:, 0:1], in1=gbc[:], op0=ALU.mult, op1=ALU.mult), [x_B, s_B, gbcB], [h_B])
                    for q4 in range(4):
                        pt, pB = ps[(tt * 4 + q4) % 4]
                        ptb = pt[:].bitcast(BF16)
                        def tr(e, q4=q4, ptb=ptb):
                            ins = None
                            for i in range(4):
                                kc = q4 * 4 + i
                                ins = e.transpose(ptb[:, i * 128:(i + 1) * 128], h_t[:, kc * 128:(kc + 1) * 128], identb[:])
                            return ins
                        P(tr, [h_B, identbB], [pB])
                        dst = hT[:, q4 * 4:(q4 + 1) * 4, tt * 128:(tt + 1) * 128]
                        srcv = ptb[:, 0:512].rearrange("p (a b) -> p a b", a=4)
                        if q4 % 2 == 0:
                            V(lambda e, dst=dst, srcv=srcv: e.tensor_copy(dst, srcv), [pB], [hTB])
                        else:
                            A(lambda e, dst=dst, srcv=srcv: e.copy(out=dst, in_=srcv), [pB], [hTB])

        def stage_inproj(l, hT, hTB):
            with ExitStack() as s2:
                def sb2(name, shape, dt):
                    t = s2.enter_context(SBT("B" + name, list(shape), dt))
                    return t, Buf("B" + name)
                wb = [sb2("w%d" % i, [128, KC, 512], BF16) for i in range(2)]
                stg = [sb2("stg%d" % i, [128, 4, L], BF16) for i in range(2)]
                cosT, cosB = sb2("cos", [128, L], F32)
                sinT, sinB = sb2("sin", [128, L], F32)
                tmp = [sb2("tmp%d" % i, [128, 512], F32) for i in range(4)]
                S.dma('sp', cosT[:], c_cos[:, :], cosB, writes=[cosB])
                S.dma('sp', sinT[:], c_sin[:, :], sinB, writes=[sinB])
                psi = [0]

                def nextps():
                    r = ps[psi[0] % 6]
                    psi[0] += 1
                    return r
                for cg in range(24):
                    w_t, w_B = wb[cg % 2]
                    g_t, g_B = stg[cg % 2]
                    wload(w_t, w_B, w_in[l], cg * 512, 512)

                    def mm_feat(pt, nc_, tg, w_t=w_t):
                        def f(e):
                            ins = None
                            for kc in range(KC):
                                ins = e.matmul(pt[:], w_t[:, kc, nc_ * 128:(nc_ + 1) * 128], hT[:, kc, tg * 512:(tg + 1) * 512], start=(kc == 0), stop=(kc == KC - 1))
                            return ins
                        return f
                    if cg < 4:
                        isk = cg >= 2
                        sc = (1.0 / 16.0) if isk else 1.0
                        for tg in range(4):
                            tsl = slice(tg * 512, (tg + 1) * 512)
                            for hh in range(2):
                                p1, p1B = nextps()
                                p2, p2B = nextps()
                                P(mm_feat(p1, 2 * hh, tg), [w_B, hTB], [p1B])
                                P(mm_feat(p2, 2 * hh + 1, tg), [w_B, hTB], [p2B])
                                (t1, t1B), (t2, t2B), (t3, t3B), (t4, t4B) = tmp
                                V(lambda e: e.scalar_tensor_tensor(out=t1[:], in0=p1[:], scalar=sc, in1=cosT[:, tsl], op0=ALU.mult, op1=ALU.mult), [p1B, cosB], [t1B])
                                V(lambda e: e.scalar_tensor_tensor(out=t2[:], in0=p2[:], scalar=sc, in1=sinT[:, tsl], op0=ALU.mult, op1=ALU.mult), [p2B, sinB], [t2B])
                                V(lambda e: e.scalar_tensor_tensor(out=t3[:], in0=p1[:], scalar=sc, in1=sinT[:, tsl], op0=ALU.mult, op1=ALU.mult), [p1B, sinB], [t3B])
                                V(lambda e: e.scalar_tensor_tensor(out=t4[:], in0=p2[:], scalar=sc, in1=cosT[:, tsl], op0=ALU.mult, op1=ALU.mult), [p2B, cosB], [t4B])
                                G(lambda e: e.tensor_tensor(out=g_t[:, 2 * hh, tsl], in0=t1[:], in1=t2[:], op=ALU.subtract), [t1B, t2B], [g_B])
                                G(lambda e: e.tensor_tensor(out=g_t[:, 2 * hh + 1, tsl], in0=t3[:], in1=t4[:], op=ALU.add), [t3B, t4B], [g_B])
                        dd, dB = (kT_d, B["kT"]) if isk else (qT_d, B["qT"])
                        r0 = (cg % 2) * 512
                        S.dma('sp', dd[r0:r0 + 512, :].rearrange("(a p) t -> p a t", p=128), g_t[:], g_B, reads=[g_B], writes=[dB])
                    elif cg < 8:
                        for tt in range(16):
                            pt, pB = nextps()
                            def f(e, pt=pt, tt=tt, w_t=w_t):
                                ins = None
                                for kc in range(KC):
                                    ins = e.matmul(pt[:], hT[:, kc, tt * 128:(tt + 1) * 128], w_t[:, kc, :], start=(kc == 0), stop=(kc == KC - 1))
                                return ins
                            P(f, [w_B, hTB], [pB])
                            dst = g_t[:, tt // 4, (tt % 4) * 512:(tt % 4 + 1) * 512]
                            if tt % 2 == 0:
                                V(lambda e, dst=dst, pt=pt: e.tensor_copy(dst, pt[:]), [pB], [g_B])
                            else:
                                A(lambda e, dst=dst, pt=pt: e.copy(out=dst, in_=pt[:]), [pB], [g_B])
                        c0 = (cg - 4) * 512
                        S.dma('sp', v_d[:, c0:c0 + 512].rearrange("(a b p) n -> p a b n", p=128, b=4),
                              g_t[:].rearrange("p a (b n) -> p a b n", b=4), g_B, reads=[g_B], writes=[B["v"]])
                    else:
                        grp = (cg - 8) // 4
                        func = [AF.Silu, AF.Copy, AF.Sigmoid, AF.Sigmoid][grp]
                        dd, dB = [(sgT_d, B["sgT"]), (uT_d, B["uT"]), (srT_d, B["srT"]), (ssT_d, B["ssT"])][grp]
                        for nc_ in range(4):
                            for tg in range(4):
                                pt, pB = nextps()
                                P(mm_feat(pt, nc_, tg), [w_B, hTB], [pB])
                                dst = g_t[:, nc_, tg * 512:(tg + 1) * 512]
                                if func == AF.Copy and (tg % 2 == 0):
                                    V(lambda e, dst=dst, pt=pt: e.tensor_copy(dst, pt[:]), [pB], [g_B])
                                else:
                                    A(lambda e, dst=dst, pt=pt, func=func: e.activation(out=dst, in_=pt[:], func=func), [pB], [g_B])
                        r0 = ((cg - 8) % 4) * 512
                        S.dma('sp', dd[r0:r0 + 512, :].rearrange("(a p) t -> p a t", p=128), g_t[:], g_B, reads=[g_B], writes=[dB])

        def stage_ret(l):
            with ExitStack() as s2:
                def sb2(name, shape, dt):
                    t = s2.enter_context(SBT("C" + name, list(shape), dt))
                    return t, Buf("C" + name)
                dtab, dtabB = sb2("dtab", [128, MW], F32)
                S.dma('sp', dtab[:], c_dtab[:, :], dtabB, writes=[dtabB])
                lgt, lgB = sb2("lg", [128, 16], F32)
                S.dma('sp', lgt[:, 0:8], lg_in[l:l + 1, :].partition_broadcast(128), lgB, reads=[B["params"]], writes=[lgB])
                V(lambda e: e.tensor_scalar(out=lgt[:, 8:16], in0=lgt[:, 0:8], scalar1=-1.0, scalar2=None, op0=ALU.mult), [lgB], [lgB])
                m1, m1B = sb2("m1", [128, MW], F32)
                m2, m2B = sb2("m2", [128, MW], F32)
                tm, tmB = sb2("tm", [128, MW], F32)
                qh2 = [sb2("qh%d" % i, [128, 2, L], BF16) for i in range(2)]
                kh2 = [sb2("kh%d" % i, [128, 2, L], BF16) for i in range(2)]
                vh2 = [sb2("vh%d" % i, [128, 16, 512], BF16) for i in range(2)]
                sg, sgB = sb2("sg", [128, 4, L], BF16)
                sr, srB = sb2("sr", [128, 4, L], BF16)
                mr, mrB = sb2("mr", [128, 4, L], BF16)
                pT = [sb2("pT%d" % i, [128, 512], BF16) for i in range(3)]
                sq, sqB = sb2("sq", [128, 4, 512], BF16)
                yc, ycB = sb2("yc", [128, 4, 512], F32)
                rs, rsB = sb2("rs", [128, 512], F32)
                rs2, rs2B = sb2("rs2", [128, 512], F32)
                epsc, epscB = sb2("epsc", [128, 1], F32)
                V(lambda e: e.memset(epsc[:], EPS), [], [epscB])
                def load_qkv(hd):
                    qh, qhB = qh2[hd % 2]
                    kh, khB = kh2[hd % 2]
                    vh, vhB = vh2[hd % 2]
                    S.dma('sp', qh[:], qT_d[hd * 256:(hd + 1) * 256, :].rearrange("(a p) t -> p a t", p=128), qhB, reads=[B["qT"]], writes=[qhB])
                    S.dma('sp', kh[:], kT_d[hd * 256:(hd + 1) * 256, :].rearrange("(a p) t -> p a t", p=128), khB, reads=[B["kT"]], writes=[khB])
                    S.dma('sp', vh[:], v_d[:, hd * 512:(hd + 1) * 512].rearrange("(a p) e -> p a e", p=128), vhB, reads=[B["v"]], writes=[vhB])

                for hd in range(4):
                    qh, qhB = qh2[hd % 2]
                    kh, khB = kh2[hd % 2]
                    vh, vhB = vh2[hd % 2]
                    V(lambda e: e.tensor_scalar(out=m1[:], in0=dtab[:], scalar1=lgt[:, hd:hd + 1], scalar2=None, op0=ALU.mult), [dtabB, lgB], [m1B])
                    V(lambda e: e.tensor_scalar(out=m2[:], in0=dtab[:], scalar1=lgt[:, 12 + hd:13 + hd], scalar2=None, op0=ALU.mult), [dtabB, lgB], [m2B])
                    V(lambda e: e.tensor_tensor(out=m1[:], in0=m1[:], in1=m2[:], op=ALU.min), [m1B, m2B], [m1B])
                    A(lambda e: e.activation(out=tm[:], in_=m1[:], func=AF.Exp), [m1B], [tmB])
                    if hd == 0:
                        load_qkv(0)
                    if hd + 1 < 4:
                        load_qkv(hd + 1)
                    S.dma('sp', sg[:], sgT_d[hd * 512:(hd + 1) * 512, :].rearrange("(a p) t -> p a t", p=128), sgB, reads=[B["sgT"]], writes=[sgB])
                    S.dma('sp', sr[:], srT_d[hd * 512:(hd + 1) * 512, :].rearrange("(a p) t -> p a t", p=128), srB, reads=[B["srT"]], writes=[srB])
                    V(lambda e: e.tensor_tensor(out=sg[:], in0=sg[:], in1=sr[:], op=ALU.mult), [sgB, srB], [sgB])
                    SBANK = [4, 5, 7]

                    def issue_S(it):
                        tg_, sc_ = it // 16, it % 16
                        pS, pSB = ps[SBANK[it % 3]]
                        tsl_ = slice(tg_ * 512, (tg_ + 1) * 512)
                        def fs(e):
                            ins = None
                            for dc in range(2):
                                ins = e.matmul(pS[:], kh[:, dc, sc_ * 128:(sc_ + 1) * 128], qh[:, dc, tsl_], start=(dc == 0), stop=(dc == 1))
                            return ins
                        P(fs, [khB, qhB], [pSB])

                    issue_S(0)
                    issue_S(1)
                    for tg in range(4):
                        tsl = slice(tg * 512, (tg + 1) * 512)
                        for sc in range(16):
                            it = tg * 16 + sc
                            pS, pSB = ps[SBANK[it % 3]]
                            p_t, p_B = pT[it % 3]
                            off = 512 * tg - 128 * sc + MOFF
                            V(lambda e, p_t=p_t, pS=pS, off=off: e.tensor_tensor(out=p_t[:], in0=pS[:], in1=tm[:, off:off + 512], op=ALU.mult), [pSB, tmB], [p_B])
                            def fy(e, p_t=p_t, sc=sc):
                                ins = None
                                for ec in range(4):
                                    ins = e.matmul(ps[ec][0][:], vh[:, sc, ec * 128:(ec + 1) * 128], p_t[:], start=(sc == 0), stop=(sc == 15))
                                return ins
                            P(fy, [vhB, p_B], [ps[0][1], ps[1][1], ps[2][1], ps[3][1]])
                            if it + 2 < 64:
                                issue_S(it + 2)
                        for ec in range(4):
                            A(lambda e, ec=ec: e.activation(out=sq[:, ec, :], in_=ps[ec][0][:], func=AF.Square), [ps[ec][1]], [sqB])
                            A(lambda e, ec=ec: e.copy(out=yc[:, ec, :], in_=ps[ec][0][:]), [ps[ec][1]], [ycB])
                        pq, pqB = ps[6]
                        def fq(e):
                            ins = None
                            for ec in range(4):
                                ins = e.matmul(pq[:], onesb[:], sq[:, ec, :], start=(ec == 0), stop=(ec == 3))
                            return ins
                        P(fq, [onesbB, sqB], [pqB])
                        A(lambda e: e.activation(out=rs2[:], in_=pq[:], func=AF.Ln, bias=epsc[:, 0:1], scale=1.0 / 512.0), [pqB, epscB], [rs2B])
                        A(lambda e: e.activation(out=rs[:], in_=rs2[:], func=AF.Exp, scale=-0.5), [rs2B], [rsB])
                        for ec in range(4):
                            G(lambda e, ec=ec: e.tensor_tensor(out=yc[:, ec, :], in0=yc[:, ec, :], in1=rs[:], op=ALU.mult), [ycB, rsB], [ycB])
                            G(lambda e, ec=ec: e.tensor_tensor(out=mr[:, ec, tsl], in0=yc[:, ec, :], in1=sg[:, ec, tsl], op=ALU.mult), [ycB, sgB], [mrB])
                    S.dma('sp', mrT_d[hd * 512:(hd + 1) * 512, :].rearrange("(a p) t -> p a t", p=128), mr[:], mrB, reads=[mrB], writes=[B["mrT"]])

        def stage_s5(l):
            with ExitStack() as s2:
                def sb2(name, shape, dt):
                    t = s2.enter_context(SBT("D" + name, list(shape), dt))
                    return t, Buf("D" + name)
                NPW = 16
                pw_idx = {0: 0, 1: 1, 2: 2, 3: 3, 4: 4, 5: 5, 6: 6, 7: 7, 8: 8, 16: 9, 32: 10, 64: 11, 128: 12, 256: 13, 512: 14, 1024: 15}
                PWr = [sb2("pwr%d" % d, [128, NPW, 64], F32) for d in range(2)]
                PWi = [sb2("pwi%d" % d, [128, NPW, 64], F32) for d in range(2)]
                PWn = [sb2("pwn%d" % d, [128, NPW, 64], F32) for d in range(2)]
                BDn = [[sb2("BDn%d%d" % (d, ri), [128, 64, 32], F32) for ri in range(2)] for d in range(2)]
                CTn = [[sb2("CTn%d%d" % (d, ri), [128, 64, 32], F32) for ri in range(2)] for d in range(2)]
                dsk, dskB = sb2("dsk", [128, 16], F32)
                bm2, bm2B = sb2("bm2", [128, 128], F32)
                S.dma('sp', bm2[:], c_bm2[:, :], bm2B, writes=[bm2B])
                vec16, vec16B = sb2("vec16", [16, 128], F32)
                S.dma('sp', vec16[:], ssm_d[l].rearrange("(c p) -> c p", p=128), vec16B, reads=[B["params"]], writes=[vec16B])
                P(lambda e: e.transpose(ps[7][0][:, 0:16], vec16[:], ident[0:16, 0:16]), [vec16B, identB], [ps[7][1]])
                V(lambda e: e.tensor_copy(dsk[:], ps[7][0][:, 0:16]), [ps[7][1]], [dskB])
                with ExitStack() as s3:
                    def sb3(name, shape, dt):
                        t = s3.enter_context(SBT("Dp" + name, list(shape), dt))
                        return t, Buf("Dp" + name)
                    T = {}
                    for nm in ["are", "aim", "ldt", "dtv", "xr", "th", "mag", "n", "r", "msk", "sn", "cs", "th2", "lre", "lim", "lm1", "nr", "ni", "den", "cr", "ci", "t0", "t1"]:
                        T[nm] = sb3(nm, [128, 64], F32)
                    ni32, ni32B = sb3("ni32", [128, 64], I32)
                    An, AnB = sb3("An", [128, 130], F32)
                    Am, AmB = sb3("Am", [128, 128], F32)
                    par, parB = sb3("par", [128, 2], F32)
                    sel, selB = sb3("sel", [128, 64], F32)
                    onesf, onesfB = sb3("onesf", [128, 64], F32)
                    S.dma('sp', par[:], c_par[:, :], parB, writes=[parB])
                    S.dma('sp', sel[:], c_sel[:, :], selB, writes=[selB])
                    V(lambda e: e.memset(onesf[:], 1.0), [], [onesfB])
                    bre, breB = sb3("bre", [128, 64, 16], F32)
                    bim, bimB = sb3("bim", [128, 64, 16], F32)
                    t16a, t16aB = sb3("t16a", [128, 64, 16], F32)
                    t16b, t16bB = sb3("t16b", [128, 64, 16], F32)
                    CBr, CBrB = sb3("CBr", [32, 64, 128], F32)
                    CBi, CBiB = sb3("CBi", [32, 64, 128], F32)

                    def tt_(o, a, b, op):
                        V(lambda e: e.tensor_tensor(out=T[o][0][:], in0=T[a][0][:], in1=T[b][0][:], op=op), [T[a][1], T[b][1]], [T[o][1]])

                    def ts_(o, a, s1, s2v, op0, op1=None):
                        if op1 is None:
                            V(lambda e: e.tensor_scalar(out=T[o][0][:], in0=T[a][0][:], scalar1=s1, scalar2=None, op0=op0), [T[a][1]], [T[o][1]])
                        else:
                            V(lambda e: e.tensor_scalar(out=T[o][0][:], in0=T[a][0][:], scalar1=s1, scalar2=s2v, op0=op0, op1=op1), [T[a][1]], [T[o][1]])

                    def sine(o, a):
                        ts_("t0", a, 1.0 / TWO_PI, 0.5, ALU.mult, ALU.add)
                        V(lambda e: e.tensor_copy(ni32[:], T["t0"][0][:]), [T["t0"][1]], [ni32B])
                        V(lambda e: e.tensor_copy(T["n"][0][:], ni32[:]), [ni32B], [T["n"][1]])
                        V(lambda e: e.scalar_tensor_tensor(out=T["r"][0][:], in0=T["n"][0][:], scalar=-C1, in1=T[a][0][:], op0=ALU.mult, op1=ALU.add), [T["n"][1], T[a][1]], [T["r"][1]])
                        V(lambda e: e.scalar_tensor_tensor(out=T["r"][0][:], in0=T["n"][0][:], scalar=-C2, in1=T["r"][0][:], op0=ALU.mult, op1=ALU.add), [T["n"][1], T["r"][1]], [T["r"][1]])
                        ts_("msk", "r", math.pi, -TWO_PI, ALU.is_gt, ALU.mult)
                        tt_("r", "r", "msk", ALU.add)
                        ts_("msk", "r", -math.pi, TWO_PI, ALU.is_lt, ALU.mult)
                        tt_("r", "r", "msk", ALU.add)
                        ts_("r", "r", math.pi, -math.pi, ALU.min, ALU.max)
                        A(lambda e: e.activation(out=T[o][0][:], in_=T["r"][0][:], func=AF.Sin), [T["r"][1]], [T[o][1]])

                    for d in range(2):
                        for g2 in range(2):
                            S.dma('sp', bre[g2 * 64:(g2 + 1) * 64], b_re[l, d].rearrange("(r g) p h -> g p r h", g=2)[g2], breB, reads=[B["params"]], writes=[breB])
                            S.dma('sp', bim[g2 * 64:(g2 + 1) * 64], b_im[l, d].rearrange("(r g) p h -> g p r h", g=2)[g2], bimB, reads=[B["params"]], writes=[bimB])
                        S.dma('sp', An[:, 0:64], a_re[l, d], AnB, reads=[B["params"]], writes=[AnB])
                        S.dma('sp', An[:, 64:128], a_im[l, d], AnB, reads=[B["params"]], writes=[AnB])
                        S.dma('sp', An[:, 128:129], log_dt[l, d].rearrange("(g o) -> g o", o=1), AnB, reads=[B["params"]], writes=[AnB])
                        for si, nm in enumerate(["are", "aim", "ldt"]):
                            for g2 in range(2):
                                if si < 2:
                                    V(lambda e, g2=g2, si=si: e.tensor_scalar(out=Am[:, g2 * 64:(g2 + 1) * 64], in0=An[:, si * 64:(si + 1) * 64], scalar1=par[:, g2:g2 + 1], scalar2=None, op0=ALU.mult), [AnB, parB], [AmB])
                                else:
                                    V(lambda e, g2=g2: e.tensor_scalar(out=Am[:, g2 * 64:(g2 + 1) * 64], in0=onesf[:], scalar1=An[:, 128:129], scalar2=par[:, g2:g2 + 1], op0=ALU.mult, op1=ALU.mult), [AnB, parB, onesfB], [AmB])
                            P(lambda e: e.matmul(ps[6][0][:, 0:64], Am[:], sel[:], start=True, stop=True), [AmB, selB], [ps[6][1]])
                            V(lambda e, nm=nm: e.tensor_copy(T[nm][0][:], ps[6][0][:, 0:64]), [ps[6][1]], [T[nm][1]])
                        A(lambda e: e.activation(out=T["dtv"][0][:], in_=T["ldt"][0][:], func=AF.Exp), [T["ldt"][1]], [T["dtv"][1]])
                        tt_("xr", "are", "dtv", ALU.mult)
                        tt_("th", "aim", "dtv", ALU.mult)
                        A(lambda e: e.activation(out=T["mag"][0][:], in_=T["xr"][0][:], func=AF.Exp), [T["xr"][1]], [T["mag"][1]])
                        sine("sn", "th")
                        ts_("th2", "th", math.pi / 2.0, None, ALU.add)
                        sine("cs", "th2")
                        tt_("lre", "mag", "cs", ALU.mult)
                        tt_("lim", "mag", "sn", ALU.mult)
                        ts_("lm1", "lre", -1.0, None, ALU.add)
                        tt_("t0", "lm1", "are", ALU.mult)
                        tt_("t1", "lim", "aim", ALU.mult)
                        tt_("nr", "t0", "t1", ALU.add)
                        tt_("t0", "lim", "are", ALU.mult)
                        tt_("t1", "lm1", "aim", ALU.mult)
                        tt_("ni", "t0", "t1", ALU.subtract)
                        tt_("t0", "are", "are", ALU.mult)
                        tt_("t1", "aim", "aim", ALU.mult)
                        tt_("den", "t0", "t1", ALU.add)
                        V(lambda e: e.reciprocal(out=T["t0"][0][:], in_=T["den"][0][:]), [T["den"][1]], [T["t0"][1]])
                        tt_("cr", "nr", "t0", ALU.mult)
                        tt_("ci", "ni", "t0", ALU.mult)
                        crb = T["cr"][0][:].unsqueeze(2).to_broadcast([128, 64, 16])
                        cib = T["ci"][0][:].unsqueeze(2).to_broadcast([128, 64, 16])
                        (BDr, BDrB), (BDi, BDiB) = BDn[d]
                        V(lambda e: e.memset(BDr[:], 0.0), [], [BDrB])
                        V(lambda e: e.memset(BDi[:], 0.0), [], [BDiB])
                        V(lambda e: e.tensor_tensor(out=t16a[:], in0=bre[:], in1=crb, op=ALU.mult), [breB, T["cr"][1]], [t16aB])
                        V(lambda e: e.tensor_tensor(out=t16b[:], in0=bim[:], in1=cib, op=ALU.mult), [bimB, T["ci"][1]], [t16bB])
                        for g2 in range(2):
                            V(lambda e, g2=g2: e.tensor_tensor(out=BDr[g2 * 64:(g2 + 1) * 64, :, g2 * 16:(g2 + 1) * 16], in0=t16a[g2 * 64:(g2 + 1) * 64], in1=t16b[g2 * 64:(g2 + 1) * 64], op=ALU.subtract), [t16aB, t16bB], [BDrB])
                        V(lambda e: e.tensor_tensor(out=t16a[:], in0=bim[:], in1=crb, op=ALU.mult), [bimB, T["cr"][1]], [t16aB])
                        V(lambda e: e.tensor_tensor(out=t16b[:], in0=bre[:], in1=cib, op=ALU.mult), [breB, T["ci"][1]], [t16bB])
                        for g2 in range(2):
                            V(lambda e, g2=g2: e.tensor_tensor(out=BDi[g2 * 64:(g2 + 1) * 64, :, g2 * 16:(g2 + 1) * 16], in0=t16a[g2 * 64:(g2 + 1) * 64], in1=t16b[g2 * 64:(g2 + 1) * 64], op=ALU.add), [t16aB, t16bB], [BDiB])
                        V(lambda e: e.memset(CBr[:], 0.0), [], [CBrB])
                        V(lambda e: e.memset(CBi[:], 0.0), [], [CBiB])
                        for g2 in range(2):
                            S.dma('sp', CBr[g2 * 16:(g2 + 1) * 16, :, g2 * 64:(g2 + 1) * 64], c_re[l, d].rearrange("(r g) h p -> g h r p", g=2)[g2], CBrB, reads=[B["params"]], writes=[CBrB])
                            S.dma('sp', CBi[g2 * 16:(g2 + 1) * 16, :, g2 * 64:(g2 + 1) * 64], c_im[l, d].rearrange("(r g) h p -> g h r p", g=2)[g2], CBiB, reads=[B["params"]], writes=[CBiB])
                        for ri, (cb, cbB) in enumerate([(CBr, CBrB), (CBi, CBiB)]):
                            ctn, ctnB = CTn[d][ri]
                            for q in range(4):
                                pt, pB = ps[4 + q % 2]
                                def ftr(e, pt=pt, cb=cb, q=q):
                                    ins = None
                                    for i in range(16):
                                        ins = e.transpose(pt[:, i * 32:(i + 1) * 32], cb[:, q * 16 + i, :], ident[0:32, 0:32])
                                    return ins
                                P(ftr, [cbB, identB], [pB])
                                V(lambda e, pt=pt, q=q, ctn=ctn: e.tensor_copy(ctn[:, q * 16:(q + 1) * 16, :], pt[:].rearrange("p (a b) -> p a b", b=32)), [pB], [ctnB])
                        pr_, prB = PWr[d]
                        pi_, piB = PWi[d]
                        pn_, pnB = PWn[d]
                        V(lambda e: e.memset(pr_[:, 0, :], 1.0), [], [prB])
                        V(lambda e: e.memset(pi_[:, 0, :], 0.0), [], [piB])
                        V(lambda e: e.tensor_copy(pr_[:, 1, :], T["lre"][0][:]), [T["lre"][1]], [prB])
                        V(lambda e: e.tensor_copy(pi_[:, 1, :], T["lim"][0][:]), [T["lim"][1]], [piB])

                        def cmul(o, a, b):
                            t0 = T["t0"][0]
                            t1 = T["t1"][0]
                            V(lambda e: e.tensor_tensor(out=t0[:], in0=pr_[:, a, :], in1=pr_[:, b, :], op=ALU.mult), [prB], [T["t0"][1]])
                            V(lambda e: e.tensor_tensor(out=t1[:], in0=pi_[:, a, :], in1=pi_[:, b, :], op=ALU.mult), [piB], [T["t1"][1]])
                            V(lambda e: e.tensor_tensor(out=pr_[:, o, :], in0=t0[:], in1=t1[:], op=ALU.subtract), [T["t0"][1], T["t1"][1]], [prB])
                            V(lambda e: e.tensor_tensor(out=t0[:], in0=pr_[:, a, :], in1=pi_[:, b, :], op=ALU.mult), [prB, piB], [T["t0"][1]])
                            V(lambda e: e.tensor_tensor(out=t1[:], in0=pi_[:, a, :], in1=pr_[:, b, :], op=ALU.mult), [prB, piB], [T["t1"][1]])
                            V(lambda e: e.tensor_tensor(out=pi_[:, o, :], in0=t0[:], in1=t1[:], op=ALU.add), [T["t0"][1], T["t1"][1]], [piB])
                        for k in range(2, 9):
                            cmul(k, k - 1, 1)
                        for k in range(9, 16):
                            cmul(k, k - 1, k - 1)
                        V(lambda e: e.tensor_scalar(out=pn_[:], in0=pi_[:], scalar1=-1.0, scalar2=None, op0=ALU.mult), [piB], [pnB])
                S.barrier()
                BDa = [sb2("BDa%d" % ri, [128, 2, 8, 128], BF16) for ri in range(2)]
                w1, w1B = sb2("w1", [128, 9, 128], F32)
                w2, w2B = sb2("w2", [128, 9, 128], F32)
                BTc = sb2("BTc", [128, 32, 128], BF16)
                CTc = [sb2("CTc%d" % i, [128, 2, 2, 9, 128], BF16) for i in range(2)]
                KTc = sb2("KTc", [128, 16, 128], BF16)
                uTs = [sb2("uT%d" % i, [128, L], BF16) for i in range(2)]
                PADW = 128
                VS = [[[sb2("VS%d%d%d" % (q, d, i), [128, 2, PADW + 256], F32) for i in range(2)] for d in range(2)] for q in range(2)]
                for q in range(2):
                    for d in range(2):
                        for i in range(2):
                            G(lambda e, q=q, d=d, i=i: e.memset(VS[q][d][i][0][:], 0.0), [], [VS[q][d][i][1]])
                XBt = [[sb2("XB%d%d" % (d, i), [128, 2, 256], BF16) for i in range(4)] for d in range(2)]
                ysf, ysfB = sb2("ysf", [128, L], F32)
                g1, g1B = sb2("g1", [128, 1024], F32)
                yst, ystB = sb2("yst", [128, L], BF16)
                identbf = identb
                GC = 2.0 * math.sqrt(2.0 / math.pi)
                Yb = [ps[i][1] for i in range(4)]
                bt_t, bt_B = BTc
                kt_t, kt_B = KTc

                def Yap(t):
                    return ps[t // 2][0][:, (t % 2) * 256:(t % 2 + 1) * 256]

                def prepA(ck):
                    pr0 = ck * 4
                    ct_t, ct_B = CTc[ck % 2]
                    for d in range(2):
                        pwB = [PWr[d][1], PWi[d][1], PWn[d][1]]
                        for isC in (False, True):
                            nk = 9 if isC else 8
                            src = CTn[d] if isC else BDn[d]
                            Prb = PWr[d][0][:, 0:nk, pr0:pr0 + 4].unsqueeze(3).to_broadcast([128, nk, 4, 32])
                            Pib = PWi[d][0][:, 0:nk, pr0:pr0 + 4].unsqueeze(3).to_broadcast([128, nk, 4, 32])
                            Pnb = PWn[d][0][:, 0:nk, pr0:pr0 + 4].unsqueeze(3).to_broadcast([128, nk, 4, 32])
                            sre = src[0][0][:, pr0:pr0 + 4, :].unsqueeze(1).to_broadcast([128, nk, 4, 32])
                            sim = src[1][0][:, pr0:pr0 + 4, :].unsqueeze(1).to_broadcast([128, nk, 4, 32])
                            sB = [src[0][1], src[1][1]]
                            a1 = w1[:, 0:nk, :].rearrange("p k (a b) -> p k a b", b=32)
                            a2 = w2[:, 0:nk, :].rearrange("p k (a b) -> p k a b", b=32)
                            if isC:
                                ore = ct_t[:, d, 0, :, :].rearrange("p k (a b) -> p k a b", b=32)
                                oim = ct_t[:, d, 1, :, :].rearrange("p k (a b) -> p k a b", b=32)
                                oB = [ct_B, ct_B]
                            else:
                                ore = BDa[0][0][:, d, :, :].rearrange("p k (a b) -> p k a b", b=32)
                                oim = BDa[1][0][:, d, :, :].rearrange("p k (a b) -> p k a b", b=32)
                                oB = [BDa[0][1], BDa[1][1]]
                            G(lambda e: e.tensor_tensor(out=a1, in0=sre, in1=Prb, op=ALU.mult), sB + pwB, [w1B])
                            G(lambda e: e.tensor_tensor(out=a2, in0=sim, in1=Pib, op=ALU.mult), sB + pwB, [w2B])
                            G(lambda e: e.tensor_tensor(out=ore, in0=a1, in1=a2, op=ALU.subtract), [w1B, w2B], [oB[0]])
                            if not isC:
                                G(lambda e: e.tensor_tensor(out=a1, in0=sre, in1=Pib, op=ALU.mult), sB + pwB, [w1B])
                                G(lambda e: e.tensor_tensor(out=a2, in0=sim, in1=Prb, op=ALU.mult), sB + pwB, [w2B])
                                G(lambda e: e.tensor_tensor(out=oim, in0=a1, in1=a2, op=ALU.add), [w1B, w2B], [oB[1]])
                            else:
                                G(lambda e: e.tensor_tensor(out=a1, in0=sre, in1=Pnb, op=ALU.mult), sB + pwB, [w1B])
                                G(lambda e: e.tensor_tensor(out=a2, in0=sim, in1=Prb, op=ALU.mult), sB + pwB, [w2B])
                                G(lambda e: e.tensor_tensor(out=oim, in0=a1, in1=a2, op=ALU.subtract), [w1B, w2B], [oB[1]])

                itp = [0]

                def prepB(ck):
                    ct_t, ct_B = CTc[ck % 2]
                    for d in range(2):
                        for ri in range(2):
                            pt, pB = ps[6 + itp[0] % 2]
                            itp[0] += 1
                            ptb = pt[:].bitcast(BF16)
                            def ftr(e, ptb=ptb, d=d, ri=ri):
                                ins = None
                                for k in range(8):
                                    ins = e.transpose(ptb[:, k * 128:(k + 1) * 128], BDa[ri][0][:, d, k, :], identbf[:])
                                return ins
                            P(ftr, [BDa[ri][1], identbB], [pB])
                            i0_ = (d * 2 + ri) * 8
                            A(lambda e, ptb=ptb, i0_=i0_: e.copy(out=bt_t[:, i0_:i0_ + 8, :], in_=ptb.rearrange("p (a b) -> p a b", b=128)), [pB], [bt_B])
                    for d in range(2):
                        for t4 in range(2):
                            pt, pB = ps[6 + itp[0] % 2]
                            itp[0] += 1
                            def fk(e, pt=pt, d=d, t4=t4):
                                ins = None
                                for i in range(4):
                                    tau = t4 * 4 + i
                                    e.matmul(pt[:, i * 128:(i + 1) * 128], BDa[0][0][:, d, 0, :], ct_t[:, d, 0, tau, :], start=True, stop=False)
                                    ins = e.matmul(pt[:, i * 128:(i + 1) * 128], BDa[1][0][:, d, 0, :], ct_t[:, d, 1, tau, :], start=False, stop=True)
                                return ins
                            P(fk, [BDa[0][1], BDa[1][1], ct_B], [pB])
                            V(lambda e, pt=pt, d=d, t4=t4: e.tensor_tensor(out=kt_t[:, d * 8 + t4 * 4:d * 8 + t4 * 4 + 4, :], in0=pt[:].rearrange("p (a b) -> p a b", b=128), in1=bm2[:].unsqueeze(1).to_broadcast([128, 4, 128]), op=ALU.mult), [pB, bm2B], [kt_B])

                prepA(0)
                for ck in range(16):
                    pr0 = ck * 4
                    u_t, u_B = uTs[ck % 2]
                    ct_t, ct_B = CTc[ck % 2]
                    S.dma('sp', u_t[:], uT_d[ck * 128:(ck + 1) * 128, :], u_B, reads=[B["uT"]], writes=[u_B])
                    u3 = u_t[:].rearrange("p (c j) -> p c j", j=8)
                    prepB(ck)
                    for t in range(8):
                        def fi(e, t=t):
                            ins = None
                            first = (t % 2 == 0)
                            for j in range(0, t + 1):
                                ins = e.matmul(Yap(t), kt_t[:, 0 * 8 + (t - j), :], u3[:, :, j], start=first, stop=True, skip_group_check=True)
                                first = False
                            for j in range(t, 8):
                                ins = e.matmul(Yap(t), kt_t[:, 1 * 8 + (j - t), :], u3[:, :, j], start=False, stop=True, skip_group_check=True)
                            return ins
                        P(fi, [kt_B, u_B], [Yb[t // 2]])

                    def SI(pr4):
                        rs_ = slice(pr4 * 32, (pr4 + 1) * 32)
                        for d in range(2):
                            vp, vpB = ps[4 + d]
                            def fv(e, vp=vp, d=d):
                                ins = None
                                for ri in range(2):
                                    for j in range(8):
                                        k = (7 - j) if d == 0 else j
                                        ins = e.matmul(vp[:, ri * 256:(ri + 1) * 256], bt_t[rs_, (d * 2 + ri) * 8 + k, :], u3[rs_, :, j], start=(j == 0), stop=(j == 7), tile_position=(pr4 * 32, 0))
                                return ins
                            P(fv, [bt_B, u_B], [vpB])
                            v0, v0B = VS[pr4 % 2][d][0]
                            off = PADW if d == 0 else 0
                            A(lambda e, vp=vp, v0=v0, off=off: e.copy(out=v0[:, :, off:off + 256], in_=vp[:].rearrange("p (a b) -> p a b", b=256)), [vpB], [v0B])

                    def HS_SO(pr4):
                        pr = pr0 + pr4
                        rs_ = slice(pr4 * 32, (pr4 + 1) * 32)
                        cur = {0: 0, 1: 0}
                        dd = 1
                        while dd < 256:
                            k = pw_idx[8 * dd]
                            ctx = []
                            for d in range(2):
                                s_t, s_B = VS[pr4 % 2][d][cur[d]]
                                t_t, t_B = VS[pr4 % 2][d][1 - cur[d]]
                                off = PADW if d == 0 else 0
                                so = off - dd if d == 0 else off + dd
                                ar = PWr[d][0][:, k, pr:pr + 1]
                                ai = PWi[d][0][:, k, pr:pr + 1]
                                an = PWn[d][0][:, k, pr:pr + 1]
                                pwB = [PWr[d][1], PWi[d][1], PWn[d][1]]
                                ctx.append((s_t, s_B, t_t, t_B, off, so, ar, ai, an, pwB))
                                cur[d] = 1 - cur[d]
                            for (s_t, s_B, t_t, t_B, off, so, ar, ai, an, pwB) in ctx:
                                V(lambda e: e.scalar_tensor_tensor(out=t_t[:, :, off:off + 256], in0=s_t[:, :, so:so + 256], scalar=ar, in1=s_t[:, :, off:off + 256], op0=ALU.mult, op1=ALU.add), [s_B] + pwB, [t_B])
                            for (s_t, s_B, t_t, t_B, off, so, ar, ai, an, pwB) in ctx:
                                V(lambda e: e.scalar_tensor_tensor(out=t_t[:, 0, off:off + 256], in0=s_t[:, 1, so:so + 256], scalar=an, in1=t_t[:, 0, off:off + 256], op0=ALU.mult, op1=ALU.add), [s_B, t_B] + pwB, [t_B])
                            for (s_t, s_B, t_t, t_B, off, so, ar, ai, an, pwB) in ctx:
                                V(lambda e: e.scalar_tensor_tensor(out=t_t[:, 1, off:off + 256], in0=s_t[:, 0, so:so + 256], scalar=ai, in1=t_t[:, 1, off:off + 256], op0=ALU.mult, op1=ALU.add), [s_B, t_B] + pwB, [t_B])
                            dd *= 2
                        for d in range(2):
                            f_t, f_B = VS[pr4 % 2][d][cur[d]]
                            off = PADW if d == 0 else 0
                            xb_t, xb_B = XBt[d][pr4]
                            A(lambda e, f_t=f_t, off=off, xb_t=xb_t: e.copy(out=xb_t[:], in_=f_t[:, :, off:off + 256]), [f_B], [xb_B])

                    def SO(prs):
                        def fo(e):
                            ins = None
                            for d in range(2):
                                for t in range(8):
                                    ex = (t + 1) if d == 0 else (8 - t)
                                    for ri in range(2):
                                        for pr4 in prs:
                                            rs_ = slice(pr4 * 32, (pr4 + 1) * 32)
                                            xb_t = XBt[d][pr4][0]
                                            if d == 0:
                                                o_ = ps[t // 2][0][rs_, (t % 2) * 256 + 1:(t % 2) * 256 + 256]
                                                r_ = xb_t[:, ri, 0:255]
                                            else:
                                                o_ = ps[t // 2][0][rs_, (t % 2) * 256:(t % 2) * 256 + 255]
                                                r_ = xb_t[:, ri, 1:256]
                                            ins = e.matmul(o_, ct_t[:, d, ri, ex, rs_], r_, start=False, stop=True, skip_group_check=True, tile_position=(0, pr4 * 32))
                            return ins
                        P(fo, [ct_B] + [XBt[d][p_][1] for d in range(2) for p_ in prs], Yb)

                    SI(0)
                    SI(1)
                    if ck + 1 < 16:
                        prepA(ck + 1)
                    HS_SO(0)
                    SI(2)
                    HS_SO(1)
                    SO([0, 1])
                    SI(3)
                    HS_SO(2)
                    HS_SO(3)
                    SO([2, 3])
                    ysf3 = ysf[:].rearrange("p (c j) -> p c j", j=8)
                    for t in range(8):
                        V(lambda e, t=t: e.scalar_tensor_tensor(out=ysf3[:, :, t], in0=u3[:, :, t], scalar=dsk[:, ck:ck + 1], in1=Yap(t), op0=ALU.mult, op1=ALU.add), [u_B, dskB, Yb[t // 2]], [ysfB])
                    for hf in range(2):
                        hs = slice(hf * 1024, (hf + 1) * 1024)
                        A(lambda e: e.activation(out=g1[:], in_=ysf[:, hs], func=AF.Square), [ysfB], [g1B])
                        G(lambda e: e.tensor_scalar(out=g1[:], in0=g1[:], scalar1=0.044715, scalar2=1.0, op0=ALU.mult, op1=ALU.add), [g1B], [g1B])
                        G(lambda e: e.tensor_tensor(out=g1[:], in0=g1[:], in1=ysf[:, hs], op=ALU.mult), [g1B, ysfB], [g1B])
                        A(lambda e: e.activation(out=g1[:], in_=g1[:], func=AF.Sigmoid, scale=GC), [g1B], [g1B])
                        G(lambda e: e.tensor_tensor(out=yst[:, hs], in0=g1[:], in1=ysf[:, hs], op=ALU.mult), [g1B, ysfB], [ystB])
                    S.dma('sp', ysT_d[ck * 128:(ck + 1) * 128, :], yst[:], ystB, reads=[ystB], writes=[B["ysT"]])

        def stage_merge_out(l, xsrc, xsB, xdst, xdB):
            with ExitStack() as s2:
                def sb2(name, shape, dt):
                    t = s2.enter_context(SBT("E" + name, list(shape), dt))
                    return t, Buf("E" + name)
                ysT, ysTB = sb2("ysT", [128, KC, L], BF16)
                mg, mgB = sb2("mg", [128, KC, L], BF16)
                wb = [sb2("w%d" % i, [128, KC, 512], BF16) for i in range(2)]
                mrt = [sb2("mr%d" % i, [128, L], BF16) for i in range(2)]
                sst = [sb2("ss%d" % i, [128, L], BF16) for i in range(2)]
                sgl = [sb2("sgl%d" % i, [128, 512], F32) for i in range(2)]
                bg, bgB = sb2("bg", [128, 16], F32)
                xr_ = [sb2("xr%d" % i, [128, 512], F32) for i in range(4)]
                xo_ = [sb2("xo%d" % i, [128, 512], F32) for i in range(2)]
                vec16, vec16B = sb2("vec16", [16, 128], F32)
                S.dma('sp', vec16[:], b_glu[l].rearrange("(c p) -> c p", p=128), vec16B, reads=[B["params"]], writes=[vec16B])
                P(lambda e: e.transpose(ps[7][0][:, 0:16], vec16[:], ident[0:16, 0:16]), [vec16B, identB], [ps[7][1]])
                V(lambda e: e.tensor_copy(bg[:], ps[7][0][:, 0:16]), [ps[7][1]], [bgB])
                for kc in range(KC):
                    S.dma('sp', ysT[:, kc, :], ysT_d[kc * 128:(kc + 1) * 128, :], ysTB, reads=[B["ysT"]], writes=[ysTB])
                it = 0
                for cg in range(4):
                    w_t, w_B = wb[cg % 2]
                    wload(w_t, w_B, w_glu[l], cg * 512, 512)
                    for nc_ in range(4):
                        ch = cg * 4 + nc_
                        m_t, m_B = mrt[ch % 2]
                        s_t, s_B = sst[ch % 2]
                        S.dma('sp', m_t[:], mrT_d[ch * 128:(ch + 1) * 128, :], m_B, reads=[B["mrT"]], writes=[m_B])
                        S.dma('sp', s_t[:], ssT_d[ch * 128:(ch + 1) * 128, :], s_B, reads=[B["ssT"]], writes=[s_B])
                        for tg in range(4):
                            tsl = slice(tg * 512, (tg + 1) * 512)
                            pt, pB = ps[it % 8]
                            g_t, g_B = sgl[it % 2]
                            it += 1
                            def f(e, pt=pt, nc_=nc_, tsl=tsl, w_t=w_t):
                                ins = None
                                for kc in range(KC):
                                    ins = e.matmul(pt[:], w_t[:, kc, nc_ * 128:(nc_ + 1) * 128], ysT[:, kc, tsl], start=(kc == 0), stop=(kc == KC - 1))
                                return ins
                            P(f, [w_B, ysTB], [pB])
                            A(lambda e, pt=pt, g_t=g_t, ch=ch: e.activation(out=g_t[:], in_=pt[:], func=AF.Sigmoid, bias=bg[:, ch:ch + 1]), [pB, bgB], [g_B])
                            V(lambda e, g_t=g_t, ch=ch, tsl=tsl: e.tensor_tensor(out=g_t[:], in0=g_t[:], in1=ysT[:, ch, tsl], op=ALU.mult), [g_B, ysTB], [g_B])
                            V(lambda e, g_t=g_t, s_t=s_t, tsl=tsl: e.tensor_tensor(out=g_t[:], in0=g_t[:], in1=s_t[:, tsl], op=ALU.mult), [g_B, s_B], [g_B])
                            V(lambda e, g_t=g_t, m_t=m_t, ch=ch, tsl=tsl: e.tensor_tensor(out=mg[:, ch, tsl], in0=g_t[:], in1=m_t[:, tsl], op=ALU.add), [g_B, m_B], [mgB])
                it = 0

                def ldx(i):
                    ng_, tt_ = i // 16, i % 16
                    t_, b_ = xr_[i % 4]
                    S.dma('sp', t_[:], xsrc[tt_ * 128:(tt_ + 1) * 128, ng_ * 512:(ng_ + 1) * 512], b_, reads=[xsB], writes=[b_])
                ldx(0)
                ldx(1)
                for ng in range(4):
                    w_t, w_B = wb[ng % 2]
                    wload(w_t, w_B, w_out[l], ng * 512, 512)
                    for tt in range(16):
                        pt, pB = ps[it % 8]
                        xr_t, xr_B = xr_[it % 4]
                        xo_t, xo_B = xo_[it % 2]
                        if it + 2 < 64:
                            ldx(it + 2)
                        it += 1
                        def f(e, pt=pt, tt=tt, w_t=w_t):
                            ins = None
                            for kc in range(KC):
                                ins = e.matmul(pt[:], mg[:, kc, tt * 128:(tt + 1) * 128], w_t[:, kc, :], start=(kc == 0), stop=(kc == KC - 1))
                            return ins
                        P(f, [w_B, mgB], [pB])
                        V(lambda e, pt=pt, xr_t=xr_t, xo_t=xo_t: e.tensor_tensor(out=xo_t[:], in0=pt[:], in1=xr_t[:], op=ALU.add), [pB, xr_B], [xo_B])
                        S.dma('sp', xdst[tt * 128:(tt + 1) * 128, ng * 512:(ng + 1) * 512], xo_t[:], xo_B, reads=[xo_B], writes=[xdB])

        def stage_ffn_up(l, hT, hTB):
            with ExitStack() as s2:
                def sb2(name, shape, dt):
                    t = s2.enter_context(SBT("H" + name, list(shape), dt))
                    return t, Buf("H" + name)
                wg = [sb2("wg%d" % i, [128, KC, 512], BF16) for i in range(2)]
                wu = [sb2("wu%d" % i, [128, KC, 512], BF16) for i in range(2)]
                stg = [sb2("stg%d" % i, [128, 4, L], BF16) for i in range(2)]
                sl = [sb2("sl%d" % i, [128, 512], F32) for i in range(2)]
                it = 0
                for fg in range(11):
                    g_t, g_B = wg[fg % 2]
                    u_t, u_B = wu[fg % 2]
                    s_t, s_B = stg[fg % 2]
                    wload(g_t, g_B, w_fg[l], fg * 512, 512)
                    wload(u_t, u_B, w_fu[l], fg * 512, 512)
                    for fc in range(4):
                        for tg in range(4):
                            tsl = slice(tg * 512, (tg + 1) * 512)
                            pg, pgB = ps[(it % 4) * 2]
                            pu, puB = ps[(it % 4) * 2 + 1]
                            l_t, l_B = sl[it % 2]
                            it += 1
                            def f(w_t, pt, fc=fc, tsl=tsl):
                                def ff(e):
                                    ins = None
                                    for kc in range(KC):
                                        ins = e.matmul(pt[:], w_t[:, kc, fc * 128:(fc + 1) * 128], hT[:, kc, tsl], start=(kc == 0), stop=(kc == KC - 1))
                                    return ins
                                return ff
                            P(f(g_t, pg), [g_B, hTB], [pgB])
                            P(f(u_t, pu), [u_B, hTB], [puB])
                            A(lambda e, pg=pg, l_t=l_t: e.activation(out=l_t[:], in_=pg[:], func=AF.Silu), [pgB], [l_B])
                            V(lambda e, pu=pu, l_t=l_t, s_t=s_t, fc=fc, tsl=tsl: e.tensor_tensor(out=s_t[:, fc, tsl], in0=pu[:], in1=l_t[:], op=ALU.mult), [puB, l_B], [s_B])
                    S.dma('sp', aT_d[fg * 512:(fg + 1) * 512, :].rearrange("(a p) t -> p a t", p=128), s_t[:], s_B, reads=[s_B], writes=[B["aT"]])

        def stage_ffn_down(l, xsrc, xsB, xdst, xdB):
            with ExitStack() as s2:
                def sb2(name, shape, dt):
                    t = s2.enter_context(SBT("I" + name, list(shape), dt))
                    return t, Buf("I" + name)
                Ah, AhB = sb2("A", [128, FC, 1024], BF16)
                wd = [sb2("wd%d" % i, [128, FC, 512], BF16) for i in range(2)]
                xr_ = [sb2("xr%d" % i, [128, 512], F32) for i in range(4)]
                xo_ = [sb2("xo%d" % i, [128, 512], F32) for i in range(2)]
                it = 0
                wi = 0

                def ldx(i):
                    half_, r_ = i // 32, i % 32
                    ng_, t8_ = r_ // 8, r_ % 8
                    tt_ = half_ * 8 + t8_
                    t_, b_ = xr_[i % 4]
                    S.dma('sp', t_[:], xsrc[tt_ * 128:(tt_ + 1) * 128, ng_ * 512:(ng_ + 1) * 512], b_, reads=[xsB], writes=[b_])
                ldx(0)
                ldx(1)
                for half in range(2):
                    for f4 in range(4):
                        S.dma('sp', Ah[:, f4 * 11:(f4 + 1) * 11, :], aT_d[f4 * 11 * 128:(f4 + 1) * 11 * 128, half * 1024:(half + 1) * 1024].rearrange("(a p) t -> p a t", p=128), AhB, reads=[B["aT"]], writes=[AhB])
                    for ng in range(4):
                        w_t, w_B = wd[wi % 2]
                        wi += 1
                        for f4 in range(4):
                            wload(w_t, w_B, w_fd[l], ng * 512, 512, k0=f4 * 11, nk=11, dk0=f4 * 11)
                        for t8 in range(8):
                            tt = half * 8 + t8
                            pt, pB = ps[it % 8]
                            xr_t, xr_B = xr_[it % 4]
                            xo_t, xo_B = xo_[it % 2]
                            if it + 2 < 64:
                                ldx(it + 2)
                            it += 1
                            def f(e, pt=pt, t8=t8, w_t=w_t):
                                ins = None
                                for fc in range(FC):
                                    ins = e.matmul(pt[:], Ah[:, fc, t8 * 128:(t8 + 1) * 128], w_t[:, fc, :], start=(fc == 0), stop=(fc == FC - 1))
                                return ins
                            P(f, [w_B, AhB], [pB])
                            V(lambda e, pt=pt, xr_t=xr_t, xo_t=xo_t: e.tensor_tensor(out=xo_t[:], in0=pt[:], in1=xr_t[:], op=ALU.add), [pB, xr_B], [xo_B])
                            S.dma('sp', xdst[tt * 128:(tt + 1) * 128, ng * 512:(ng + 1) * 512], xo_t[:], xo_B, reads=[xo_B], writes=[xdB])

        def stage_final(xsrc, xsB):
            with ExitStack() as s2:
                def sb2(name, shape, dt):
                    t = s2.enter_context(SBT("Z" + name, list(shape), dt))
                    return t, Buf("Z" + name)
                gbc, gbcB = sb2("gbc", [128, D], F32)
                S.dma('sp', gbc[:], ln_final_g.partition_broadcast(128), gbcB, reads=[B["params"]], writes=[gbcB])
                xt = [sb2("xt%d" % i, [128, D], F32) for i in range(2)]
                ot = [sb2("ot%d" % i, [128, D], F32) for i in range(2)]
                junk, junkB = sb2("junk", [128, D], BF16)
                st_ = [sb2("st%d" % i, [128, 4], F32) for i in range(2)]
                for tt in range(16):
                    x_t, x_B = xt[tt % 2]
                    o_t, o_B = ot[tt % 2]
                    s_t, s_B = st_[tt % 2]
                    S.dma('sp', x_t[:], xsrc[tt * 128:(tt + 1) * 128, :], x_B, reads=[xsB], writes=[x_B])
                    A(lambda e: e.activation(out=junk[:], in_=x_t[:], func=AF.Square, accum_out=s_t[:, 0:1]), [x_B], [junkB, s_B])
                    A(lambda e: e.copy(out=s_t[:, 1:2], in_=s_t[:, 0:1]), [s_B], [s_B])
                    V(lambda e: e.tensor_scalar(out=s_t[:, 2:3], in0=s_t[:, 1:2], scalar1=1.0 / D, scalar2=EPS, op0=ALU.mult, op1=ALU.add), [s_B], [s_B])
                    A(lambda e: e.activation(out=s_t[:, 3:4], in_=s_t[:, 2:3], func=AF.Sqrt), [s_B], [s_B])
                    V(lambda e: e.reciprocal(out=s_t[:, 0:1], in_=s_t[:, 3:4]), [s_B], [s_B])
                    V(lambda e: e.scalar_tensor_tensor(out=o_t[:], in0=x_t[:], scalar=s_t[:, 0:1], in1=gbc[:], op0=ALU.mult, op1=ALU.mult), [x_B, s_B, gbcB], [o_B])
                    S.dma('sp', out_d[tt * 128:(tt + 1) * 128, :], o_t[:], o_B, reads=[o_B], writes=[B["out"]])

        def run():
            stages = dbg.get("stages")
            on = lambda n: (stages is None) or (n in stages)
            xcur, xcurB = x_in, B["x"]
            for l in range(nlayer):
                if on("norm1") or on("inproj"):
                    with ExitStack() as sh:
                        hT = sh.enter_context(SBT("hT_a%d" % l, [128, KC, L], BF16))
                        hTB = Buf("hT")
                        stage_norm(xcur, xcurB, ln_mix_g[l:l + 1, :], hT, hTB, "A%d" % l)
                        S.barrier()
                        if on("inproj"):
                            stage_inproj(l, hT, hTB)
                            S.barrier()
                if on("ret"):
                    stage_ret(l)
                    S.barrier()
                if on("s5"):
                    stage_s5(l)
                    S.barrier()
                if on("merge"):
                    stage_merge_out(l, xcur, xcurB, xa_d, B["xa"])
                    S.barrier()
                if on("ffnup"):
                    with ExitStack() as sh:
                        hT = sh.enter_context(SBT("hT_b%d" % l, [128, KC, L], BF16))
                        hTB = Buf("hT2")
                        stage_norm(xa_d, B["xa"], ln_ffn_g[l:l + 1, :], hT, hTB, "G%d" % l)
                        S.barrier()
                        stage_ffn_up(l, hT, hTB)
                        S.barrier()
                if on("ffndown"):
                    stage_ffn_down(l, xa_d, B["xa"], xb_d, B["xb"])
                    S.barrier()
                xcur, xcurB = xb_d, B["xb"]
            if on("final"):
                stage_final(xcur, xcurB)
        run()
        S.finish()
    return nc


_NC = None


def _prep(inputs):
    f = lambda a: np.ascontiguousarray(np.asarray(a, dtype=np.float32))
    shared = {k: f(v) for k, v in inputs.items() if k != "x"}
    shared["ret_log_gamma"] = shared["ret_log_gamma"].reshape(2, 8)
    shared["ln_final_g"] = shared["ln_final_g"].reshape(1, D)
    shared.update(_consts())
    x = f(inputs["x"])
    return [dict(shared, x=x[b]) for b in range(8)]


def kernel(**inputs):
    global _NC
    if _NC is None:
        _NC = build_nc()
    in_maps = _prep(inputs)
    res = run_bass_kernel_spmd(_NC, in_maps, core_ids=list(range(8)))
    return np.stack([np.asarray(r["out"], dtype=np.float32) for r in res.results], axis=0)
```

```python
import math, os
from contextlib import ExitStack
import numpy as np
import concourse.bass as bass
import concourse.mybir as mybir
from concourse.bass_utils import run_bass_kernel_spmd

F32 = mybir.dt.float32
BF16 = mybir.dt.bfloat16
I32 = mybir.dt.int32
AF = mybir.ActivationFunctionType
ALU = mybir.AluOpType

L = 2048
D = 2048
KC = 16
DFF = 5632
FC = 44
INW = 12288
EPS = 1e-6
NLAYER = 2
TWO_PI = 2.0 * math.pi
C1 = 6.28125
C2 = TWO_PI - C1
MOFF = 1920
MW = 3968


class Buf:
    def __init__(self, name, excl=False):
        self.name = name
        self.excl = excl
        self.w = {}
        self.r = {}
        self.dsem = None
        self.dkind = None


class Sched:
    def __init__(self, nc, stack):
        self.nc = nc
        self.stack = stack
        self.eng = {'pe': nc.tensor, 'act': nc.scalar, 'dve': nc.vector, 'pool': nc.gpsimd, 'sp': nc.sync}
        self.sems = {}
        self.cnt = {}
        self.isdma = {}
        self.seen = {e: {} for e in self.eng}
        self.self_sync = {'pe': False, 'act': True, 'dve': True, 'pool': True}
        self.nsem = 0
        self.dpools = {'hw': [], 'sw': []}
        self.dnexts = {'hw': 0, 'sw': 0}
        for e in ('pe', 'act', 'dve', 'pool'):
            self._newsem('E_' + e, False)

    def _newsem(self, key, isdma):
        s = self.stack.enter_context(self.nc.semaphore('s_' + key))
        self.sems[key] = s
        self.cnt[key] = 0
        self.isdma[key] = isdma
        self.nsem += 1
        return key

    def _deps(self, reads, writes):
        deps = {}
        for b in reads:
            for k, v in b.w.items():
                if deps.get(k, 0) < v:
                    deps[k] = v
        for b in writes:
            for dd in (b.w, b.r):
                for k, v in dd.items():
                    if deps.get(k, 0) < v:
                        deps[k] = v
        return deps

    def _wait(self, e, deps):
        for k, v in deps.items():
            if self.isdma[k]:
                v = self.cnt[k]
            if self.seen[e].get(k, 0) < v:
                self.eng[e].wait_ge(self.sems[k], v)
                self.seen[e][k] = v

    def op(self, e, fn, reads=(), writes=()):
        ex = [b for b in reads if b.excl]
        if ex:
            reads = [b for b in reads if not b.excl]
            writes = list(writes) + ex
        self._wait(e, self._deps(reads, writes))
        ins = fn(self.eng[e])
        k = 'E_' + e
        self.cnt[k] += 1
        ins.then_inc(self.sems[k], 1)
        v = self.cnt[k]
        if not self.self_sync[e]:
            self.seen[e][k] = v
        for b in reads:
            b.r[k] = v
        for b in writes:
            b.w[k] = v
        return ins

    def dma(self, q, out, in_, sb, reads=(), writes=(), **kw):
        self._wait(q, self._deps(reads, writes))
        kind = 'sw' if q == 'pool' else 'hw'
        if sb.dsem is None:
            pool_, cap = self.dpools[kind], (8 if kind == 'sw' else 36)
            if len(pool_) < cap:
                pool_.append(self._newsem('D%s%d' % (kind, len(pool_)), True))
                sb.dsem = pool_[-1]
            else:
                sb.dsem = pool_[self.dnexts[kind] % cap]
            self.dnexts[kind] += 1
            sb.dkind = kind
        assert sb.dkind == kind, (sb.name, sb.dkind, kind)
        k = sb.dsem
        ins = self.eng[q].dma_start(out=out, in_=in_, **kw)
        ins.then_inc(self.sems[k], 16)
        self.cnt[k] += 16
        v = self.cnt[k]
        for b in reads:
            b.r[k] = v
        for b in writes:
            b.w[k] = v
        return ins

    def barrier(self):
        deps = {k: self.cnt[k] for k in self.cnt if self.cnt[k] > 0}
        for e in self.eng:
            self._wait(e, dict(deps))

    def finish(self):
        self.barrier()
        done = self._newsem('DONE', False)
        for e in self.eng:
            self.eng[e].sem_inc(self.sems[done], 1)
        self.eng['sp'].wait_ge(self.sems[done], len(self.eng))
        for k, sm in self.sems.items():
            if k != done:
                self.eng['sp'].sem_clear(sm)
        self.eng['sp'].sem_clear(self.sems[done])


def _consts():
    ident = np.eye(128, dtype=np.float32)
    half = 128
    inv = (1.0 / (10000.0 ** (np.arange(half, dtype=np.float32) / np.float32(half)))).astype(np.float32)
    ang = (np.arange(L, dtype=np.float32)[None, :] * inv[:, None]).astype(np.float32)
    cos = np.cos(ang.astype(np.float64)).astype(np.float32)
    sin = np.sin(ang.astype(np.float64)).astype(np.float32)
    j = np.arange(MW, dtype=np.float32)[None, :]
    p = np.arange(128, dtype=np.float32)[:, None]
    dtab = (j - MOFF - p).astype(np.float32)
    g = np.arange(128)
    par = np.stack([(g % 2 == 0), (g % 2 == 1)], axis=1).astype(np.float32)
    sel = (g[:, None] // 2 == np.arange(64)[None, :]).astype(np.float32)
    bm2 = (g[:, None] // 32 == g[None, :] // 32).astype(np.float32)
    return {"c_ident": ident, "c_cos": cos, "c_sin": sin, "c_dtab": dtab, "c_par": par, "c_sel": sel, "c_bm2": bm2}


def build_nc(dbg=None):
    nc = bass.Bass("TRN2", target_bir_lowering=False)
    dbg = dbg or {}
    stop_after = dbg.get("stop_after")
    nlayer = dbg.get("nlayer", NLAYER)

    def din(name, shape, dt=F32):
        return nc.dram_tensor(name, list(shape), dt, kind="ExternalInput").ap()

    x_in = din("x", [L, D])
    ln_mix_g = din("ln_mix_g", [2, D])
    w_in = din("w_in", [2, D, INW])
    lg_in = din("ret_log_gamma", [2, 8])
    a_re = din("ssm_a_re", [2, 2, 128, 64])
    a_im = din("ssm_a_im", [2, 2, 128, 64])
    log_dt = din("ssm_log_dt", [2, 2, 128])
    b_re = din("ssm_b_re", [2, 2, 128, 64, 16])
    b_im = din("ssm_b_im", [2, 2, 128, 64, 16])
    c_re = din("ssm_c_re", [2, 2, 128, 16, 64])
    c_im = din("ssm_c_im", [2, 2, 128, 16, 64])
    ssm_d = din("ssm_d", [2, D])
    w_glu = din("w_glu", [2, D, D])
    b_glu = din("b_glu", [2, D])
    w_out = din("w_out", [2, D, D])
    ln_ffn_g = din("ln_ffn_g", [2, D])
    w_fg = din("w_ffn_gate", [2, D, DFF])
    w_fu = din("w_ffn_up", [2, D, DFF])
    w_fd = din("w_ffn_down", [2, DFF, D])
    ln_final_g = din("ln_final_g", [1, D])
    c_ident = din("c_ident", [128, 128])
    c_cos = din("c_cos", [128, L])
    c_sin = din("c_sin", [128, L])
    c_dtab = din("c_dtab", [128, MW])
    c_par = din("c_par", [128, 2])
    c_sel = din("c_sel", [128, 64])
    c_bm2 = din("c_bm2", [128, 128])
    out_d = nc.dram_tensor("out", [L, D], F32, kind="ExternalOutput").ap()

    skind = "ExternalOutput" if dbg.get("dump") else "Internal"

    def dscr(name, shape, dt):
        k = "ExternalInput" if name in dbg.get("preload", ()) else skind
        return nc.dram_tensor(name, list(shape), dt, kind=k).ap()

    qT_d = dscr("s_qT", [1024, L], BF16)
    kT_d = dscr("s_kT", [1024, L], BF16)
    v_d = dscr("s_v", [L, D], BF16)
    sgT_d = dscr("s_sgT", [D, L], BF16)
    uT_d = dscr("s_uT", [D, L], BF16)
    srT_d = dscr("s_srT", [D, L], BF16)
    ssT_d = dscr("s_ssT", [D, L], BF16)
    mrT_d = dscr("s_mrT", [D, L], BF16)
    ysT_d = dscr("s_ysT", [D, L], BF16)
    aT_d = dscr("s_aT", [DFF, L], BF16)
    xa_d = dscr("s_xa", [L, D], F32)
    xb_d = dscr("s_xb", [L, D], F32)
    B = {n: Buf(n) for n in ["x", "qT", "kT", "v", "sgT", "uT", "srT", "ssT", "mrT", "ysT", "aT", "xa", "xb", "out", "params"]}

    _uid = [0]

    def SBT(name, shape, dt):
        _uid[0] += 1
        return nc.sbuf_tensor("%s_%d" % (name, _uid[0]), shape, dt)

    with ExitStack() as st:
        S = Sched(nc, st)

        def sb(name, shape, dt):
            t = st.enter_context(SBT(name, list(shape), dt))
            return t, Buf(name)

        ps = []
        for i in range(8):
            t = st.enter_context(nc.psum_tensor("ps%d" % i, [128, 512], F32))
            ps.append((t, Buf("ps%d" % i, excl=True)))
        ident, identB = sb("ident", [128, 128], F32)
        identb, identbB = sb("identb", [128, 128], BF16)
        onesb, onesbB = sb("onesb", [128, 128], BF16)
        S.dma('sp', ident[:], c_ident[:, :], identB, writes=[identB])
        S.op('dve', lambda e: e.tensor_copy(identb[:], ident[:]), [identB], [identbB])
        S.op('dve', lambda e: e.memset(onesb[:], 1.0), [], [onesbB])

        V = lambda fn, r, w: S.op('dve', fn, r, w)
        A = lambda fn, r, w: S.op('act', fn, r, w)
        P = lambda fn, r, w: S.op('pe', fn, r, w)
        G = lambda fn, r, w: S.op('pool', fn, r, w)

        def wload(dst, dstB, wsrc, c0, ncol, k0=0, nk=KC, dk0=0):
            src = wsrc[k0 * 128:(k0 + nk) * 128, c0:c0 + ncol].rearrange("(kc p) n -> p kc n", p=128)
            S.dma('pool', dst[:, dk0:dk0 + nk, 0:ncol], src, dstB, reads=[B["params"]], writes=[dstB])

        def stage_norm(xsrc, xB, gvec, hT, hTB, stg):
            with ExitStack() as s2:
                def sb2(name, shape, dt):
                    t = s2.enter_context(SBT(stg + name, list(shape), dt))
                    return t, Buf(stg + name)
                gbc, gbcB = sb2("gbc", [128, D], F32)
                S.dma('sp', gbc[:], gvec.partition_broadcast(128), gbcB, reads=[B["params"]], writes=[gbcB])
                xt = [sb2("xt%d" % i, [128, D], F32) for i in range(2)]
                junk, junkB = sb2("junk", [128, D], BF16)
                hb = [sb2("hb%d" % i, [128, D], BF16) for i in range(2)]
                st_ = [sb2("st%d" % i, [128, 4], F32) for i in range(2)]
                for tt in range(16):
                    x_t, x_B = xt[tt % 2]
                    h_t, h_B = hb[tt % 2]
                    s_t, s_B = st_[tt % 2]
                    S.dma('sp', x_t[:], xsrc[tt * 128:(tt + 1) * 128, :], x_B, reads=[xB], writes=[x_B])
                    A(lambda e: e.activation(out=junk[:], in_=x_t[:], func=AF.Square, accum_out=s_t[:, 0:1]), [x_B], [junkB, s_B])
                    A(lambda e: e.copy(out=s_t[:, 1:2], in_=s_t[:, 0:1]), [s_B], [s_B])
                    V(lambda e: e.tensor_scalar(out=s_t[:, 2:3], in0=s_t[:, 1:2], scalar1=1.0 / D, scalar2=EPS, op0=ALU.mult, op1=ALU.add), [s_B], [s_B])
                    A(lambda e: e.activation(out=s_t[:, 3:4], in_=s_t[:, 2:3], func=AF.Sqrt), [s_B], [s_B])
                    V(lambda e: e.reciprocal(out=s_t[:, 0:1], in_=s_t[:, 3:4]), [s_B], [s_B])
                    V(lambda e: e.scalar_tensor_tensor(out=h_t[:], in0=x_t[:], scalar=s_t[:, 0:1], in1=gbc[:], op0=ALU.mult, op1=ALU.mult), [x_B, s_B, gbcB], [h_B])
                    for q4 in range(4):
                        pt, pB = ps[(tt * 4 + q4) % 4]
                        ptb = pt[:].bitcast(BF16)
                        def tr(e, q4=q4, ptb=ptb):
                            ins = None
                            for i in range(4):
                                kc = q4 * 4 + i
                                ins = e.transpose(ptb[:, i * 128:(i + 1) * 128], h_t[:, kc * 128:(kc + 1) * 128], identb[:])
                            return ins
                        P(tr, [h_B, identbB], [pB])
                        dst = hT[:, q4 * 4:(q4 + 1) * 4, tt * 128:(tt + 1) * 128]
                        srcv = ptb[:, 0:512].rearrange("p (a b) -> p a b", a=4)
                        if q4 % 2 == 0:
                            V(lambda e, dst=dst, srcv=srcv: e.tensor_copy(dst, srcv), [pB], [hTB])
                        else:
                            A(lambda e, dst=dst, srcv=srcv: e.copy(out=dst, in_=srcv), [pB], [hTB])

        def stage_inproj(l, hT, hTB):
            with ExitStack() as s2:
                def sb2(name, shape, dt):
                    t = s2.enter_context(SBT("B" + name, list(shape), dt))
                    return t, Buf("B" + name)
                wb = [sb2("w%d" % i, [128, KC, 512], BF16) for i in range(2)]
                stg = [sb2("stg%d" % i, [128, 4, L], BF16) for i in range(2)]
                cosT, cosB = sb2("cos", [128, L], F32)
                sinT, sinB = sb2("sin", [128, L], F32)
                tmp = [sb2("tmp%d" % i, [128, 512], F32) for i in range(4)]
                S.dma('sp', cosT[:], c_cos[:, :], cosB, writes=[cosB])
                S.dma('sp', sinT[:], c_sin[:, :], sinB, writes=[sinB])
                psi = [0]

                def nextps():
                    r = ps[psi[0] % 6]
                    psi[0] += 1
                    return r
                for cg in range(24):
                    w_t, w_B = wb[cg % 2]
                    g_t, g_B = stg[cg % 2]
                    wload(w_t, w_B, w_in[l], cg * 512, 512)

                    def mm_feat(pt, nc_, tg, w_t=w_t):
                        def f(e):
                            ins = None
                            for kc in range(KC):
                                ins = e.matmul(pt[:], w_t[:, kc, nc_ * 128:(nc_ + 1) * 128], hT[:, kc, tg * 512:(tg + 1) * 512], start=(kc == 0), stop=(kc == KC - 1))
                            return ins
                        return f
                    if cg < 4:
                        isk = cg >= 2
                        sc = (1.0 / 16.0) if isk else 1.0
                        for tg in range(4):
                            tsl = slice(tg * 512, (tg + 1) * 512)
                            for hh in range(2):
                                p1, p1B = nextps()
                                p2, p2B = nextps()
                                P(mm_feat(p1, 2 * hh, tg), [w_B, hTB], [p1B])
                                P(mm_feat(p2, 2 * hh + 1, tg), [w_B, hTB], [p2B])
                                (t1, t1B), (t2, t2B), (t3, t3B), (t4, t4B) = tmp
                                V(lambda e: e.scalar_tensor_tensor(out=t1[:], in0=p1[:], scalar=sc, in1=cosT[:, tsl], op0=ALU.mult, op1=ALU.mult), [p1B, cosB], [t1B])
                                V(lambda e: e.scalar_tensor_tensor(out=t2[:], in0=p2[:], scalar=sc, in1=sinT[:, tsl], op0=ALU.mult, op1=ALU.mult), [p2B, sinB], [t2B])
                                V(lambda e: e.scalar_tensor_tensor(out=t3[:], in0=p1[:], scalar=sc, in1=sinT[:, tsl], op0=ALU.mult, op1=ALU.mult), [p1B, sinB], [t3B])
                                V(lambda e: e.scalar_tensor_tensor(out=t4[:], in0=p2[:], scalar=sc, in1=cosT[:, tsl], op0=ALU.mult, op1=ALU.mult), [p2B, cosB], [t4B])
                                G(lambda e: e.tensor_tensor(out=g_t[:, 2 * hh, tsl], in0=t1[:], in1=t2[:], op=ALU.subtract), [t1B, t2B], [g_B])
                                G(lambda e: e.tensor_tensor(out=g_t[:, 2 * hh + 1, tsl], in0=t3[:], in1=t4[:], op=ALU.add), [t3B, t4B], [g_B])
                        dd, dB = (kT_d, B["kT"]) if isk else (qT_d, B["qT"])
                        r0 = (cg % 2) * 512
                        S.dma('sp', dd[r0:r0 + 512, :].rearrange("(a p) t -> p a t", p=128), g_t[:], g_B, reads=[g_B], writes=[dB])
                    elif cg < 8:
                        for tt in range(16):
                            pt, pB = nextps()
                            def f(e, pt=pt, tt=tt, w_t=w_t):
                                ins = None
                                for kc in range(KC):
                                    ins = e.matmul(pt[:], hT[:, kc, tt * 128:(tt + 1) * 128], w_t[:, kc, :], start=(kc == 0), stop=(kc == KC - 1))
                                return ins
                            P(f, [w_B, hTB], [pB])
                            dst = g_t[:, tt // 4, (tt % 4) * 512:(tt % 4 + 1) * 512]
                            if tt % 2 == 0:
                                V(lambda e, dst=dst, pt=pt: e.tensor_copy(dst, pt[:]), [pB], [g_B])
                            else:
                                A(lambda e, dst=dst, pt=pt: e.copy(out=dst, in_=pt[:]), [pB], [g_B])
                        c0 = (cg - 4) * 512
                        S.dma('sp', v_d[:, c0:c0 + 512].rearrange("(a b p) n -> p a b n", p=128, b=4),
                              g_t[:].rearrange("p a (b n) -> p a b n", b=4), g_B, reads=[g_B], writes=[B["v"]])
                    else:
                        grp = (cg - 8) // 4
                        func = [AF.Silu, AF.Copy, AF.Sigmoid, AF.Sigmoid][grp]
                        dd, dB = [(sgT_d, B["sgT"]), (uT_d, B["uT"]), (srT_d, B["srT"]), (ssT_d, B["ssT"])][grp]
                        for nc_ in range(4):
                            for tg in range(4):
                                pt, pB = nextps()
                                P(mm_feat(pt, nc_, tg), [w_B, hTB], [pB])
                                dst = g_t[:, nc_, tg * 512:(tg + 1) * 512]
                                if func == AF.Copy and (tg % 2 == 0):
                                    V(lambda e, dst=dst, pt=pt: e.tensor_copy(dst, pt[:]), [pB], [g_B])
                                else:
                                    A(lambda e, dst=dst, pt=pt, func=func: e.activation(out=dst, in_=pt[:], func=func), [pB], [g_B])
                        r0 = ((cg - 8) % 4) * 512
                        S.dma('sp', dd[r0:r0 + 512, :].rearrange("(a p) t -> p a t", p=128), g_t[:], g_B, reads=[g_B], writes=[dB])

        def stage_ret(l):
            with ExitStack() as s2:
                def sb2(name, shape, dt):
                    t = s2.enter_context(SBT("C" + name, list(shape), dt))
                    return t, Buf("C" + name)
                dtab, dtabB = sb2("dtab", [128, MW], F32)
                S.dma('sp', dtab[:], c_dtab[:, :], dtabB, writes=[dtabB])
                lgt, lgB = sb2("lg", [128, 16], F32)
                S.dma('sp', lgt[:, 0:8], lg_in[l:l + 1, :].partition_broadcast(128), lgB, reads=[B["params"]], writes=[lgB])
                V(lambda e: e.tensor_scalar(out=lgt[:, 8:16], in0=lgt[:, 0:8], scalar1=-1.0, scalar2=None, op0=ALU.mult), [lgB], [lgB])
                m1, m1B = sb2("m1", [128, MW], F32)
                m2, m2B = sb2("m2", [128, MW], F32)
                tm, tmB = sb2("tm", [128, MW], F32)
                qh2 = [sb2("qh%d" % i, [128, 2, L], BF16) for i in range(2)]
                kh2 = [sb2("kh%d" % i, [128, 2, L], BF16) for i in range(2)]
                vh2 = [sb2("vh%d" % i, [128, 16, 512], BF16) for i in range(2)]
                sg, sgB = sb2("sg", [128, 4, L], BF16)
                sr, srB = sb2("sr", [128, 4, L], BF16)
                mr, mrB = sb2("mr", [128, 4, L], BF16)
                pT = [sb2("pT%d" % i, [128, 512], BF16) for i in range(3)]
                sq, sqB = sb2("sq", [128, 4, 512], BF16)
                yc, ycB = sb2("yc", [128, 4, 512], F32)
                rs, rsB = sb2("rs", [128, 512], F32)
                rs2, rs2B = sb2("rs2", [128, 512], F32)
                epsc, epscB = sb2("epsc", [128, 1], F32)
                V(lambda e: e.memset(epsc[:], EPS), [], [epscB])
                def load_qkv(hd):
                    qh, qhB = qh2[hd % 2]
                    kh, khB = kh2[hd % 2]
                    vh, vhB = vh2[hd % 2]
                    S.dma('sp', qh[:], qT_d[hd * 256:(hd + 1) * 256, :].rearrange("(a p) t -> p a t", p=128), qhB, reads=[B["qT"]], writes=[qhB])
                    S.dma('sp', kh[:], kT_d[hd * 256:(hd + 1) * 256, :].rearrange("(a p) t -> p a t", p=128), khB, reads=[B["kT"]], writes=[khB])
                    S.dma('sp', vh[:], v_d[:, hd * 512:(hd + 1) * 512].rearrange("(a p) e -> p a e", p=128), vhB, reads=[B["v"]], writes=[vhB])

                for hd in range(4):
                    qh, qhB = qh2[hd % 2]
                    kh, khB = kh2[hd % 2]
                    vh, vhB = vh2[hd % 2]
                    V(lambda e: e.tensor_scalar(out=m1[:], in0=dtab[:], scalar1=lgt[:, hd:hd + 1], scalar2=None, op0=ALU.mult), [dtabB, lgB], [m1B])
                    V(lambda e: e.tensor_scalar(out=m2[:], in0=dtab[:], scalar1=lgt[:, 12 + hd:13 + hd], scalar2=None, op0=ALU.mult), [dtabB, lgB], [m2B])
                    V(lambda e: e.tensor_tensor(out=m1[:], in0=m1[:], in1=m2[:], op=ALU.min), [m1B, m2B], [m1B])
                    A(lambda e: e.activation(out=tm[:], in_=m1[:], func=AF.Exp), [m1B], [tmB])
                    if hd == 0:
                        load_qkv(0)
                    if hd + 1 < 4:
                        load_qkv(hd + 1)
                    S.dma('sp', sg[:], sgT_d[hd * 512:(hd + 1) * 512, :].rearrange("(a p) t -> p a t", p=128), sgB, reads=[B["sgT"]], writes=[sgB])
                    S.dma('sp', sr[:], srT_d[hd * 512:(hd + 1) * 512, :].rearrange("(a p) t -> p a t", p=128), srB, reads=[B["srT"]], writes=[srB])
                    V(lambda e: e.tensor_tensor(out=sg[:], in0=sg[:], in1=sr[:], op=ALU.mult), [sgB, srB], [sgB])
                    SBANK = [4, 5, 7]

                    def issue_S(it):
                        tg_, sc_ = it // 16, it % 16
                        pS, pSB = ps[SBANK[it % 3]]
                        tsl_ = slice(tg_ * 512, (tg_ + 1) * 512)
                        def fs(e):
                            ins = None
                            for dc in range(2):
                                ins = e.matmul(pS[:], kh[:, dc, sc_ * 128:(sc_ + 1) * 128], qh[:, dc, tsl_], start=(dc == 0), stop=(dc == 1))
                            return ins
                        P(fs, [khB, qhB], [pSB])

                    issue_S(0)
                    issue_S(1)
                    for tg in range(4):
                        tsl = slice(tg * 512, (tg + 1) * 512)
                        for sc in range(16):
                            it = tg * 16 + sc
                            pS, pSB = ps[SBANK[it % 3]]
                            p_t, p_B = pT[it % 3]
                            off = 512 * tg - 128 * sc + MOFF
                            V(lambda e, p_t=p_t, pS=pS, off=off: e.tensor_tensor(out=p_t[:], in0=pS[:], in1=tm[:, off:off + 512], op=ALU.mult), [pSB, tmB], [p_B])
                            def fy(e, p_t=p_t, sc=sc):
                                ins = None
                                for ec in range(4):
                                    ins = e.matmul(ps[ec][0][:], vh[:, sc, ec * 128:(ec + 1) * 128], p_t[:], start=(sc == 0), stop=(sc == 15))
                                return ins
                            P(fy, [vhB, p_B], [ps[0][1], ps[1][1], ps[2][1], ps[3][1]])
                            if it + 2 < 64:
                                issue_S(it + 2)
                        for ec in range(4):
                            A(lambda e, ec=ec: e.activation(out=sq[:, ec, :], in_=ps[ec][0][:], func=AF.Square), [ps[ec][1]], [sqB])
                            A(lambda e, ec=ec: e.copy(out=yc[:, ec, :], in_=ps[ec][0][:]), [ps[ec][1]], [ycB])
                        pq, pqB = ps[6]
                        def fq(e):
                            ins = None
                            for ec in range(4):
                                ins = e.matmul(pq[:], onesb[:], sq[:, ec, :], start=(ec == 0), stop=(ec == 3))
                            return ins
                        P(fq, [onesbB, sqB], [pqB])
                        A(lambda e: e.activation(out=rs2[:], in_=pq[:], func=AF.Ln, bias=epsc[:, 0:1], scale=1.0 / 512.0), [pqB, epscB], [rs2B])
                        A(lambda e: e.activation(out=rs[:], in_=rs2[:], func=AF.Exp, scale=-0.5), [rs2B], [rsB])
                        for ec in range(4):
                            G(lambda e, ec=ec: e.tensor_tensor(out=yc[:, ec, :], in0=yc[:, ec, :], in1=rs[:], op=ALU.mult), [ycB, rsB], [ycB])
                            G(lambda e, ec=ec: e.tensor_tensor(out=mr[:, ec, tsl], in0=yc[:, ec, :], in1=sg[:, ec, tsl], op=ALU.mult), [ycB, sgB], [mrB])
                    S.dma('sp', mrT_d[hd * 512:(hd + 1) * 512, :].rearrange("(a p) t -> p a t", p=128), mr[:], mrB, reads=[mrB], writes=[B["mrT"]])

        def stage_s5(l):
            with ExitStack() as s2:
                def sb2(name, shape, dt):
                    t = s2.enter_context(SBT("D" + name, list(shape), dt))
                    return t, Buf("D" + name)
                NPW = 16
                pw_idx = {0: 0, 1: 1, 2: 2, 3: 3, 4: 4, 5: 5, 6: 6, 7: 7, 8: 8, 16: 9, 32: 10, 64: 11, 128: 12, 256: 13, 512: 14, 1024: 15}
                PWr = [sb2("pwr%d" % d, [128, NPW, 64], F32) for d in range(2)]
                PWi = [sb2("pwi%d" % d, [128, NPW, 64], F32) for d in range(2)]
                PWn = [sb2("pwn%d" % d, [128, NPW, 64], F32) for d in range(2)]
                BDn = [[sb2("BDn%d%d" % (d, ri), [128, 64, 32], F32) for ri in range(2)] for d in range(2)]
                CTn = [[sb2("CTn%d%d" % (d, ri), [128, 64, 32], F32) for ri in range(2)] for d in range(2)]
                dsk, dskB = sb2("dsk", [128, 16], F32)
                bm2, bm2B = sb2("bm2", [128, 128], F32)
                S.dma('sp', bm2[:], c_bm2[:, :], bm2B, writes=[bm2B])
                vec16, vec16B = sb2("vec16", [16, 128], F32)
                S.dma('sp', vec16[:], ssm_d[l].rearrange("(c p) -> c p", p=128), vec16B, reads=[B["params"]], writes=[vec16B])
                P(lambda e: e.transpose(ps[7][0][:, 0:16], vec16[:], ident[0:16, 0:16]), [vec16B, identB], [ps[7][1]])
                V(lambda e: e.tensor_copy(dsk[:], ps[7][0][:, 0:16]), [ps[7][1]], [dskB])
                with ExitStack() as s3:
                    def sb3(name, shape, dt):
                        t = s3.enter_context(SBT("Dp" + name, list(shape), dt))
                        return t, Buf("Dp" + name)
                    T = {}
                    for nm in ["are", "aim", "ldt", "dtv", "xr", "th", "mag", "n", "r", "msk", "sn", "cs", "th2", "lre", "lim", "lm1", "nr", "ni", "den", "cr", "ci", "t0", "t1"]:
                        T[nm] = sb3(nm, [128, 64], F32)
                    ni32, ni32B = sb3("ni32", [128, 64], I32)
                    An, AnB = sb3("An", [128, 130], F32)
                    Am, AmB = sb3("Am", [128, 128], F32)
                    par, parB = sb3("par", [128, 2], F32)
                    sel, selB = sb3("sel", [128, 64], F32)
                    onesf, onesfB = sb3("onesf", [128, 64], F32)
                    S.dma('sp', par[:], c_par[:, :], parB, writes=[parB])
                    S.dma('sp', sel[:], c_sel[:, :], selB, writes=[selB])
                    V(lambda e: e.memset(onesf[:], 1.0), [], [onesfB])
                    bre, breB = sb3("bre", [128, 64, 16], F32)
                    bim, bimB = sb3("bim", [128, 64, 16], F32)
                    t16a, t16aB = sb3("t16a", [128, 64, 16], F32)
                    t16b, t16bB = sb3("t16b", [128, 64, 16], F32)
                    CBr, CBrB = sb3("CBr", [32, 64, 128], F32)
                    CBi, CBiB = sb3("CBi", [32, 64, 128], F32)

                    def tt_(o, a, b, op):
                        V(lambda e: e.tensor_tensor(out=T[o][0][:], in0=T[a][0][:], in1=T[b][0][:], op=op), [T[a][1], T[b][1]], [T[o][1]])

                    def ts_(o, a, s1, s2v, op0, op1=None):
                        if op1 is None:
                            V(lambda e: e.tensor_scalar(out=T[o][0][:], in0=T[a][0][:], scalar1=s1, scalar2=None, op0=op0), [T[a][1]], [T[o][1]])
                        else:
                            V(lambda e: e.tensor_scalar(out=T[o][0][:], in0=T[a][0][:], scalar1=s1, scalar2=s2v, op0=op0, op1=op1), [T[a][1]], [T[o][1]])

                    def sine(o, a):
                        ts_("t0", a, 1.0 / TWO_PI, 0.5, ALU.mult, ALU.add)
                        V(lambda e: e.tensor_copy(ni32[:], T["t0"][0][:]), [T["t0"][1]], [ni32B])
                        V(lambda e: e.tensor_copy(T["n"][0][:], ni32[:]), [ni32B], [T["n"][1]])
                        V(lambda e: e.scalar_tensor_tensor(out=T["r"][0][:], in0=T["n"][0][:], scalar=-C1, in1=T[a][0][:], op0=ALU.mult, op1=ALU.add), [T["n"][1], T[a][1]], [T["r"][1]])
                        V(lambda e: e.scalar_tensor_tensor(out=T["r"][0][:], in0=T["n"][0][:], scalar=-C2, in1=T["r"][0][:], op0=ALU.mult, op1=ALU.add), [T["n"][1], T["r"][1]], [T["r"][1]])
                        ts_("msk", "r", math.pi, -TWO_PI, ALU.is_gt, ALU.mult)
                        tt_("r", "r", "msk", ALU.add)
                        ts_("msk", "r", -math.pi, TWO_PI, ALU.is_lt, ALU.mult)
                        tt_("r", "r", "msk", ALU.add)
                        ts_("r", "r", math.pi, -math.pi, ALU.min, ALU.max)
                        A(lambda e: e.activation(out=T[o][0][:], in_=T["r"][0][:], func=AF.Sin), [T["r"][1]], [T[o][1]])

                    for d in range(2):
                        for g2 in range(2):
                            S.dma('sp', bre[g2 * 64:(g2 + 1) * 64], b_re[l, d].rearrange("(r g) p h -> g p r h", g=2)[g2], breB, reads=[B["params"]], writes=[breB])
                            S.dma('sp', bim[g2 * 64:(g2 + 1) * 64], b_im[l, d].rearrange("(r g) p h -> g p r h", g=2)[g2], bimB, reads=[B["params"]], writes=[bimB])
                        S.dma('sp', An[:, 0:64], a_re[l, d], AnB, reads=[B["params"]], writes=[AnB])
                        S.dma('sp', An[:, 64:128], a_im[l, d], AnB, reads=[B["params"]], writes=[AnB])
                        S.dma('sp', An[:, 128:129], log_dt[l, d].rearrange("(g o) -> g o", o=1), AnB, reads=[B["params"]], writes=[AnB])
                        for si, nm in enumerate(["are", "aim", "ldt"]):
                            for g2 in range(2):
                                if si < 2:
                                    V(lambda e, g2=g2, si=si: e.tensor_scalar(out=Am[:, g2 * 64:(g2 + 1) * 64], in0=An[:, si * 64:(si + 1) * 64], scalar1=par[:, g2:g2 + 1], scalar2=None, op0=ALU.mult), [AnB, parB], [AmB])
                                else:
                                    V(lambda e, g2=g2: e.tensor_scalar(out=Am[:, g2 * 64:(g2 + 1) * 64], in0=onesf[:], scalar1=An[:, 128:129], scalar2=par[:, g2:g2 + 1], op0=ALU.mult, op1=ALU.mult), [AnB, parB, onesfB], [AmB])
                            P(lambda e: e.matmul(ps[6][0][:, 0:64], Am[:], sel[:], start=True, stop=True), [AmB, selB], [ps[6][1]])
                            V(lambda e, nm=nm: e.tensor_copy(T[nm][0][:], ps[6][0][:, 0:64]), [ps[6][1]], [T[nm][1]])
                        A(lambda e: e.activation(out=T["dtv"][0][:], in_=T["ldt"][0][:], func=AF.Exp), [T["ldt"][1]], [T["dtv"][1]])
                        tt_("xr", "are", "dtv", ALU.mult)
                        tt_("th", "aim", "dtv", ALU.mult)
                        A(lambda e: e.activation(out=T["mag"][0][:], in_=T["xr"][0][:], func=AF.Exp), [T["xr"][1]], [T["mag"][1]])
                        sine("sn", "th")
                        ts_("th2", "th", math.pi / 2.0, None, ALU.add)
                        sine("cs", "th2")
                        tt_("lre", "mag", "cs", ALU.mult)
                        tt_("lim", "mag", "sn", ALU.mult)
                        ts_("lm1", "lre", -1.0, None, ALU.add)
                        tt_("t0", "lm1", "are", ALU.mult)
                        tt_("t1", "lim", "aim", ALU.mult)
                        tt_("nr", "t0", "t1", ALU.add)
                        tt_("t0", "lim", "are", ALU.mult)
                        tt_("t1", "lm1", "aim", ALU.mult)
                        tt_("ni", "t0", "t1", ALU.subtract)
                        tt_("t0", "are", "are", ALU.mult)
                        tt_("t1", "aim", "aim", ALU.mult)
                        tt_("den", "t0", "t1", ALU.add)
                        V(lambda e: e.reciprocal(out=T["t0"][0][:], in_=T["den"][0][:]), [T["den"][1]], [T["t0"][1]])
                        tt_("cr", "nr", "t0", ALU.mult)
                        tt_("ci", "ni", "t0", ALU.mult)
                        crb = T["cr"][0][:].unsqueeze(2).to_broadcast([128, 64, 16])
                        cib = T["ci"][0][:].unsqueeze(2).to_broadcast([128, 64, 16])
                        (BDr, BDrB), (BDi, BDiB) = BDn[d]
                        V(lambda e: e.memset(BDr[:], 0.0), [], [BDrB])
                        V(lambda e: e.memset(BDi[:], 0.0), [], [BDiB])
                        V(lambda e: e.tensor_tensor(out=t16a[:], in0=bre[:], in1=crb, op=ALU.mult), [breB, T["cr"][1]], [t16aB])
                        V(lambda e: e.tensor_tensor(out=t16b[:], in0=bim[:], in1=cib, op=ALU.mult), [bimB, T["ci"][1]], [t16bB])
                        for g2 in range(2):
                            V(lambda e, g2=g2: e.tensor_tensor(out=BDr[g2 * 64:(g2 + 1) * 64, :, g2 * 16:(g2 + 1) * 16], in0=t16a[g2 * 64:(g2 + 1) * 64], in1=t16b[g2 * 64:(g2 + 1) * 64], op=ALU.subtract), [t16aB, t16bB], [BDrB])
                        V(lambda e: e.tensor_tensor(out=t16a[:], in0=bim[:], in1=crb, op=ALU.mult), [bimB, T["cr"][1]], [t16aB])
                        V(lambda e: e.tensor_tensor(out=t16b[:], in0=bre[:], in1=cib, op=ALU.mult), [breB, T["ci"][1]], [t16bB])
                        for g2 in range(2):
                            V(lambda e, g2=g2: e.tensor_tensor(out=BDi[g2 * 64:(g2 + 1) * 64, :, g2 * 16:(g2 + 1) * 16], in0=t16a[g2 * 64:(g2 + 1) * 64], in1=t16b[g2 * 64:(g2 + 1) * 64], op=ALU.add), [t16aB, t16bB], [BDiB])
                        V(lambda e: e.memset(CBr[:], 0.0), [], [CBrB])
                        V(lambda e: e.memset(CBi[:], 0.0), [], [CBiB])
                        for g2 in range(2):
                            S.dma('sp', CBr[g2 * 16:(g2 + 1) * 16, :, g2 * 64:(g2 + 1) * 64], c_re[l, d].rearrange("(r g) h p -> g h r p", g=2)[g2], CBrB, reads=[B["params"]], writes=[CBrB])
                            S.dma('sp', CBi[g2 * 16:(g2 + 1) * 16, :, g2 * 64:(g2 + 1) * 64], c_im[l, d].rearrange("(r g) h p -> g h r p", g=2)[g2], CBiB, reads=[B["params"]], writes=[CBiB])
                        for ri, (cb, cbB) in enumerate([(CBr, CBrB), (CBi, CBiB)]):
                            ctn, ctnB = CTn[d][ri]
                            for q in range(4):
                                pt, pB = ps[4 + q % 2]
                                def ftr(e, pt=pt, cb=cb, q=q):
                                    ins = None
                                    for i in range(16):
                                        ins = e.transpose(pt[:, i * 32:(i + 1) * 32], cb[:, q * 16 + i, :], ident[0:32, 0:32])
                                    return ins
                                P(ftr, [cbB, identB], [pB])
                                V(lambda e, pt=pt, q=q, ctn=ctn: e.tensor_copy(ctn[:, q * 16:(q + 1) * 16, :], pt[:].rearrange("p (a b) -> p a b", b=32)), [pB], [ctnB])
                        pr_, prB = PWr[d]
                        pi_, piB = PWi[d]
                        pn_, pnB = PWn[d]
                        V(lambda e: e.memset(pr_[:, 0, :], 1.0), [], [prB])
                        V(lambda e: e.memset(pi_[:, 0, :], 0.0), [], [piB])
                        V(lambda e: e.tensor_copy(pr_[:, 1, :], T["lre"][0][:]), [T["lre"][1]], [prB])
                        V(lambda e: e.tensor_copy(pi_[:, 1, :], T["lim"][0][:]), [T["lim"][1]], [piB])

                        def cmul(o, a, b):
                            t0 = T["t0"][0]
                            t1 = T["t1"][0]
                            V(lambda e: e.tensor_tensor(out=t0[:], in0=pr_[:, a, :], in1=pr_[:, b, :], op=ALU.mult), [prB], [T["t0"][1]])
                            V(lambda e: e.tensor_tensor(out=t1[:], in0=pi_[:, a, :], in1=pi_[:, b, :], op=ALU.mult), [piB], [T["t1"][1]])
                            V(lambda e: e.tensor_tensor(out=pr_[:, o, :], in0=t0[:], in1=t1[:], op=ALU.subtract), [T["t0"][1], T["t1"][1]], [prB])
                            V(lambda e: e.tensor_tensor(out=t0[:], in0=pr_[:, a, :], in1=pi_[:, b, :], op=ALU.mult), [prB, piB], [T["t0"][1]])
                            V(lambda e: e.tensor_tensor(out=t1[:], in0=pi_[:, a, :], in1=pr_[:, b, :], op=ALU.mult), [prB, piB], [T["t1"][1]])
                            V(lambda e: e.tensor_tensor(out=pi_[:, o, :], in0=t0[:], in1=t1[:], op=ALU.add), [T["t0"][1], T["t1"][1]], [piB])
                        for k in range(2, 9):
                            cmul(k, k - 1, 1)
                        for k in range(9, 16):
                            cmul(k, k - 1, k - 1)
                        V(lambda e: e.tensor_scalar(out=pn_[:], in0=pi_[:], scalar1=-1.0, scalar2=None, op0=ALU.mult), [piB], [pnB])
                S.barrier()
                BDa = [sb2("BDa%d" % ri, [128, 2, 8, 128], BF16) for ri in range(2)]
                w1, w1B = sb2("w1", [128, 9, 128], F32)
                w2, w2B = sb2("w2", [128, 9, 128], F32)
                BTc = sb2("BTc", [128, 32, 128], BF16)
                CTc = [sb2("CTc%d" % i, [128, 2, 2, 9, 128], BF16) for i in range(2)]
                KTc = sb2("KTc", [128, 16, 128], BF16)
                uTs = [sb2("uT%d" % i, [128, L], BF16) for i in range(2)]
                PADW = 128
                VS = [[[sb2("VS%d%d%d" % (q, d, i), [128, 2, PADW + 256], F32) for i in range(2)] for d in range(2)] for q in range(2)]
                VSB = {}
                for q in range(2):
                    for d in range(2):
                        for i in range(2):
                            VSB[(q, d, i)] = (Buf("VSre%d%d%d" % (q, d, i)), Buf("VSim%d%d%d" % (q, d, i)))
                            G(lambda e, q=q, d=d, i=i: e.memset(VS[q][d][i][0][:], 0.0), [], list(VSB[(q, d, i)]))
                XBt = [[sb2("XB%d%d" % (d, i), [128, 2, 256], BF16) for i in range(4)] for d in range(2)]
                ysf, ysfB = sb2("ysf", [128, L], F32)
                g1, g1B = sb2("g1", [128, 1024], F32)
                yst, ystB = sb2("yst", [128, L], BF16)
                identbf = identb
                GC = 2.0 * math.sqrt(2.0 / math.pi)
                Yb = [ps[i][1] for i in range(4)]
                bt_t, bt_B = BTc
                kt_t, kt_B = KTc

                def Yap(t):
                    return ps[t // 2][0][:, (t % 2) * 256:(t % 2 + 1) * 256]

                def prepA(ck):
                    pr0 = ck * 4
                    ct_t, ct_B = CTc[ck % 2]
                    for d in range(2):
                        pwB = [PWr[d][1], PWi[d][1], PWn[d][1]]
                        for isC in (False, True):
                            nk = 9 if isC else 8
                            src = CTn[d] if isC else BDn[d]
                            Prb = PWr[d][0][:, 0:nk, pr0:pr0 + 4].unsqueeze(3).to_broadcast([128, nk, 4, 32])
                            Pib = PWi[d][0][:, 0:nk, pr0:pr0 + 4].unsqueeze(3).to_broadcast([128, nk, 4, 32])
                            Pnb = PWn[d][0][:, 0:nk, pr0:pr0 + 4].unsqueeze(3).to_broadcast([128, nk, 4, 32])
                            sre = src[0][0][:, pr0:pr0 + 4, :].unsqueeze(1).to_broadcast([128, nk, 4, 32])
                            sim = src[1][0][:, pr0:pr0 + 4, :].unsqueeze(1).to_broadcast([128, nk, 4, 32])
                            sB = [src[0][1], src[1][1]]
                            a1 = w1[:, 0:nk, :].rearrange("p k (a b) -> p k a b", b=32)
                            a2 = w2[:, 0:nk, :].rearrange("p k (a b) -> p k a b", b=32)
                            if isC:
                                ore = ct_t[:, d, 0, :, :].rearrange("p k (a b) -> p k a b", b=32)
                                oim = ct_t[:, d, 1, :, :].rearrange("p k (a b) -> p k a b", b=32)
                                oB = [ct_B, ct_B]
                            else:
                                ore = BDa[0][0][:, d, :, :].rearrange("p k (a b) -> p k a b", b=32)
                                oim = BDa[1][0][:, d, :, :].rearrange("p k (a b) -> p k a b", b=32)
                                oB = [BDa[0][1], BDa[1][1]]
                            G(lambda e: e.tensor_tensor(out=a1, in0=sre, in1=Prb, op=ALU.mult), sB + pwB, [w1B])
                            G(lambda e: e.tensor_tensor(out=a2, in0=sim, in1=Pib, op=ALU.mult), sB + pwB, [w2B])
                            G(lambda e: e.tensor_tensor(out=ore, in0=a1, in1=a2, op=ALU.subtract), [w1B, w2B], [oB[0]])
                            if not isC:
                                G(lambda e: e.tensor_tensor(out=a1, in0=sre, in1=Pib, op=ALU.mult), sB + pwB, [w1B])
                                G(lambda e: e.tensor_tensor(out=a2, in0=sim, in1=Prb, op=ALU.mult), sB + pwB, [w2B])
                                G(lambda e: e.tensor_tensor(out=oim, in0=a1, in1=a2, op=ALU.add), [w1B, w2B], [oB[1]])
                            else:
                                G(lambda e: e.tensor_tensor(out=a1, in0=sre, in1=Pnb, op=ALU.mult), sB + pwB, [w1B])
                                G(lambda e: e.tensor_tensor(out=a2, in0=sim, in1=Prb, op=ALU.mult), sB + pwB, [w2B])
                                G(lambda e: e.tensor_tensor(out=oim, in0=a1, in1=a2, op=ALU.subtract), [w1B, w2B], [oB[1]])

                itp = [0]

                def prepB(ck):
                    ct_t, ct_B = CTc[ck % 2]
                    for d in range(2):
                        for ri in range(2):
                            pt, pB = ps[6 + itp[0] % 2]
                            itp[0] += 1
                            ptb = pt[:].bitcast(BF16)
                            def ftr(e, ptb=ptb, d=d, ri=ri):
                                ins = None
                                for k in range(8):
                                    ins = e.transpose(ptb[:, k * 128:(k + 1) * 128], BDa[ri][0][:, d, k, :], identbf[:])
                                return ins
                            P(ftr, [BDa[ri][1], identbB], [pB])
                            i0_ = (d * 2 + ri) * 8
                            A(lambda e, ptb=ptb, i0_=i0_: e.copy(out=bt_t[:, i0_:i0_ + 8, :], in_=ptb.rearrange("p (a b) -> p a b", b=128)), [pB], [bt_B])
                    for d in range(2):
                        for t4 in range(2):
                            pt, pB = ps[6 + itp[0] % 2]
                            itp[0] += 1
                            def fk(e, pt=pt, d=d, t4=t4):
                                ins = None
                                for i in range(4):
                                    tau = t4 * 4 + i
                                    e.matmul(pt[:, i * 128:(i + 1) * 128], BDa[0][0][:, d, 0, :], ct_t[:, d, 0, tau, :], start=True, stop=False)
                                    ins = e.matmul(pt[:, i * 128:(i + 1) * 128], BDa[1][0][:, d, 0, :], ct_t[:, d, 1, tau, :], start=False, stop=True)
                                return ins
                            P(fk, [BDa[0][1], BDa[1][1], ct_B], [pB])
                            V(lambda e, pt=pt, d=d, t4=t4: e.tensor_tensor(out=kt_t[:, d * 8 + t4 * 4:d * 8 + t4 * 4 + 4, :], in0=pt[:].rearrange("p (a b) -> p a b", b=128), in1=bm2[:].unsqueeze(1).to_broadcast([128, 4, 128]), op=ALU.mult), [pB, bm2B], [kt_B])

                prepA(0)
                for ck in range(16):
                    pr0 = ck * 4
                    u_t, u_B = uTs[ck % 2]
                    ct_t, ct_B = CTc[ck % 2]
                    S.dma('sp', u_t[:], uT_d[ck * 128:(ck + 1) * 128, :], u_B, reads=[B["uT"]], writes=[u_B])
                    u3 = u_t[:].rearrange("p (c j) -> p c j", j=8)
                    prepB(ck)
                    for t in range(8):
                        def fi(e, t=t):
                            ins = None
                            first = (t % 2 == 0)
                            for j in range(0, t + 1):
                                ins = e.matmul(Yap(t), kt_t[:, 0 * 8 + (t - j), :], u3[:, :, j], start=first, stop=True, skip_group_check=True)
                                first = False
                            for j in range(t, 8):
                                ins = e.matmul(Yap(t), kt_t[:, 1 * 8 + (j - t), :], u3[:, :, j], start=False, stop=True, skip_group_check=True)
                            return ins
                        P(fi, [kt_B, u_B], [Yb[t // 2]])

                    def SI(pr4):
                        rs_ = slice(pr4 * 32, (pr4 + 1) * 32)
                        for d in range(2):
                            vp, vpB = ps[4 + d]
                            def fv(e, vp=vp, d=d):
                                ins = None
                                for ri in range(2):
                                    for j in range(8):
                                        k = (7 - j) if d == 0 else j
                                        ins = e.matmul(vp[:, ri * 256:(ri + 1) * 256], bt_t[rs_, (d * 2 + ri) * 8 + k, :], u3[rs_, :, j], start=(j == 0), stop=(j == 7), tile_position=(pr4 * 32, 0))
                                return ins
                            P(fv, [bt_B, u_B], [vpB])
                            v0, v0B = VS[pr4 % 2][d][0]
                            off = PADW if d == 0 else 0
                            A(lambda e, vp=vp, v0=v0, off=off: e.copy(out=v0[:, :, off:off + 256], in_=vp[:].rearrange("p (a b) -> p a b", b=256)), [vpB], list(VSB[(pr4 % 2, d, 0)]))

                    def HS_SO(pr4):
                        pr = pr0 + pr4
                        rs_ = slice(pr4 * 32, (pr4 + 1) * 32)
                        cur = {0: 0, 1: 0}
                        dd = 1
                        while dd < 256:
                            k = pw_idx[8 * dd]
                            ctx = []
                            for d in range(2):
                                s_t, s_B = VS[pr4 % 2][d][cur[d]]
                                t_t, t_B = VS[pr4 % 2][d][1 - cur[d]]
                                off = PADW if d == 0 else 0
                                so = off - dd if d == 0 else off + dd
                                ar = PWr[d][0][:, k, pr:pr + 1]
                                ai = PWi[d][0][:, k, pr:pr + 1]
                                an = PWn[d][0][:, k, pr:pr + 1]
                                pwB = [PWr[d][1], PWi[d][1], PWn[d][1]]
                                s_B = VSB[(pr4 % 2, d, cur[d])]
                                t_B = VSB[(pr4 % 2, d, 1 - cur[d])]
                                ctx.append((s_t, s_B, t_t, t_B, off, so, ar, ai, an, pwB))
                                cur[d] = 1 - cur[d]
                            for (s_t, s_B, t_t, t_B, off, so, ar, ai, an, pwB) in ctx:
                                V(lambda e: e.scalar_tensor_tensor(out=t_t[:, :, off:off + 256], in0=s_t[:, :, so:so + 256], scalar=ar, in1=s_t[:, :, off:off + 256], op0=ALU.mult, op1=ALU.add), [s_B[0], s_B[1]] + pwB, [t_B[0], t_B[1]])
                            for (s_t, s_B, t_t, t_B, off, so, ar, ai, an, pwB) in ctx:
                                V(lambda e: e.scalar_tensor_tensor(out=t_t[:, 0, off:off + 256], in0=s_t[:, 1, so:so + 256], scalar=an, in1=t_t[:, 0, off:off + 256], op0=ALU.mult, op1=ALU.add), [s_B[1], t_B[0]] + pwB, [t_B[0]])
                            for (s_t, s_B, t_t, t_B, off, so, ar, ai, an, pwB) in ctx:
                                V(lambda e: e.scalar_tensor_tensor(out=t_t[:, 1, off:off + 256], in0=s_t[:, 0, so:so + 256], scalar=ai, in1=t_t[:, 1, off:off + 256], op0=ALU.mult, op1=ALU.add), [s_B[0], t_B[1]] + pwB, [t_B[1]])
                            dd *= 2
                        for d in range(2):
                            f_t, f_B = VS[pr4 % 2][d][cur[d]]
                            off = PADW if d == 0 else 0
                            xb_t, xb_B = XBt[d][pr4]
                            A(lambda e, f_t=f_t, off=off, xb_t=xb_t: e.copy(out=xb_t[:], in_=f_t[:, :, off:off + 256]), list(VSB[(pr4 % 2, d, cur[d])]), [xb_B])

                    def SO(prs):
                        def fo(e):
                            ins = None
                            for d in range(2):
                                for t in range(8):
                                    ex = (t + 1) if d == 0 else (8 - t)
                                    for ri in range(2):
                                        for pr4 in prs:
                                            rs_ = slice(pr4 * 32, (pr4 + 1) * 32)
                                            xb_t = XBt[d][pr4][0]
                                            if d == 0:
                                                o_ = ps[t // 2][0][rs_, (t % 2) * 256 + 1:(t % 2) * 256 + 256]
                                                r_ = xb_t[:, ri, 0:255]
                                            else:
                                                o_ = ps[t // 2][0][rs_, (t % 2) * 256:(t % 2) * 256 + 255]
                                                r_ = xb_t[:, ri, 1:256]
                                            ins = e.matmul(o_, ct_t[:, d, ri, ex, rs_], r_, start=False, stop=True, skip_group_check=True, tile_position=(0, pr4 * 32))
                            return ins
                        P(fo, [ct_B] + [XBt[d][p_][1] for d in range(2) for p_ in prs], Yb)

                    SI(0)
                    SI(1)
                    if ck + 1 < 16:
                        prepA(ck + 1)
                    HS_SO(0)
                    SI(2)
                    HS_SO(1)
                    SO([0, 1])
                    SI(3)
                    HS_SO(2)
                    HS_SO(3)
                    SO([2, 3])
                    ysf3 = ysf[:].rearrange("p (c j) -> p c j", j=8)
                    for t in range(8):
                        V(lambda e, t=t: e.scalar_tensor_tensor(out=ysf3[:, :, t], in0=u3[:, :, t], scalar=dsk[:, ck:ck + 1], in1=Yap(t), op0=ALU.mult, op1=ALU.add), [u_B, dskB, Yb[t // 2]], [ysfB])
                    for hf in range(2):
                        hs = slice(hf * 1024, (hf + 1) * 1024)
                        A(lambda e: e.activation(out=g1[:], in_=ysf[:, hs], func=AF.Square), [ysfB], [g1B])
                        G(lambda e: e.tensor_scalar(out=g1[:], in0=g1[:], scalar1=0.044715, scalar2=1.0, op0=ALU.mult, op1=ALU.add), [g1B], [g1B])
                        G(lambda e: e.tensor_tensor(out=g1[:], in0=g1[:], in1=ysf[:, hs], op=ALU.mult), [g1B, ysfB], [g1B])
                        A(lambda e: e.activation(out=g1[:], in_=g1[:], func=AF.Sigmoid, scale=GC), [g1B], [g1B])
                        G(lambda e: e.tensor_tensor(out=yst[:, hs], in0=g1[:], in1=ysf[:, hs], op=ALU.mult), [g1B, ysfB], [ystB])
                    S.dma('sp', ysT_d[ck * 128:(ck + 1) * 128, :], yst[:], ystB, reads=[ystB], writes=[B["ysT"]])

        def stage_merge_out(l, xsrc, xsB, xdst, xdB):
            with ExitStack() as s2:
                def sb2(name, shape, dt):
                    t = s2.enter_context(SBT("E" + name, list(shape), dt))
                    return t, Buf("E" + name)
                ysT, ysTB = sb2("ysT", [128, KC, L], BF16)
                mg, mgB = sb2("mg", [128, KC, L], BF16)
                wb = [sb2("w%d" % i, [128, KC, 512], BF16) for i in range(2)]
                mrt = [sb2("mr%d" % i, [128, L], BF16) for i in range(2)]
                sst = [sb2("ss%d" % i, [128, L], BF16) for i in range(2)]
                sgl = [sb2("sgl%d" % i, [128, 512], F32) for i in range(2)]
                bg, bgB = sb2("bg", [128, 16], F32)
                xr_ = [sb2("xr%d" % i, [128, 512], F32) for i in range(2)]
                xo_ = [sb2("xo%d" % i, [128, 512], F32) for i in range(2)]
                vec16, vec16B = sb2("vec16", [16, 128], F32)
                S.dma('sp', vec16[:], b_glu[l].rearrange("(c p) -> c p", p=128), vec16B, reads=[B["params"]], writes=[vec16B])
                P(lambda e: e.transpose(ps[7][0][:, 0:16], vec16[:], ident[0:16, 0:16]), [vec16B, identB], [ps[7][1]])
                V(lambda e: e.tensor_copy(bg[:], ps[7][0][:, 0:16]), [ps[7][1]], [bgB])
                for kc in range(KC):
                    S.dma('sp', ysT[:, kc, :], ysT_d[kc * 128:(kc + 1) * 128, :], ysTB, reads=[B["ysT"]], writes=[ysTB])
                it = 0
                for cg in range(4):
                    w_t, w_B = wb[cg % 2]
                    wload(w_t, w_B, w_glu[l], cg * 512, 512)
                    for nc_ in range(4):
                        ch = cg * 4 + nc_
                        m_t, m_B = mrt[ch % 2]
                        s_t, s_B = sst[ch % 2]
                        S.dma('sp', m_t[:], mrT_d[ch * 128:(ch + 1) * 128, :], m_B, reads=[B["mrT"]], writes=[m_B])
                        S.dma('sp', s_t[:], ssT_d[ch * 128:(ch + 1) * 128, :], s_B, reads=[B["ssT"]], writes=[s_B])
                        for tg in range(4):
                            tsl = slice(tg * 512, (tg + 1) * 512)
                            pt, pB = ps[it % 4]
                            g_t, g_B = sgl[it % 2]
                            it += 1
                            def f(e, pt=pt, nc_=nc_, tsl=tsl, w_t=w_t):
                                ins = None
                                for kc in range(KC):
                                    ins = e.matmul(pt[:], w_t[:, kc, nc_ * 128:(nc_ + 1) * 128], ysT[:, kc, tsl], start=(kc == 0), stop=(kc == KC - 1))
                                return ins
                            P(f, [w_B, ysTB], [pB])
                            A(lambda e, pt=pt, g_t=g_t, ch=ch: e.activation(out=g_t[:], in_=pt[:], func=AF.Sigmoid, bias=bg[:, ch:ch + 1]), [pB, bgB], [g_B])
                            V(lambda e, g_t=g_t, ch=ch, tsl=tsl: e.tensor_tensor(out=g_t[:], in0=g_t[:], in1=ysT[:, ch, tsl], op=ALU.mult), [g_B, ysTB], [g_B])
                            V(lambda e, g_t=g_t, s_t=s_t, tsl=tsl: e.tensor_tensor(out=g_t[:], in0=g_t[:], in1=s_t[:, tsl], op=ALU.mult), [g_B, s_B], [g_B])
                            V(lambda e, g_t=g_t, m_t=m_t, ch=ch, tsl=tsl: e.tensor_tensor(out=mg[:, ch, tsl], in0=g_t[:], in1=m_t[:, tsl], op=ALU.add), [g_B, m_B], [mgB])
                it = 0
                for ng in range(4):
                    w_t, w_B = wb[ng % 2]
                    wload(w_t, w_B, w_out[l], ng * 512, 512)
                    for tt in range(16):
                        pt, pB = ps[4 + it % 4]
                        xr_t, xr_B = xr_[it % 2]
                        xo_t, xo_B = xo_[it % 2]
                        it += 1
                        def f(e, pt=pt, tt=tt, w_t=w_t):
                            ins = None
                            for kc in range(KC):
                                ins = e.matmul(pt[:], mg[:, kc, tt * 128:(tt + 1) * 128], w_t[:, kc, :], start=(kc == 0), stop=(kc == KC - 1))
                            return ins
                        P(f, [w_B, mgB], [pB])
                        S.dma('sp', xr_t[:], xsrc[tt * 128:(tt + 1) * 128, ng * 512:(ng + 1) * 512], xr_B, reads=[xsB], writes=[xr_B])
                        V(lambda e, pt=pt, xr_t=xr_t, xo_t=xo_t: e.tensor_tensor(out=xo_t[:], in0=pt[:], in1=xr_t[:], op=ALU.add), [pB, xr_B], [xo_B])
                        S.dma('sp', xdst[tt * 128:(tt + 1) * 128, ng * 512:(ng + 1) * 512], xo_t[:], xo_B, reads=[xo_B], writes=[xdB])

        def stage_ffn_up(l, hT, hTB):
            with ExitStack() as s2:
                def sb2(name, shape, dt):
                    t = s2.enter_context(SBT("H" + name, list(shape), dt))
                    return t, Buf("H" + name)
                wg = [sb2("wg%d" % i, [128, KC, 512], BF16) for i in range(2)]
                wu = [sb2("wu%d" % i, [128, KC, 512], BF16) for i in range(2)]
                stg = [sb2("stg%d" % i, [128, 4, L], BF16) for i in range(2)]
                sl = [sb2("sl%d" % i, [128, 512], F32) for i in range(2)]
                it = 0
                for fg in range(11):
                    g_t, g_B = wg[fg % 2]
                    u_t, u_B = wu[fg % 2]
                    s_t, s_B = stg[fg % 2]
                    wload(g_t, g_B, w_fg[l], fg * 512, 512)
                    wload(u_t, u_B, w_fu[l], fg * 512, 512)
                    for fc in range(4):
                        for tg in range(4):
                            tsl = slice(tg * 512, (tg + 1) * 512)
                            pg, pgB = ps[(it % 4) * 2]
                            pu, puB = ps[(it % 4) * 2 + 1]
                            l_t, l_B = sl[it % 2]
                            it += 1
                            def f(w_t, pt, fc=fc, tsl=tsl):
                                def ff(e):
                                    ins = None
                                    for kc in range(KC):
                                        ins = e.matmul(pt[:], w_t[:, kc, fc * 128:(fc + 1) * 128], hT[:, kc, tsl], start=(kc == 0), stop=(kc == KC - 1))
                                    return ins
                                return ff
                            P(f(g_t, pg), [g_B, hTB], [pgB])
                            P(f(u_t, pu), [u_B, hTB], [puB])
                            A(lambda e, pg=pg, l_t=l_t: e.activation(out=l_t[:], in_=pg[:], func=AF.Silu), [pgB], [l_B])
                            V(lambda e, pu=pu, l_t=l_t, s_t=s_t, fc=fc, tsl=tsl: e.tensor_tensor(out=s_t[:, fc, tsl], in0=pu[:], in1=l_t[:], op=ALU.mult), [puB, l_B], [s_B])
                    S.dma('sp', aT_d[fg * 512:(fg + 1) * 512, :].rearrange("(a p) t -> p a t", p=128), s_t[:], s_B, reads=[s_B], writes=[B["aT"]])

        def stage_ffn_down(l, xsrc, xsB, xdst, xdB):
            with ExitStack() as s2:
                def sb2(name, shape, dt):
                    t = s2.enter_context(SBT("I" + name, list(shape), dt))
                    return t, Buf("I" + name)
                Ah, AhB = sb2("A", [128, FC, 1024], BF16)
                wd = [sb2("wd%d" % i, [128, FC, 512], BF16) for i in range(2)]
                xr_ = [sb2("xr%d" % i, [128, 512], F32) for i in range(2)]
                xo_ = [sb2("xo%d" % i, [128, 512], F32) for i in range(2)]
                it = 0
                wi = 0
                for half in range(2):
                    for f4 in range(4):
                        S.dma('sp', Ah[:, f4 * 11:(f4 + 1) * 11, :], aT_d[f4 * 11 * 128:(f4 + 1) * 11 * 128, half * 1024:(half + 1) * 1024].rearrange("(a p) t -> p a t", p=128), AhB, reads=[B["aT"]], writes=[AhB])
                    for ng in range(4):
                        w_t, w_B = wd[wi % 2]
                        wi += 1
                        for f4 in range(4):
                            wload(w_t, w_B, w_fd[l], ng * 512, 512, k0=f4 * 11, nk=11, dk0=f4 * 11)
                        for t8 in range(8):
                            tt = half * 8 + t8
                            pt, pB = ps[it % 4]
                            xr_t, xr_B = xr_[it % 2]
                            xo_t, xo_B = xo_[it % 2]
                            it += 1
                            def f(e, pt=pt, t8=t8, w_t=w_t):
                                ins = None
                                for fc in range(FC):
                                    ins = e.matmul(pt[:], Ah[:, fc, t8 * 128:(t8 + 1) * 128], w_t[:, fc, :], start=(fc == 0), stop=(fc == FC - 1))
                                return ins
                            P(f, [w_B, AhB], [pB])
                            S.dma('sp', xr_t[:], xsrc[tt * 128:(tt + 1) * 128, ng * 512:(ng + 1) * 512], xr_B, reads=[xsB], writes=[xr_B])
                            V(lambda e, pt=pt, xr_t=xr_t, xo_t=xo_t: e.tensor_tensor(out=xo_t[:], in0=pt[:], in1=xr_t[:], op=ALU.add), [pB, xr_B], [xo_B])
                            S.dma('sp', xdst[tt * 128:(tt + 1) * 128, ng * 512:(ng + 1) * 512], xo_t[:], xo_B, reads=[xo_B], writes=[xdB])

        def stage_final(xsrc, xsB):
            with ExitStack() as s2:
                def sb2(name, shape, dt):
                    t = s2.enter_context(SBT("Z" + name, list(shape), dt))
                    return t, Buf("Z" + name)
                gbc, gbcB = sb2("gbc", [128, D], F32)
                S.dma('sp', gbc[:], ln_final_g.partition_broadcast(128), gbcB, reads=[B["params"]], writes=[gbcB])
                xt = [sb2("xt%d" % i, [128, D], F32) for i in range(2)]
                ot = [sb2("ot%d" % i, [128, D], F32) for i in range(2)]
                junk, junkB = sb2("junk", [128, D], BF16)
                st_ = [sb2("st%d" % i, [128, 4], F32) for i in range(2)]
                for tt in range(16):
                    x_t, x_B = xt[tt % 2]
                    o_t, o_B = ot[tt % 2]
                    s_t, s_B = st_[tt % 2]
                    S.dma('sp', x_t[:], xsrc[tt * 128:(tt + 1) * 128, :], x_B, reads=[xsB], writes=[x_B])
                    A(lambda e: e.activation(out=junk[:], in_=x_t[:], func=AF.Square, accum_out=s_t[:, 0:1]), [x_B], [junkB, s_B])
                    A(lambda e: e.copy(out=s_t[:, 1:2], in_=s_t[:, 0:1]), [s_B], [s_B])
                    V(lambda e: e.tensor_scalar(out=s_t[:, 2:3], in0=s_t[:, 1:2], scalar1=1.0 / D, scalar2=EPS, op0=ALU.mult, op1=ALU.add), [s_B], [s_B])
                    A(lambda e: e.activation(out=s_t[:, 3:4], in_=s_t[:, 2:3], func=AF.Sqrt), [s_B], [s_B])
                    V(lambda e: e.reciprocal(out=s_t[:, 0:1], in_=s_t[:, 3:4]), [s_B], [s_B])
                    V(lambda e: e.scalar_tensor_tensor(out=o_t[:], in0=x_t[:], scalar=s_t[:, 0:1], in1=gbc[:], op0=ALU.mult, op1=ALU.mult), [x_B, s_B, gbcB], [o_B])
                    S.dma('sp', out_d[tt * 128:(tt + 1) * 128, :], o_t[:], o_B, reads=[o_B], writes=[B["out"]])

        def run():
            stages = dbg.get("stages")
            on = lambda n: (stages is None) or (n in stages)
            xcur, xcurB = x_in, B["x"]
            for l in range(nlayer):
                if on("norm1") or on("inproj"):
                    with ExitStack() as sh:
                        hT = sh.enter_context(SBT("hT_a%d" % l, [128, KC, L], BF16))
                        hTB = Buf("hT")
                        stage_norm(xcur, xcurB, ln_mix_g[l:l + 1, :], hT, hTB, "A%d" % l)
                        S.barrier()
                        if on("inproj"):
                            stage_inproj(l, hT, hTB)
                            S.barrier()
                if on("ret"):
                    stage_ret(l)
                    S.barrier()
                if on("s5"):
                    stage_s5(l)
                    S.barrier()
                if on("merge"):
                    stage_merge_out(l, xcur, xcurB, xa_d, B["xa"])
                    S.barrier()
                if on("ffnup"):
                    with ExitStack() as sh:
                        hT = sh.enter_context(SBT("hT_b%d" % l, [128, KC, L], BF16))
                        hTB = Buf("hT2")
                        stage_norm(xa_d, B["xa"], ln_ffn_g[l:l + 1, :], hT, hTB, "G%d" % l)
                        S.barrier()
                        stage_ffn_up(l, hT, hTB)
                        S.barrier()
                if on("ffndown"):
                    stage_ffn_down(l, xa_d, B["xa"], xb_d, B["xb"])
                    S.barrier()
                xcur, xcurB = xb_d, B["xb"]
            if on("final"):
                stage_final(xcur, xcurB)
        run()
        S.finish()
    return nc


_NC = None


def _prep(inputs):
    f = lambda a: np.ascontiguousarray(np.asarray(a, dtype=np.float32))
    shared = {k: f(v) for k, v in inputs.items() if k != "x"}
    shared["ret_log_gamma"] = shared["ret_log_gamma"].reshape(2, 8)
    shared["ln_final_g"] = shared["ln_final_g"].reshape(1, D)
    shared.update(_consts())
    x = f(inputs["x"])
    return [dict(shared, x=x[b]) for b in range(8)]


def kernel(**inputs):
    global _NC
    if _NC is None:
        _NC = build_nc()
    in_maps = _prep(inputs)
    res = run_bass_kernel_spmd(_NC, in_maps, core_ids=list(range(8)))
    return np.stack([np.asarray(r["out"], dtype=np.float32) for r in res.results], axis=0)
```
